# Optimizing a Trainium2 kernel written in Bass

```python
import jax, jax.numpy as jnp
from jax import lax
import numpy as np

D_MODEL = 1024
BATCH = 16
SEQ = 2048
DEPTH = 2
DEC_BATCH = 16
DEC_SEQ = 32
PAST_LEN = 1024

CHUNK = 64
N_A_LAYERS = DEPTH // 2
N_B_LAYERS = DEPTH - N_A_LAYERS
D_RNN = 1408
N_RG_BLOCKS = 16
RG_BLOCK = D_RNN // N_RG_BLOCKS
CONV_W = 4
LRU_C = 8.0
N_HEADS = 16
N_KV_HEADS = 4
HEAD_DIM = 64
GROUP = N_HEADS // N_KV_HEADS
WINDOW = 128
WINDOW_CHUNKS = WINDOW // CHUNK
ROPE_THETA = 10000.0
EPS = 1e-6

kernel_name = "yoco_rglru_swa_sink_stream_step"


def rmsnorm(x, g):
    xf = x.astype(jnp.float32)
    y = xf * lax.rsqrt(jnp.mean(xf * xf, axis=-1, keepdims=True) + EPS)
    return (y * g.astype(jnp.float32)).astype(x.dtype)


def rope(x, pos):
    half = HEAD_DIM // 2
    inv = ROPE_THETA ** (-jnp.arange(half, dtype=jnp.float32) / half)
    ang = pos.astype(jnp.float32)[:, None] * inv[None, :]
    cos = jnp.cos(ang)[None, :, None, :]
    sin = jnp.sin(ang)[None, :, None, :]
    xf = x.astype(jnp.float32)
    x1, x2 = xf[..., :half], xf[..., half:]
    return jnp.concatenate([x1 * cos - x2 * sin, x2 * cos + x1 * sin], axis=-1).astype(x.dtype)


def rglru_block(x, norm_pre, w_in, conv_w, conv_b, w_r, b_r, w_i, b_i, lam, w_out, norm_post,
                conv_buf, h0):
    B, T, _ = x.shape
    u = rmsnorm(x, norm_pre) @ w_in
    branch, gate = u[..., :D_RNN], u[..., D_RNN:]
    ext = jnp.concatenate([conv_buf.astype(branch.dtype), branch], axis=1)
    conv = conv_b + ext[:, 0:T] * conv_w[0]
    for tap in range(1, CONV_W):
        conv = conv + ext[:, tap:tap + T] * conv_w[tap]
    new_conv = ext[:, -(CONV_W - 1):]
    xb = conv.reshape(B, T, N_RG_BLOCKS, RG_BLOCK)
    r = jax.nn.sigmoid(jnp.einsum('btnc,ncd->btnd', xb, w_r).reshape(B, T, D_RNN) + b_r)
    i = jax.nn.sigmoid(jnp.einsum('btnc,ncd->btnd', xb, w_i).reshape(B, T, D_RNN) + b_i)
    log_a = (-LRU_C * r.astype(jnp.float32)) * jax.nn.softplus(-lam.astype(jnp.float32))
    a = jnp.exp(log_a)
    b = jnp.sqrt(-jnp.expm1(2.0 * log_a)) * (i * conv).astype(jnp.float32)
    b = b.at[:, 0].add(a[:, 0] * h0.astype(jnp.float32))

    def combine(lhs, rhs):
        return (lhs[0] * rhs[0], rhs[0] * lhs[1] + rhs[1])

    _, h = lax.associative_scan(combine, (a, b), axis=1)
    new_h = h[:, -1].astype(x.dtype)
    y = (h.astype(x.dtype) * jax.nn.silu(gate)) @ w_out
    return x + rmsnorm(y, norm_post), new_conv, new_h


def shared_kv(h, norm_kv, w_kv, pos):
    B, T, _ = h.shape
    kv = rmsnorm(h, norm_kv) @ w_kv
    kvw = N_KV_HEADS * HEAD_DIM
    k = rope(kv[..., :kvw].reshape(B, T, N_KV_HEADS, HEAD_DIM), pos)
    v = kv[..., kvw:].reshape(B, T, N_KV_HEADS, HEAD_DIM)
    return k, v


def sink_softmax(s, sink):
    m = jnp.maximum(jnp.max(s, axis=-1, keepdims=True), sink)
    p = jnp.exp(s - m)
    return p / (jnp.sum(p, axis=-1, keepdims=True) + jnp.exp(sink - m))


def attend_prompt(q, k, v, sinks):
    B, S = q.shape[:2]
    NC = S // CHUNK
    qb = q.reshape(B, NC, CHUNK, N_KV_HEADS, GROUP, HEAD_DIM)

    def band(t):
        tp = jnp.pad(t, ((0, 0), (WINDOW, 0), (0, 0), (0, 0)))
        tc = tp.reshape(B, NC + WINDOW_CHUNKS, CHUNK, N_KV_HEADS, HEAD_DIM)
        return jnp.concatenate([tc[:, j:j + NC] for j in range(WINDOW_CHUNKS + 1)], axis=2)

    kb, vb = band(k), band(v)
    scale = HEAD_DIM ** -0.5
    s = jnp.einsum('bnqkgd,bnskd->bnkgqs', qb, kb).astype(jnp.float32) * scale
    key_chunk = (jnp.arange(NC)[:, None]
                 + jnp.repeat(jnp.arange(WINDOW_CHUNKS + 1), CHUNK)[None, :] - WINDOW_CHUNKS)
    valid = key_chunk >= 0
    s = jnp.where(valid[None, :, None, None, None, :], s, -jnp.inf)
    sink = sinks.astype(jnp.float32).reshape(N_KV_HEADS, GROUP)[None, None, :, :, None, None]
    p = sink_softmax(s, sink)
    o = jnp.einsum('bnkgqs,bnskd->bnqkgd', p.astype(vb.dtype), vb)
    return o.reshape(B, S, N_HEADS * HEAD_DIM)


def attend_cached(q, k_all, v_all, sinks):
    B, T = q.shape[:2]
    qg = q.reshape(B, T, N_KV_HEADS, GROUP, HEAD_DIM)
    scale = HEAD_DIM ** -0.5
    s = jnp.einsum('btkgd,bskd->bkgts', qg, k_all).astype(jnp.float32) * scale
    sink = sinks.astype(jnp.float32).reshape(N_KV_HEADS, GROUP)[None, :, :, None, None]
    p = sink_softmax(s, sink)
    o = jnp.einsum('bkgts,bskd->btkgd', p.astype(v_all.dtype), v_all)
    return o.reshape(B, T, N_HEADS * HEAD_DIM)


def swa_block(x, pos, norm_pre, w_in, sinks, w_out, norm_post, k, v, attend):
    B, T, _ = x.shape
    qw = N_HEADS * HEAD_DIM
    u = rmsnorm(x, norm_pre) @ w_in
    q = rope(u[..., :qw].reshape(B, T, N_HEADS, HEAD_DIM), pos)
    o = attend(q, k, v, sinks)
    y = (o * jax.nn.silu(u[..., qw:])) @ w_out
    return x + rmsnorm(y, norm_post)


def run_trunk(x, pos, conv_state, rnn_state, attend, a_w, kv_w, b_w):
    new_conv, new_rnn = [], []
    k = v = None
    for layer in range(DEPTH):
        if layer < N_A_LAYERS:
            x, c, h = rglru_block(x, *[w[layer] for w in a_w], conv_state[layer], rnn_state[layer])
            new_conv.append(c)
            new_rnn.append(h)
        else:
            if layer == N_A_LAYERS:
                k, v = shared_kv(x, kv_w[0], kv_w[1], pos)
            j = layer - N_A_LAYERS
            x = swa_block(x, pos, *[w[j] for w in b_w], k, v, attend)
    return x, jnp.stack(new_conv), jnp.stack(new_rnn), k, v


def setup_inputs(seed: int = 0) -> dict:
    key = jax.random.key(seed)
    ks = jax.random.split(key, 32)
    f32 = jnp.float32
    nrm = lambda k, shape, s: jax.random.normal(k, shape, f32) * s
    u = jax.random.uniform(ks[20], (N_A_LAYERS, D_RNN), f32, 0.9, 0.999)
    a0 = u ** (1.0 / LRU_C)
    lru_lambda = jnp.log(a0) - jnp.log1p(-a0)
    return {
        "x_prompt": nrm(ks[0], (BATCH, SEQ, D_MODEL), 1.0),
        "x_sample": nrm(ks[1], (DEC_BATCH, DEC_SEQ, D_MODEL), 1.0),
        "state_conv": nrm(ks[2], (N_A_LAYERS, DEC_BATCH, CONV_W - 1, D_RNN), 1.0),
        "state_rnn": nrm(ks[3], (N_A_LAYERS, DEC_BATCH, D_RNN), 0.5),
        "cache_k": nrm(ks[4], (DEC_BATCH, WINDOW, N_KV_HEADS, HEAD_DIM), 1.0),
        "cache_v": nrm(ks[5], (DEC_BATCH, WINDOW, N_KV_HEADS, HEAD_DIM), 1.0),
        "norm_pre_a": 1.0 + nrm(ks[6], (N_A_LAYERS, D_MODEL), 0.02),
        "w_in_a": nrm(ks[7], (N_A_LAYERS, D_MODEL, 2 * D_RNN), D_MODEL ** -0.5),
        "conv_w_a": nrm(ks[8], (N_A_LAYERS, CONV_W, D_RNN), CONV_W ** -0.5),
        "conv_b_a": nrm(ks[9], (N_A_LAYERS, D_RNN), 0.01),
        "w_gate_r": nrm(ks[10], (N_A_LAYERS, N_RG_BLOCKS, RG_BLOCK, RG_BLOCK), RG_BLOCK ** -0.5),
        "b_gate_r": nrm(ks[11], (N_A_LAYERS, D_RNN), 0.01),
        "w_gate_i": nrm(ks[12], (N_A_LAYERS, N_RG_BLOCKS, RG_BLOCK, RG_BLOCK), RG_BLOCK ** -0.5),
        "b_gate_i": nrm(ks[13], (N_A_LAYERS, D_RNN), 0.01),
        "lru_lambda": lru_lambda,
        "w_out_a": nrm(ks[14], (N_A_LAYERS, D_RNN, D_MODEL), D_RNN ** -0.5),
        "norm_post_a": 1.0 + nrm(ks[15], (N_A_LAYERS, D_MODEL), 0.02),
        "norm_kv": 1.0 + nrm(ks[16], (D_MODEL,), 0.02),
        "w_kv": nrm(ks[17], (D_MODEL, 2 * N_KV_HEADS * HEAD_DIM), D_MODEL ** -0.5),
        "norm_pre_b": 1.0 + nrm(ks[18], (N_B_LAYERS, D_MODEL), 0.02),
        "w_in_b": nrm(ks[19], (N_B_LAYERS, D_MODEL, 2 * N_HEADS * HEAD_DIM), D_MODEL ** -0.5),
        "attn_sinks": nrm(ks[21], (N_B_LAYERS, N_HEADS), 0.5),
        "w_out_b": nrm(ks[22], (N_B_LAYERS, N_HEADS * HEAD_DIM, D_MODEL), (N_HEADS * HEAD_DIM) ** -0.5),
        "norm_post_b": 1.0 + nrm(ks[23], (N_B_LAYERS, D_MODEL), 0.02),
    }


def reference(x_prompt, x_sample, state_conv, state_rnn, cache_k, cache_v,
              norm_pre_a, w_in_a, conv_w_a, conv_b_a, w_gate_r, b_gate_r, w_gate_i, b_gate_i,
              lru_lambda, w_out_a, norm_post_a, norm_kv, w_kv,
              norm_pre_b, w_in_b, attn_sinks, w_out_b, norm_post_b):
    a_w = (norm_pre_a, w_in_a, conv_w_a, conv_b_a, w_gate_r, b_gate_r, w_gate_i, b_gate_i,
           lru_lambda, w_out_a, norm_post_a)
    kv_w = (norm_kv, w_kv)
    b_w = (norm_pre_b, w_in_b, attn_sinks, w_out_b, norm_post_b)

    bp, sp = x_prompt.shape[0], x_prompt.shape[1]
    zero_conv = jnp.zeros((N_A_LAYERS, bp, CONV_W - 1, D_RNN), x_prompt.dtype)
    zero_rnn = jnp.zeros((N_A_LAYERS, bp, D_RNN), x_prompt.dtype)
    pos_p = jnp.arange(sp)
    y_prompt, p_conv, p_rnn, p_k, p_v = run_trunk(
        x_prompt, pos_p, zero_conv, zero_rnn, attend_prompt, a_w, kv_w, b_w)

    pos_s = PAST_LEN + jnp.arange(x_sample.shape[1])

    def attend_sample(q, k, v, sinks):
        k_all = jnp.concatenate([cache_k.astype(k.dtype), k], axis=1)
        v_all = jnp.concatenate([cache_v.astype(v.dtype), v], axis=1)
        return attend_cached(q, k_all, v_all, sinks)

    y_sample, s_conv, s_rnn, s_k, s_v = run_trunk(
        x_sample, pos_s, state_conv, state_rnn, attend_sample, a_w, kv_w, b_w)

    return (y_prompt, y_sample, p_conv, p_rnn, p_k[:, -WINDOW:], p_v[:, -WINDOW:],
            s_conv, s_rnn, s_k, s_v)
```

```python
import contextlib
import numpy as np
import concourse.bass as bass
import concourse.mybir as mybir
from concourse.bass_utils import run_bass_kernel_spmd

F32 = mybir.dt.float32
BF16 = mybir.dt.bfloat16
AF = mybir.ActivationFunctionType
ALU = mybir.AluOpType

NCORES = 8
D = 1024
DR = 1408
NCH = 11
KC = 8
SEQ = 2048
SL = 32
NSEQ = 2
PAST = 1024
EPS = 1e-6
import os
ROT = int(os.environ.get('K_ROT', '800'))
KN = lambda k, d: int(os.environ.get(k, d))
DUM = float(os.environ.get('K_DUM', '2.5'))
NPOS = SEQ + 2 * SL


class Sem:
    def __init__(self, h):
        self.h = h
        self.v = 0


class Eng:
    def __init__(self, name, sem):
        self.name = name
        self.sem = sem
        self.seen = {}


class Buf:
    __slots__ = ("w", "r", "pre")

    def __init__(self, pre=None):
        self.w = []
        self.r = []
        self.pre = dict(pre) if pre else None


class Op:
    __slots__ = ("eng", "fn", "deps", "pre", "dsem", "cost", "lat", "tok", "idx", "nd", "ready", "users", "start", "fin", "ts")


class DS:
    def __init__(self, kb, name, s):
        self.kb = kb
        self.name = name
        self.s = s


COST0 = {"pe": 0.25, "act": 0.25, "dve": 0.15, "pool": 0.25, "sp": 0.05}
COSTN = {"pe": 1.0 / 2400, "act": 1.0 / 1200, "dve": 1.0 / 900, "pool": 1.0 / 420, "sp": 0.0}


class KB:
    def __init__(self, nc, stack):
        self.nc = nc
        self.stack = stack
        self.nsem = 0
        self.PE = Eng("pe", self.new_sem("pe"))
        self.ACT = Eng("act", self.new_sem("act"))
        self.DVE = Eng("dve", self.new_sem("dve"))
        self.POOL = Eng("pool", self.new_sem("pool"))
        self.SP = Eng("sp", self.new_sem("sp"))
        self.engs = [self.PE, self.ACT, self.DVE, self.POOL, self.SP]
        self.dsems = []
        self.alias_pre = {}
        self.esems = [e.sem for e in self.engs]
        self.pending = []
        self.nops = 0

    def new_sem(self, name):
        self.nsem += 1
        return Sem(self.stack.enter_context(self.nc.semaphore(name)))

    def dsem(self, name):
        s = self.new_sem(name)
        self.dsems.append(s)
        return DS(self, name, s)

    def buf(self):
        return Buf(self.alias_pre)

    def mark(self):
        return len(self.pending)

    def since(self, m):
        return self.pending[m:]

    def op(self, eng, fn, reads=(), writes=(), dsem=None, n=512, c=None, after=(), ts=None):
        o = Op()
        o.ts = ts
        o.eng = eng
        o.fn = fn
        o.dsem = dsem
        o.tok = None
        if dsem is not None:
            o.cost = 0.06
            o.lat = 2.5 + n / 290.0
        else:
            o.cost = c if c is not None else COST0[eng.name] + n * COSTN[eng.name]
            o.lat = o.cost
        deps = {}
        pre = {}
        for b in reads:
            for w in b.w:
                deps[id(w)] = w
        for b in list(writes) + list(after):
            for w in b.w:
                deps[id(w)] = w
            for w in b.r:
                deps[id(w)] = w
            if b.pre:
                for s_, v in b.pre.items():
                    if pre.get(s_, 0) < v:
                        pre[s_] = v
                b.pre = None
        o.deps = list(deps.values())
        o.pre = pre
        self.pending.append(o)
        for b in reads:
            b.r.append(o)
        for b in writes:
            b.w = [o]
            b.r = []
        return o

    def barrier_tokens(self):
        d = {}
        for sm in self.esems:
            if sm.v:
                d[sm] = sm.v
        for s in self.dsems:
            if s.v:
                d[s] = s.v
        return d

    def _schedule(self, ops, W=128):
        for i, o in enumerate(ops):
            o.idx = i
            o.users = []
            o.nd = 0
            o.ready = 0.0
            o.start = None
        inseg = set(id(o) for o in ops)
        for o in ops:
            for d in o.deps:
                if id(d) in inseg:
                    d.users.append(o)
                    o.nd += 1
        tail = {}
        for o in reversed(ops):
            t_ = 0.0
            for u in o.users:
                if tail[id(u)] > t_:
                    t_ = tail[id(u)]
            tail[id(o)] = t_ + o.lat
        PB = float(os.environ.get('K_PB', '3'))
        queues = {e.name: [] for e in self.engs}
        for o in ops:
            queues[o.eng.name].append(o)
        heads = {k: 0 for k in queues}
        free = {k: 0.0 for k in queues}
        order = {k: [] for k in queues}
        glob = []
        cur_ts = {}
        left = len(ops)
        while left:
            best = None
            for k, q in queues.items():
                h = heads[k]
                while h < len(q) and q[h].start is not None:
                    h += 1
                heads[k] = h
                cnt = 0
                i = h
                while i < len(q) and cnt < W:
                    o = q[i]
                    i += 1
                    if o.start is not None:
                        continue
                    cnt += 1
                    if o.nd:
                        continue
                    st = o.ready if o.ready > free[k] else free[k]
                    pen = 0.0
                    if o.ts is not None and o.ts != cur_ts.get(k):
                        pen = 2.0
                    if PB > 0:
                        key = (int((st + pen) / PB), -tail[id(o)], o.idx)
                    else:
                        key = (st + pen, o.idx)
                    if best is None or key < best[0]:
                        best = (key, o, k, st)
                    if PB <= 0 and st + pen <= free[k]:
                        break
            _, o, k, st = best
            if o.ts is not None and o.ts != cur_ts.get(k):
                cur_ts[k] = o.ts
                st += 1.3
            o.start = st
            free[k] = st + o.cost
            o.fin = st + o.lat
            order[k].append(o)
            glob.append(o)
            left -= 1
            for u in o.users:
                u.nd -= 1
                if u.ready < o.fin:
                    u.ready = o.fin
        return order, glob

    def emit(self, block, last=True):
        ops = self.pending
        self.pending = []
        order, glob = self._schedule(ops)
        for o in glob:
            if o.dsem is not None:
                sm = o.dsem.s
                sm.v += 16
                o.tok = (sm, sm.v)
        for e in self.engs:
            for o in order[e.name]:
                if o.dsem is None:
                    if e.sem.v >= ROT:
                        e.sem = self.new_sem(e.name + "_%d" % self.nsem)
                        self.esems.append(e.sem)
                    e.sem.v += 1
                    o.tok = (e.sem, e.sem.v)
        prog = {}
        for e in self.engs:
            lst = []
            for o in order[e.name]:
                need = dict(o.pre)
                for d in o.deps:
                    s_, v = d.tok
                    if need.get(s_, 0) < v:
                        need[s_] = v
                waits = []
                for s_, v in need.items():
                    if e.seen.get(s_, 0) < v:
                        e.seen[s_] = v
                        waits.append((s_, v))
                lst.append((waits, o.fn, o.tok[0], 16 if o.dsem is not None else 1, 0))
            if e.name == "pe" and DUM > 0 and len(lst) > 200:
                prev_end = None
                for i_, o in enumerate(order[e.name]):
                    if prev_end is not None and i_ > 20:
                        gap = o.start - prev_end
                        if gap > 0.4:
                            nd = int(min(gap, DUM) / 0.08)
                            w_, f_, s_, inc_, _ = lst[i_]
                            lst[i_] = (w_, f_, s_, inc_, nd)
                    prev_end = o.start + o.cost
            prog[e.name] = lst
        final = self.barrier_tokens() if last else {}

        def run(e, lst):
            for waits, fn, sem, inc, nd in lst:
                for _ in range(nd):
                    e.ldweights(self.dummy_w)
                for s_, v in waits:
                    e.wait_ge(s_.h, v)
                ins = fn(e)
                ins.then_inc(sem.h, inc)

        @block.tensor
        def _(e):
            run(e, prog["pe"])

        @block.scalar
        def _(e):
            run(e, prog["act"])

        @block.vector
        def _(e):
            run(e, prog["dve"])

        @block.gpsimd
        def _(e):
            run(e, prog["pool"])

        @block.sync
        def _(e):
            run(e, prog["sp"])
            for s_, v in final.items():
                e.wait_ge(s_.h, v)


def build_program(do_b=True, tile_sel=None):
    nc = bass.Bass("TRN2", target_bir_lowering=False)

    def din(name, shape):
        return nc.dram_tensor(name, list(shape), F32, kind="ExternalInput").ap()

    def dout(name, shape):
        return nc.dram_tensor(name, list(shape), F32, kind="ExternalOutput").ap()

    xp = din("xp", [NSEQ * SEQ, D])
    xs = din("xs", [NSEQ * SL, D])
    st_conv = din("st_conv", [NSEQ, 3, DR])
    st_rnn = din("st_rnn", [NSEQ, DR])
    cache_k = din("cache_k", [NSEQ, 128, 256])
    cache_v = din("cache_v", [NSEQ, 128, 256])
    norm_pre_a = din("norm_pre_a", [D])
    w_in_a = din("w_in_a", [D, 2 * DR])
    conv_w = din("conv_w", [4, DR])
    conv_b = din("conv_b", [DR])
    w_gate_r = din("w_gate_r", [16, 88, 88])
    b_gate_r = din("b_gate_r", [DR])
    w_gate_i = din("w_gate_i", [16, 88, 88])
    b_gate_i = din("b_gate_i", [DR])
    lru_lambda = din("lru_lambda", [DR])
    w_out_a = din("w_out_a", [DR, D])
    norm_post_a = din("norm_post_a", [1, D])
    norm_kv = din("norm_kv", [D])
    w_kv = din("w_kv", [D, 512])
    norm_pre_b = din("norm_pre_b", [D])
    w_in_b = din("w_in_b", [D, 2048])
    sinks = din("sinks", [16])
    w_out_b = din("w_out_b", [D, D])
    norm_post_b = din("norm_post_b", [1, D])
    ident_d = din("ident", [128, 128])
    prot_d = din("prot", [128, 128])
    cos_d = din("cos_t", [128, NPOS])
    sin_d = din("sin_t", [128, NPOS])

    yp = dout("yp", [NSEQ * SEQ, D])
    ys = dout("ys", [NSEQ * SL, D])
    p_conv = dout("p_conv", [NSEQ, 3, DR])
    p_rnn = dout("p_rnn", [NSEQ, DR])
    p_k = dout("p_k", [NSEQ, 128, 256])
    p_v = dout("p_v", [NSEQ, 128, 256])
    s_conv = dout("s_conv", [NSEQ, 3, DR])
    s_rnn = dout("s_rnn", [NSEQ, DR])
    s_k = dout("s_k", [NSEQ, SL, 256])
    s_v = dout("s_v", [NSEQ, SL, 256])

    x1p = nc.dram_tensor("x1p", [NSEQ * SEQ, D], F32).ap()
    x1s = nc.dram_tensor("x1s", [NSEQ * SL, D], F32).ap()

    with contextlib.ExitStack() as stack, nc.Block() as block:
        stack.enter_context(nc.allow_non_contiguous_dma("small strided parameter / state loads"))
        stack.enter_context(nc.allow_low_precision("bf16 matmul operands, fp32 accumulation"))
        kb = KB(nc, stack)
        op = kb.op
        PE, ACT, DVE, POOL, SP = kb.PE, kb.ACT, kb.DVE, kb.POOL, kb.SP

        def sb(st, name, shape, dt=F32):
            return st.enter_context(nc.sbuf_tensor(name, list(shape), dt))

        def ps(st, name, shape, dt=F32):
            return st.enter_context(nc.psum_tensor(name, list(shape), dt))

        tiles = []
        tiles.append(("s", [0, 1], SL, 0, 0))
        for s in range(NSEQ):
            for t in range(SEQ // 512):
                tiles.append(("p", [s], 512, s * SEQ + t * 512, t))
        if tile_sel is not None:
            tiles = [tiles[i] for i in tile_sel]

        x1_bufs = {}

        ident_f = sb(stack, "ident_f", [128, 128])
        ident_b = sb(stack, "ident_b", [128, 128], BF16)
        b_ident_f, b_ident_b = kb.buf(), kb.buf()
        s_const = kb.dsem("const")
        op(SP, lambda e: e.dma_start(out=ident_f[:], in_=ident_d), writes=[b_ident_f], dsem=s_const)
        epsb = sb(stack, "epsb", [128, 1])
        b_eps = kb.buf()
        op(POOL, lambda e: e.memset(epsb[:], EPS), writes=[b_eps])
        q25 = sb(stack, "q25", [128, 1])
        b_q25 = kb.buf()
        op(POOL, lambda e: e.memset(q25[:], 0.25), writes=[b_q25])

        def load_fm(st, name, src1d, nchunk, sem, eng=SP):
            t = sb(st, name, [128, nchunk])
            b = kb.buf()
            op(eng, lambda e: e.dma_start(out=t[:], in_=src1d.rearrange("(c p) -> p c", p=128)),
               writes=[b], dsem=sem)
            return t, b

        stA = contextlib.ExitStack()
        wbig = sb(stack, "wbig", [128, KC * 2 * DR], BF16)
        wbigv = wbig[:, :]
        wia = wbigv.rearrange("p (k n) -> p k n", k=KC)
        stgE = [wbigv[:, 0:4096].bitcast(F32), wbigv[:, 4096:8192].bitcast(F32)]
        wq = wbigv[:, 8192:16384].rearrange("p (k n) -> p k n", k=KC)
        wk = wbigv[:, 16384:20480].rearrange("p (k n) -> p k n", k=KC)
        wv = wbigv[:, 20480:22528].rearrange("p (k n) -> p k n", k=KC)
        b_wq = [kb.buf() for _ in range(KC)]
        b_wk = [kb.buf() for _ in range(KC)]
        b_stgE = [kb.buf() for _ in range(2)]
        g_kv, b_g_kv = load_fm(stack, "g_kv", norm_kv, KC, s_const)
        g_b, b_g_b = load_fm(stack, "g_b", norm_pre_b, KC, s_const)
        woa = sb(stA, "woa", [128, NCH, D], BF16)
        wgr = sb(stA, "wgr", [128, NCH, 3, 128], BF16)
        wgi = sb(stA, "wgi", [128, NCH, 3, 128], BF16)
        b_wia = [kb.buf() for _ in range(KC)]
        b_woa = [kb.buf() for _ in range(NCH)]
        b_wgr, b_wgi = kb.buf(), kb.buf()
        g_a, b_g_a = load_fm(stA, "g_a", norm_pre_a, KC, s_const)
        cb_t, b_cb = load_fm(stA, "cb_t", conv_b, NCH, s_const)
        br_t, b_br = load_fm(stA, "br_t", b_gate_r, NCH, s_const)
        bi_t, b_bi = load_fm(stA, "bi_t", b_gate_i, NCH, s_const)
        lam_t, b_lam = load_fm(stA, "lam_t", lru_lambda, NCH, s_const)
        cw_t = sb(stA, "cw_t", [128, 4, NCH])
        b_cw = kb.buf()
        op(SP, lambda e: e.dma_start(out=cw_t[:], in_=conv_w.rearrange("t (c p) -> p t c", p=128)),
           writes=[b_cw], dsem=s_const)
        gpost_a = sb(stA, "gpost_a", [128, D])
        b_gpa = kb.buf()
        op(SP, lambda e: e.dma_start(out=gpost_a[:], in_=norm_post_a.partition_broadcast(128)),
           writes=[b_gpa], dsem=s_const)

        for b_ in (b_ident_f, b_g_a, b_cb, b_br, b_bi, b_lam, b_cw, b_gpa, b_g_kv, b_g_b):
            b_.w = [o_ for o_ in kb.since(0) if o_.dsem is s_const]
        op(DVE, lambda e: e.tensor_copy(out=ident_b[:], in_=ident_f[:]), reads=[b_ident_f], writes=[b_ident_b])
        kb.dummy_w = ident_b[:, :]

        op(POOL, lambda e: e.memset(wgr[:], 0.0), writes=[b_wgr])
        op(POOL, lambda e: e.memset(wgi[:], 0.0), writes=[b_wgi])
        s_gate = kb.dsem("gate")
        for (wsrc, wdst, bw) in ((w_gate_r, wgr, b_wgr), (w_gate_i, wgi, b_wgi)):
            for blk in range(16):
                lo = 88 * blk
                hi = lo + 88
                for i in range(lo // 128, (hi - 1) // 128 + 1):
                    r0, r1 = max(lo, 128 * i), min(hi, 128 * i + 128)
                    for j in range(lo // 128, (hi - 1) // 128 + 1):
                        c0, c1 = max(lo, 128 * j), min(hi, 128 * j + 128)
                        op(POOL,
                           (lambda e, wsrc=wsrc, wdst=wdst, blk=blk, r0=r0, r1=r1, c0=c0, c1=c1, i=i, j=j, lo=lo:
                            e.dma_start(out=wdst[r0 - 128 * i:r1 - 128 * i, i, j - i + 1, c0 - 128 * j:c1 - 128 * j],
                                        in_=wsrc[blk, r0 - lo:r1 - lo, c0 - lo:c1 - lo])),
                           reads=[bw], dsem=s_gate, n=64)

        b_wgr.w = [o_ for o_ in kb.since(0) if o_.dsem is s_gate] + b_wgr.w
        b_wgi.w = list(b_wgr.w) + b_wgi.w

        hbr = sb(stA, "hbr", [128, NCH])
        hbi = sb(stA, "hbi", [128, NCH])
        cc = sb(stA, "cc", [128, NCH])
        hc = sb(stA, "hc", [128, NCH])
        xl = sb(stA, "xl", [128, NCH])
        pl = sb(stA, "pl", [128, NCH])
        b_hbr, b_hbi, b_cc, b_hc, b_xl, b_pl = (kb.buf() for _ in range(6))
        op(DVE, lambda e: e.tensor_scalar(out=hbr[:], in0=br_t[:], scalar1=0.5, scalar2=None, op0=ALU.mult),
           reads=[b_br], writes=[b_hbr])
        op(DVE, lambda e: e.tensor_scalar(out=hbi[:], in0=bi_t[:], scalar1=0.5, scalar2=None, op0=ALU.mult),
           reads=[b_bi], writes=[b_hbi])
        op(ACT, lambda e: e.activation(out=xl[:], in_=lam_t[:], func=AF.Exp, scale=-1.0), reads=[b_lam], writes=[b_xl], ts="e")
        coef = [1.0, -1.0 / 2, 1.0 / 3, -1.0 / 4, 1.0 / 5, -1.0 / 6, 1.0 / 7, -1.0 / 8]
        op(DVE, lambda e: e.tensor_scalar(out=pl[:], in0=xl[:], scalar1=coef[7], scalar2=coef[6], op0=ALU.mult, op1=ALU.add),
           reads=[b_xl], writes=[b_pl])
        for k in range(5, -1, -1):
            op(DVE, lambda e: e.tensor_tensor(out=pl[:], in0=pl[:], in1=xl[:], op=ALU.mult), reads=[b_pl, b_xl], writes=[b_pl])
            op(DVE, lambda e, k=k: e.tensor_scalar(out=pl[:], in0=pl[:], scalar1=coef[k], scalar2=None, op0=ALU.add),
               reads=[b_pl], writes=[b_pl])
        op(DVE, lambda e: e.tensor_tensor(out=pl[:], in0=pl[:], in1=xl[:], op=ALU.mult), reads=[b_pl, b_xl], writes=[b_pl])
        op(DVE, lambda e: e.tensor_scalar(out=cc[:], in0=pl[:], scalar1=-8.0, scalar2=None, op0=ALU.mult),
           reads=[b_pl], writes=[b_cc])
        op(DVE, lambda e: e.tensor_scalar(out=hc[:], in0=pl[:], scalar1=-4.0, scalar2=None, op0=ALU.mult),
           reads=[b_pl], writes=[b_hc])

        stS = contextlib.ExitStack()
        NSTG = 5
        stg = [sb(stS, "stg%d" % i, [128, 2 * DR]) for i in range(NSTG)]
        b_stg = [kb.buf() for _ in range(NSTG)]
        s_stg = [kb.dsem("stg%d" % i) for i in range(NSTG)]
        ci = 0
        for kc in range(KC):
            k2 = ci % NSTG
            dq = SP if ci % 2 == 0 else ACT
            op(dq, lambda e, kc=kc, k2=k2: e.dma_start(out=stg[k2][:], in_=w_in_a[kc * 128:(kc + 1) * 128, :]),
               writes=[b_stg[k2]], dsem=s_stg[k2])
            half = DR
            op(DVE, lambda e, kc=kc, k2=k2: e.tensor_scalar(out=wia[:, kc, 0:half], in0=stg[k2][:, 0:half],
                                                             scalar1=g_a[:, kc:kc + 1], scalar2=None, op0=ALU.mult),
               reads=[b_stg[k2], b_g_a], writes=[b_wia[kc]])
            op(ACT, lambda e, kc=kc, k2=k2: e.activation(out=wia[:, kc, half:2 * half], in_=stg[k2][:, half:2 * half],
                                                          func=AF.Identity, scale=g_a[:, kc:kc + 1]),
               reads=[b_stg[k2], b_g_a], writes=[b_wia[kc]])
            ci += 1
        for j in range(NCH):
            k2 = ci % NSTG
            dq = SP if ci % 2 == 0 else ACT
            op(dq, lambda e, j=j, k2=k2: e.dma_start(out=stg[k2][:, 0:D], in_=w_out_a[j * 128:(j + 1) * 128, :]),
               writes=[b_stg[k2]], dsem=s_stg[k2])
            if j % 2 == 0:
                op(DVE, lambda e, j=j, k2=k2: e.tensor_scalar(out=woa[:, j, :], in0=stg[k2][:, 0:D], scalar1=0.5, scalar2=None, op0=ALU.mult),
                   reads=[b_stg[k2]], writes=[b_woa[j]])
            else:
                op(ACT, lambda e, j=j, k2=k2: e.activation(out=woa[:, j, :], in_=stg[k2][:, 0:D], func=AF.Identity, scale=0.5),
                   reads=[b_stg[k2]], writes=[b_woa[j]])
            ci += 1
        kb.emit(block, last=False)
        kb.alias_pre = kb.barrier_tokens()
        stS.close()

        NT = 512
        brs = sb(stA, "brs", [128, NCH, 3 + NT])
        b_brs = [kb.buf() for _ in range(NCH)]
        hst = sb(stA, "hst", [128, NSEQ, NCH])
        b_hst = [kb.buf() for _ in range(NCH)]
        hstp = sb(stA, "hstp", [128, NSEQ, NCH])
        b_hstp = [[kb.buf() for _ in range(NCH)] for _ in range(NSEQ)]
        hsv = sb(stA, "hsv", [128, NSEQ, NCH, 3])
        b_hsv = [[kb.buf() for _ in range(NCH)] for _ in range(NSEQ)]
        NCV = 4
        cv = [sb(stA, "cv%d" % i, [128, NT]) for i in range(NCV)]
        cvb = [sb(stA, "cvb%d" % i, [128, NT], BF16) for i in range(NCV)]
        b_cv = [kb.buf() for _ in range(NCV)]
        b_cvb = [kb.buf() for _ in range(NCV)]
        tg = [sb(stA, "tg%d" % i, [128, NT]) for i in range(2)]
        b_tg = [kb.buf() for _ in range(2)]
        NSG = 4
        sg = [sb(stA, "sg%d" % i, [128, NT]) for i in range(NSG)]
        b_sg = [kb.buf() for _ in range(NSG)]
        NTS = KN('K_NTS', 2)
        tA = [sb(stA, "tA%d" % i, [128, NT]) for i in range(NTS)]
        tB = [sb(stA, "tB%d" % i, [128, NT]) for i in range(NTS)]
        tC = [sb(stA, "tC%d" % i, [128, NT]) for i in range(NTS)]
        tH = [sb(stA, "tH%d" % i, [128, NT]) for i in range(NTS)]
        b_tA = [kb.buf() for _ in range(NTS)]
        b_tB = [kb.buf() for _ in range(NTS)]
        b_tC = [kb.buf() for _ in range(NTS)]
        b_tH = [kb.buf() for _ in range(NTS)]
        NHG = KN('K_NHG', 1)
        hg_l = [sb(stA, "hg%d" % i, [128, NCH, NT], BF16) for i in range(NHG)]
        b_hg_l = [[kb.buf() for _ in range(NCH)] for _ in range(NHG)]
        NXB = KN('K_NXB', 2)
        xb = [sb(stA, "xb%d" % i, [128, D]) for i in range(NXB)]
        b_xb = [kb.buf() for _ in range(NXB)]
        s_xb = [kb.dsem("xb%d" % i) for i in range(NXB)]
        s_xo = [kb.dsem("xo%d" % i) for i in range(3)]
        NXR = KN('K_NXR', 3)
        xr = [sb(stA, "xr%d" % i, [128, D]) for i in range(NXR)]
        b_xr = [kb.buf() for _ in range(NXR)]
        s_xr = [kb.dsem("xr%d" % i) for i in range(NXR)]
        xnb = [sb(stA, "xnb%d" % i, [128, D], BF16) for i in range(2)]
        b_xnb = [kb.buf() for _ in range(2)]
        NXT = KN('K_NXT', 1)
        xnT_l = [sb(stA, "xnT%d" % i, [128, KC, NT], BF16) for i in range(NXT)]
        b_xnT_l = [[kb.buf() for _ in range(4)] for _ in range(NXT)]
        junk = sb(stA, "junk", [128, D], BF16)
        b_junk = kb.buf()
        junk2 = sb(stA, "junk2", [128, 2, 512], BF16)
        b_junk2 = [kb.buf() for _ in range(2)]
        yn_l = [sb(stA, "yn%d" % i, [128, D]) for i in range(2)]
        b_yn_l = [kb.buf() for _ in range(2)]
        ss = sb(stA, "ss", [128, 8])
        b_ss = [kb.buf() for _ in range(4)]
        ss2 = sb(stA, "ss2", [128, 8])
        b_ss2 = [kb.buf() for _ in range(2)]
        s_xb_A, s_xo_A, s_stg_A, s_const_A, s_xr_A = s_xb, s_xo, s_stg, s_const, s_xr
        s_small_ld = kb.dsem("small_ld")
        s_small_st = kb.dsem("small_st")

        psA = contextlib.ExitStack()
        ub = [ps(psA, "ub%d" % i, [128, NT]) for i in range(2)]
        b_ub = [kb.buf() for _ in range(2)]
        ug_l = [ps(psA, "ug%d" % i, [128, NT]) for i in range(2)]
        b_ug_l = [kb.buf() for _ in range(2)]
        pr = ps(psA, "pr", [128, NT])
        pi_ = ps(psA, "pi", [128, NT])
        b_pr, b_pi = kb.buf(), kb.buf()
        tp = [pr[:, :].bitcast(BF16).rearrange("p (k t) -> p k t", k=8)] * 2
        b_tp = [b_pr] * 2
        py = [ps(psA, "py%d" % i, [128, NT]) for i in range(2)]
        b_py = [kb.buf() for _ in range(2)]

        cnt = {"xb": 0, "xnb": 0, "tp": 0, "ub": 0, "cv": 0, "tg": 0, "sg": 0, "t": 0, "ss": 0, "ss2": 0, "tile": 0, "xr": 0, "yn": 0}

        def rstd_from_ss(col_ap, rd, wr_buf, out_ap):
            op(ACT, lambda e: e.activation(out=out_ap, in_=col_ap, func=AF.Sqrt, bias=epsb[0:col_ap.shape[0], :], scale=1.0 / D),
               reads=list(rd) + [b_eps], writes=[wr_buf], ts="q")
            op(DVE, lambda e: e.reciprocal(out=out_ap, in_=out_ap), reads=[wr_buf], writes=[wr_buf])

        def load_norm_transpose(src, row0, nrows, si, xnT_t, b_xnT_s, col0):
            P = nrows
            i = cnt["xb"] % NXB
            cnt["xb"] += 1
            op(SP, lambda e: e.dma_start(out=xb[i][0:P, :], in_=src[row0:row0 + P, :]), writes=[b_xb[i]], dsem=s_xb[i])
            k = cnt["ss"] % 4
            cnt["ss"] += 1
            op(ACT, lambda e: e.activation(out=junk[0:P, :], in_=xb[i][0:P, :], func=AF.Square,
                                           accum_out=ss[0:P, 2 * k:2 * k + 1]),
               reads=[b_xb[i]], writes=[b_junk, b_ss[k]])
            rstd_from_ss(ss[0:P, 2 * k:2 * k + 1], [b_ss[k]], b_ss[k], ss[0:P, 2 * k + 1:2 * k + 2])
            n = cnt["xnb"] % 2
            cnt["xnb"] += 1
            op(DVE, lambda e: e.tensor_scalar(out=xnb[n][0:P, :], in0=xb[i][0:P, :], scalar1=ss[0:P, 2 * k + 1:2 * k + 2],
                                              scalar2=None, op0=ALU.mult),
               reads=[b_xb[i], b_ss[k]], writes=[b_xnb[n]])
            q = cnt["tp"] % 2
            cnt["tp"] += 1

            def tr(e, q=q, n=n):
                ins = None
                for kc in range(KC):
                    ins = e.transpose(out=tp[q][:, kc, 0:P], in_=xnb[n][0:P, kc * 128:(kc + 1) * 128],
                                      identity=ident_b[0:P, 0:P])
                return ins
            op(PE, tr, reads=[b_xnb[n], b_ident_b], writes=[b_tp[q]], c=0.7)
            op(ACT, lambda e, q=q: e.activation(out=xnT_t[:, :, col0:col0 + P], in_=tp[q][:, :, 0:P], func=AF.Copy),
               reads=[b_tp[q]], writes=[b_xnT_s])

        def phaseA_tile(kind, seqs, L, row0, tidx):
            ns = len(seqs)
            N = ns * L
            xi_ = cnt["tile"] % NXT
            cnt["tile"] += 1
            xnT = xnT_l[xi_]
            b_xnT = b_xnT_l[xi_]
            hg = hg_l[(cnt["tile"] - 1) % NHG]
            b_hg = b_hg_l[(cnt["tile"] - 1) % NHG]
            src = xp if kind == "p" else xs
            dst1 = x1p if kind == "p" else x1s
            nsub = (N + 127) // 128
            if kind == "p":
                sq0 = seqs[0]
                for j in range(NCH):
                    if tidx == 0:
                        op(POOL, lambda e, j=j: e.memset(brs[:, j, 0:3], 0.0), writes=[b_brs[j]], n=3)
                        op(POOL, lambda e, j=j, sq0=sq0: e.memset(hstp[:, sq0, j:j + 1], 0.0), writes=[b_hstp[sq0][j]], n=1)
                    else:
                        op(POOL, lambda e, j=j, sq0=sq0: e.tensor_copy(out=brs[:, j, 0:3], in_=hsv[:, sq0, j, :]),
                           reads=[b_hsv[sq0][j]], writes=[b_brs[j]], n=3)
            if kind == "s":
                m_ld = kb.mark()
                for j in range(NCH):
                    bv = brs[:, j, 0:ns * (3 + L)].rearrange("p (s l) -> p s l", s=ns)
                    for s_ in range(ns):
                        op(SP, lambda e, j=j, s_=s_, bv=bv: e.dma_start(
                            out=bv[:, s_, 0:3],
                            in_=st_conv[s_, :, j * 128:(j + 1) * 128].rearrange("r p -> p r")),
                           after=[b_brs[j]], dsem=s_small_ld, n=8)
                    op(SP, lambda e, j=j: e.dma_start(
                        out=hst[:, :, j], in_=st_rnn[:, j * 128:(j + 1) * 128].rearrange("s p -> p s")),
                       after=[b_hst[j]], dsem=s_small_ld, n=8)
                lds = [o_ for o_ in kb.since(m_ld) if o_.dsem is s_small_ld]
                for j in range(NCH):
                    b_brs[j].w = list(lds)
                    b_brs[j].r = []
                    b_hst[j].w = list(lds)
                    b_hst[j].r = []
            for s_ in range(nsub):
                P = min(128, N - 128 * s_)
                load_norm_transpose(src, row0 + 128 * s_, P, s_, xnT, b_xnT[s_], 128 * s_)

            def brv(j, lo, hi):
                return brs[:, j, 0:ns * (3 + L)].rearrange("p (s l) -> p s l", s=ns)[:, :, lo:hi]

            def v3(ap):
                return ap[:, 0:N].rearrange("p (s l) -> p s l", s=ns)

            slot_cv = {}
            slot_sg = {}

            def gates_chain(j):
                iis = [i for i in (j - 1, j, j + 1) if 0 <= i < NCH and blocks_overlap(i, j)]

                def mmr(e, w=wgr, pt=pr):
                    ins = None
                    for n_, i in enumerate(iis):
                        ins = e.matmul(out=pt[:, 0:N], lhsT=w[:, i, j - i + 1, :], rhs=cvb[slot_cv[i]][:, 0:N],
                                       start=(n_ == 0), stop=(n_ == len(iis) - 1))
                    return ins
                op(PE, mmr, reads=[b_wgr] + [b_cvb[slot_cv[i]] for i in iis], writes=[b_pr], c=0.75)
                op(PE, lambda e: mmr(e, wgi, pi_), reads=[b_wgi] + [b_cvb[slot_cv[i]] for i in iis], writes=[b_pi], c=0.75)
                k = cnt["t"] % NTS
                cnt["t"] += 1
                op(ACT, lambda e: e.activation(out=tA[k][:, 0:N], in_=pr[:, 0:N], func=AF.Tanh,
                                               bias=hbr[:, j:j + 1], scale=0.5),
                   reads=[b_pr, b_hbr], writes=[b_tA[k]], ts="e")
                op(ACT, lambda e: e.activation(out=tC[k][:, 0:N], in_=pi_[:, 0:N], func=AF.Tanh,
                                               bias=hbi[:, j:j + 1], scale=0.5),
                   reads=[b_pi, b_hbi], writes=[b_tC[k]], ts="e")
                op(ACT, lambda e: e.activation(out=tB[k][:, 0:N], in_=tA[k][:, 0:N], func=AF.Exp,
                                               bias=hc[:, j:j + 1], scale=hc[:, j:j + 1]),
                   reads=[b_tA[k], b_hc], writes=[b_tB[k]], ts="e")
                if KN('K_A2', 0) == 0:
                    op(POOL, lambda e: e.tensor_tensor(out=tA[k][:, 0:N], in0=tB[k][:, 0:N], in1=tB[k][:, 0:N], op=ALU.mult),
                       reads=[b_tB[k]], writes=[b_tA[k]])
                else:
                    op(ACT, lambda e: e.activation(out=tA[k][:, 0:N], in_=tB[k][:, 0:N], func=AF.Square),
                       reads=[b_tB[k]], writes=[b_tA[k]])
                op(ACT, lambda e: e.activation(out=tA[k][:, 0:N], in_=tA[k][:, 0:N], func=AF.Sqrt, bias=q25[:, 0:1], scale=-0.25),
                   reads=[b_tA[k], b_q25], writes=[b_tA[k]], ts="q")
                c_ = slot_cv[j]
                op(DVE, lambda e: e.scalar_tensor_tensor(out=tC[k][:, 0:N], in0=tC[k][:, 0:N], scalar=1.0,
                                                          in1=cv[c_][:, 0:N], op0=ALU.add, op1=ALU.mult),
                   reads=[b_tC[k], b_cv[c_]], writes=[b_tC[k]])
                op(POOL if KN('K_BE', 0) == 0 else DVE, lambda e: e.tensor_tensor(out=tC[k][:, 0:N], in0=tA[k][:, 0:N], in1=tC[k][:, 0:N], op=ALU.mult),
                   reads=[b_tA[k], b_tC[k]], writes=[b_tC[k]])
                for s_ in range(ns):
                    if kind == "s":
                        hcol = hst[:, seqs[s_], j:j + 1]
                        b_hcol = b_hst[j]
                    else:
                        hcol = hstp[:, seqs[0], j:j + 1]
                        b_hcol = b_hstp[seqs[0]][j]
                    op(DVE, lambda e, s_=s_, hcol=hcol: e.tensor_tensor_scan(
                        out=tH[k][:, s_ * L:(s_ + 1) * L], data0=tB[k][:, s_ * L:(s_ + 1) * L],
                        data1=tC[k][:, s_ * L:(s_ + 1) * L], initial=hcol, op0=ALU.mult, op1=ALU.add),
                       reads=[b_tB[k], b_tC[k], b_hcol], writes=[b_tH[k]], c=1.6 * L / 512 + 0.2)
                    op(POOL, lambda e, s_=s_, hcol=hcol: e.tensor_copy(out=hcol, in_=tH[k][:, (s_ + 1) * L - 1:(s_ + 1) * L]),
                       reads=[b_tH[k]], writes=[b_hcol], n=1)
                g_ = slot_sg[j]
                op(POOL if KN('K_HG', 0) == 0 else DVE, lambda e: e.tensor_tensor(out=hg[:, j, 0:N], in0=tH[k][:, 0:N], in1=sg[g_][:, 0:N], op=ALU.mult),
                   reads=[b_tH[k], b_sg[g_]], writes=[b_hg[j]])

            for j in range(NCH):
                u = cnt["ub"] % 2
                cnt["ub"] += 1
                ug, b_ug = ug_l[u], b_ug_l[u]

                def mm_in(e, col, pt):
                    ins = None
                    for kc in range(KC):
                        ins = e.matmul(out=pt[:, 0:N], lhsT=wia[:, kc, col:col + 128], rhs=xnT[:, kc, 0:N],
                                       start=(kc == 0), stop=(kc == KC - 1))
                    return ins
                op(PE, lambda e, j=j, u=u: mm_in(e, j * 128, ub[u]), reads=b_wia + b_xnT[0:nsub], writes=[b_ub[u]], c=2.0)
                op(PE, lambda e, j=j, ug=ug: mm_in(e, DR + j * 128, ug), reads=b_wia + b_xnT[0:nsub], writes=[b_ug], c=2.0)
                op(ACT, lambda e, j=j, u=u: e.activation(out=brv(j, 3, 3 + L), in_=v3(ub[u]), func=AF.Copy),
                   reads=[b_ub[u]], writes=[b_brs[j]])
                c_ = cnt["cv"] % NCV
                cnt["cv"] += 1
                slot_cv[j] = c_
                op(ACT, lambda e, j=j, c_=c_, u=u: e.activation(out=v3(cv[c_]), in_=v3(ub[u]), func=AF.Identity,
                                                                 scale=cw_t[:, 3, j:j + 1], bias=cb_t[:, j:j + 1]),
                   reads=[b_ub[u], b_cw, b_cb], writes=[b_cv[c_]])
                for tap in (2, 1, 0):
                    ce = DVE
                    op(ce, lambda e, j=j, c_=c_, tap=tap: e.scalar_tensor_tensor(
                        out=v3(cv[c_]), in0=brv(j, tap, tap + L), scalar=cw_t[:, tap, j:j + 1], in1=v3(cv[c_]),
                        op0=ALU.mult, op1=ALU.add),
                       reads=[b_brs[j], b_cw, b_cv[c_]], writes=[b_cv[c_]], c=0.9)
                op(ACT, lambda e, c_=c_: e.activation(out=cvb[c_][:, 0:N], in_=cv[c_][:, 0:N], func=AF.Copy),
                   reads=[b_cv[c_]], writes=[b_cvb[c_]])
                if kind == "p":
                    op(POOL, lambda e, j=j, sq0=seqs[0]: e.tensor_copy(out=hsv[:, sq0, j, :], in_=brs[:, j, L:L + 3]),
                       reads=[b_brs[j]], writes=[b_hsv[seqs[0]][j]], n=3)
                t_ = cnt["tg"] % 2
                cnt["tg"] += 1
                op(ACT, lambda e, t_=t_, ug=ug: e.activation(out=tg[t_][:, 0:N], in_=ug[:, 0:N], func=AF.Tanh, scale=0.5),
                   reads=[b_ug], writes=[b_tg[t_]], ts="e")
                g_ = cnt["sg"] % NSG
                cnt["sg"] += 1
                slot_sg[j] = g_
                op(DVE, lambda e, t_=t_, g_=g_, ug=ug: e.scalar_tensor_tensor(out=sg[g_][:, 0:N], in0=tg[t_][:, 0:N], scalar=1.0,
                                                                       in1=ug[:, 0:N], op0=ALU.add, op1=ALU.mult),
                   reads=[b_tg[t_], b_ug], writes=[b_sg[g_]], c=1.0)
                if j >= 1:
                    gates_chain(j - 1)
            gates_chain(NCH - 1)

            last = (kind == "s") or (tidx == SEQ // 512 - 1)
            if last:
                m_st = kb.mark()
                for s_ in range(ns):
                    oc = (p_conv if kind == "p" else s_conv)[seqs[s_]]
                    orr = (p_rnn if kind == "p" else s_rnn)[seqs[s_]]
                    for j in range(NCH):
                        if kind == "p":
                            srcv = hsv[:, seqs[0], j, :]
                            rb = b_hsv[seqs[0]][j]
                        else:
                            srcv = brv(j, L, L + 3)[:, s_, :]
                            rb = b_brs[j]
                        op(SP, lambda e, j=j, oc=oc, srcv=srcv: e.dma_start(
                            out=oc[:, j * 128:(j + 1) * 128].rearrange("r p -> p r"), in_=srcv),
                           reads=[rb], dsem=s_small_st, n=8)
                    if kind == "s":
                        op(SP, lambda e, orr=orr, hs=seqs[s_]: e.dma_start(out=orr.rearrange("(c p) -> p c", p=128), in_=hst[:, hs, :]),
                           reads=b_hst, dsem=s_small_st, n=16)
                    else:
                        op(SP, lambda e, orr=orr, hs=seqs[0]: e.dma_start(out=orr.rearrange("(c p) -> p c", p=128), in_=hstp[:, hs, :]),
                           reads=b_hstp[seqs[0]], dsem=s_small_st, n=16)
                sts = [o_ for o_ in kb.since(m_st) if o_.dsem is s_small_st]
                for j in range(NCH):
                    b_brs[j].r = b_brs[j].r + sts
                    b_hst[j].r = b_hst[j].r + sts

            for s_ in range(nsub):
                P = min(128, N - 128 * s_)
                for h in range(2):
                    def mm_out(e, h=h, s_=s_, P=P):
                        ins = None
                        for j in range(NCH):
                            ins = e.matmul(out=py[h][0:P, :], lhsT=hg[:, j, 128 * s_:128 * s_ + P],
                                           rhs=woa[:, j, h * 512:(h + 1) * 512], start=(j == 0), stop=(j == NCH - 1))
                        return ins
                    op(PE, mm_out, reads=b_hg + b_woa, writes=[b_py[h]], c=2.8)
                k = cnt["ss2"] % 2
                cnt["ss2"] += 1
                yi_ = cnt["yn"] % 2
                cnt["yn"] += 1
                yn, b_yn = yn_l[yi_], b_yn_l[yi_]
                for h in range(2):
                    op(ACT, lambda e, h=h, k=k, P=P: e.activation(out=junk2[0:P, h, :], in_=py[h][0:P, :], func=AF.Square,
                                                                  accum_out=ss2[0:P, 4 * k + h:4 * k + h + 1]),
                       reads=[b_py[h]], writes=[b_junk2[h], b_ss2[k]])
                    op(DVE, lambda e, h=h, P=P, yn=yn: e.tensor_copy(out=yn[0:P, h * 512:(h + 1) * 512], in_=py[h][0:P, :]),
                       reads=[b_py[h], b_junk2[h]], writes=[b_yn], c=0.7)
                op(DVE, lambda e, k=k, P=P: e.tensor_tensor(out=ss2[0:P, 4 * k + 2:4 * k + 3], in0=ss2[0:P, 4 * k:4 * k + 1],
                                                            in1=ss2[0:P, 4 * k + 1:4 * k + 2], op=ALU.add),
                   reads=[b_ss2[k]], writes=[b_ss2[k]])
                rstd_from_ss(ss2[0:P, 4 * k + 2:4 * k + 3], [b_ss2[k]], b_ss2[k], ss2[0:P, 4 * k + 3:4 * k + 4])
                for h in range(2):
                    op(DVE, lambda e, h=h, k=k, P=P, yn=yn: e.scalar_tensor_tensor(
                        out=yn[0:P, h * 512:(h + 1) * 512], in0=yn[0:P, h * 512:(h + 1) * 512], scalar=ss2[0:P, 4 * k + 3:4 * k + 4],
                        in1=gpost_a[0:P, h * 512:(h + 1) * 512], op0=ALU.mult, op1=ALU.mult),
                       reads=[b_yn, b_ss2[k], b_gpa], writes=[b_yn], c=0.8)
                i = cnt["xr"] % NXR
                cnt["xr"] += 1
                r0 = row0 + 128 * s_
                op(SP, lambda e, i=i, P=P, r0=r0: e.dma_start(out=xr[i][0:P, :], in_=src[r0:r0 + P, :]),
                   writes=[b_xr[i]], dsem=s_xr[i], n=1024)
                op(POOL, lambda e, i=i, P=P, yn=yn: e.tensor_tensor(out=xr[i][0:P, :], in0=xr[i][0:P, :], in1=yn[0:P, :], op=ALU.add),
                   reads=[b_xr[i], b_yn], writes=[b_xr[i]], n=1024)
                bx = kb.buf()
                x1_bufs[(kind, r0)] = bx
                op(SP, lambda e, i=i, P=P, r0=r0: e.dma_start(out=dst1[r0:r0 + P, :], in_=xr[i][0:P, :]),
                   reads=[b_xr[i]], writes=[bx], dsem=s_xo[i], n=1024)

        def blocks_overlap(i, j):
            for blk in range(16):
                lo, hi = 88 * blk, 88 * blk + 88
                if max(lo, 128 * i) < min(hi, 128 * i + 128) and max(lo, 128 * j) < min(hi, 128 * j + 128):
                    return True
            return False


        def phase_b():
            NRING = 10
            NSLOT = NRING
            stB = contextlib.ExitStack()
            psB = contextlib.ExitStack()
            wgb = sb(stB, "wgb", [128, KC, 1024], BF16)
            wob = sb(stB, "wob", [128, KC, 1024], BF16)
            b_wg = [kb.buf() for _ in range(KC)]
            b_wob = [kb.buf() for _ in range(KC)]
            s_cB = s_const_A
            m_cB = kb.mark()
            gpost_b = sb(stB, "gpost_b", [128, D])
            b_gpb = kb.buf()
            op(SP, lambda e: e.dma_start(out=gpost_b[:], in_=norm_post_b.partition_broadcast(128)), writes=[b_gpb], dsem=s_cB)
            cos_sb = sb(stB, "cos_sb", [128, NPOS])
            sin_sb = sb(stB, "sin_sb", [128, NPOS])
            b_cos, b_sin = kb.buf(), kb.buf()
            op(SP, lambda e: e.dma_start(out=cos_sb[:], in_=cos_d), writes=[b_cos], dsem=s_cB)
            op(SP, lambda e: e.dma_start(out=sin_sb[:], in_=sin_d), writes=[b_sin], dsem=s_cB)
            prot_f = sb(stB, "prot_f", [128, 128])
            b_protf = kb.buf()
            op(SP, lambda e: e.dma_start(out=prot_f[:], in_=prot_d), writes=[b_protf], dsem=s_cB)
            skb = sb(stB, "skb", [128, 8])
            b_skb = kb.buf()
            sk2 = sinks.rearrange("(hp two) -> two hp", two=2)
            op(SP, lambda e: e.dma_start(out=skb[0:64, :], in_=sk2[0:1, :].partition_broadcast(64)), writes=[b_skb], dsem=s_cB)
            op(SP, lambda e: e.dma_start(out=skb[64:128, :], in_=sk2[1:2, :].partition_broadcast(64)), dsem=s_cB)
            for b_ in (b_gpb, b_cos, b_sin, b_skb, b_protf):
                b_.w = [o_ for o_ in kb.since(m_cB) if o_.dsem is s_cB]
            prot_b = sb(stB, "prot_b", [128, 128], BF16)
            b_prot = kb.buf()
            op(DVE, lambda e: e.tensor_copy(out=prot_b[:], in_=prot_f[:]), reads=[b_protf], writes=[b_prot])
            esk = sb(stB, "esk", [128, 8])
            b_esk = kb.buf()
            op(ACT, lambda e: e.activation(out=esk[:], in_=skb[:], func=AF.Exp), reads=[b_skb], writes=[b_esk], ts="e")
            ones_bd = sb(stB, "ones_bd", [128, 128], BF16)
            ones_pt = sb(stB, "ones_pt", [128, 128], BF16)
            b_onesbd, b_onespt = kb.buf(), kb.buf()
            op(POOL, lambda e: e.memset(ones_bd[:], 0.0), writes=[b_onesbd])
            op(POOL, lambda e: e.memset(ones_bd[0:64, 0:64], 1.0), writes=[b_onesbd])
            op(POOL, lambda e: e.memset(ones_bd[64:128, 64:128], 1.0), writes=[b_onesbd])
            op(POOL, lambda e: e.memset(ones_pt[:], 0.0), writes=[b_onespt])
            op(POOL, lambda e: e.memset(ones_pt[0:32, 0:64], 1.0), writes=[b_onespt])
            op(POOL, lambda e: e.memset(ones_pt[64:96, 64:128], 1.0), writes=[b_onespt])

            NSB = 2
            stg2 = [sb(stB, "stgb%d" % i, [128, 1024]) for i in range(NSB)]
            b_stg2 = [kb.buf() for _ in range(NSB)]
            s_stg2 = s_stg_A
            ci = 0
            for kc in range(KC):
                k2 = ci % NSB
                ci += 1
                gs = g_b[:, kc:kc + 1]
                st_ = stg2[k2]
                op(SP if ci % 2 else ACT, lambda e, kc=kc, st_=st_: e.dma_start(out=st_[:, 0:1024], in_=w_in_b[kc * 128:(kc + 1) * 128, 1024:2048]),
                   writes=[b_stg2[k2]], dsem=s_stg2[k2], n=1024)
                op(ACT, lambda e, kc=kc, st_=st_, gs=gs: e.activation(out=wgb[:, kc, :], in_=st_[:, 0:1024], func=AF.Identity, scale=gs),
                   reads=[b_stg2[k2], b_g_b], writes=[b_wg[kc]], n=1024)
            for kc in range(KC):
                k2 = ci % NSB
                ci += 1
                st_ = stg2[k2]
                op(SP if ci % 2 else ACT, lambda e, kc=kc, st_=st_: e.dma_start(out=st_[:, 0:1024], in_=w_out_b[kc * 128:(kc + 1) * 128, :]),
                   writes=[b_stg2[k2]], dsem=s_stg2[k2])
                op(DVE,
                   lambda e, kc=kc, st_=st_: e.tensor_scalar(out=wob[:, kc, :], in0=st_[:, 0:1024], scalar1=0.5, scalar2=None, op0=ALU.mult),
                   reads=[b_stg2[k2]], writes=[b_wob[kc]])

            NT = 512
            PADL = 64
            kbd = sb(stB, "kbd", [128, NSLOT, 4, 128], BF16)
            vbd = sb(stB, "vbd", [128, NSLOT, 4, 128], BF16)
            b_kbd = [kb.buf() for _ in range(NSLOT)]
            b_vbd = [kb.buf() for _ in range(NSLOT)]
            op(POOL, lambda e: e.memset(kbd[:], 0.0), writes=b_kbd)
            op(POOL, lambda e: e.memset(vbd[:], 0.0), writes=b_vbd)
            xnT = sb(stB, "xnTb", [128, KC, PADL + NT + 64], BF16)
            b_xnT = [kb.buf() for _ in range(4)]
            op(POOL, lambda e: e.memset(xnT[:], 0.0), writes=b_xnT)
            qT = sb(stB, "qT", [128, KC, NT], BF16)
            b_qT = [kb.buf() for _ in range(KC)]
            ogb = sb(stB, "ogb", [128, KC, NT], BF16)
            b_ogb = [kb.buf() for _ in range(KC)]
            NXB = 2
            xb = [sb(stB, "xbb%d" % i, [128, D]) for i in range(NXB)]
            b_xb = [kb.buf() for _ in range(NXB)]
            s_xb = s_xb_A
            s_xo = s_xo_A
            s_xr = s_xr_A
            NXR = 2
            xr = [sb(stB, "xrb%d" % i, [128, D]) for i in range(NXR)]
            b_xr = [kb.buf() for _ in range(NXR)]
            xnb = [sb(stB, "xnbb%d" % i, [128, D], BF16) for i in range(2)]
            b_xnb = [kb.buf() for _ in range(2)]
            junk = sb(stB, "junkb", [128, D], BF16)
            b_junk = kb.buf()
            junk2 = sb(stB, "junkb2", [128, 2, 512], BF16)
            b_junk2 = [kb.buf() for _ in range(2)]
            yn_l = [sb(stB, "ynb%d" % i, [128, D]) for i in range(2)]
            b_yn_l = [kb.buf() for _ in range(2)]
            ss = sb(stB, "ssb", [128, 8])
            b_ss = [kb.buf() for _ in range(4)]
            ss2 = sb(stB, "ss2b", [128, 8])
            b_ss2 = [kb.buf() for _ in range(2)]
            zb_l = [sb(stB, "zb%d" % i, [128, NT], BF16) for i in range(2)]
            b_zb_l = [kb.buf() for _ in range(2)]
            wtmp = wbigv[:, 0:8192].bitcast(F32).rearrange("p (k n) -> p k n", k=8)
            t1_l = [wtmp[:, i, :] for i in range(2)]
            t2_l = [wtmp[:, 2 + i, :] for i in range(2)]
            b_t1_l = [kb.buf() for _ in range(2)]
            b_t2_l = [kb.buf() for _ in range(2)]
            kf = [sb(stB, "kf%d" % i, [128, NT]) for i in range(2)]
            b_kf = [kb.buf() for _ in range(2)]
            NPT = 3
            pT = [sb(stB, "pT%d" % i, [128, 384], BF16) for i in range(NPT)]
            b_pT = [kb.buf() for _ in range(NPT)]
            tg_l = [sb(stB, "tgb%d" % i, [128, NT]) for i in range(2)]
            sg_l = [sb(stB, "sgb%d" % i, [128, NT]) for i in range(2)]
            b_tg_l = [kb.buf() for _ in range(2)]
            b_sg_l = [kb.buf() for _ in range(2)]
            rden = [wtmp[:, 4 + i, :] for i in range(2)]
            oraw = [wtmp[:, 6 + i, :] for i in range(2)]
            b_rden = [kb.buf() for _ in range(2)]
            b_oraw = [kb.buf() for _ in range(2)]
            vout = sb(stB, "vout", [128, 256])
            kout = sb(stB, "kout", [128, 256])
            b_vout, b_kout = kb.buf(), kb.buf()
            s_vo, s_ko = kb.dsem("vo"), kb.dsem("ko")
            ck = sb(stB, "ck", [128, 256])
            ckb = sb(stB, "ckb", [128, 2, 256], BF16)
            cvt = sb(stB, "cvt", [128, 2, 256])
            b_ck, b_ckb, b_cvt = kb.buf(), kb.buf(), kb.buf()
            s_ck, s_cv = kb.dsem("ckl"), kb.dsem("cvl")

            pz_l = [ps(psB, "pz%d" % i, [128, NT]) for i in range(2)]
            b_pz_l = [kb.buf() for _ in range(2)]
            pz, b_pz = pz_l[0], b_pz_l[0]
            pzr = ps(psB, "pzr", [128, NT])
            b_pzr = kb.buf()
            tpb = pzr[:, :].bitcast(BF16).rearrange("p (k t) -> p k t", k=8)
            b_tpb = b_pzr
            pS_l = [ps(psB, "pS0", [128, NT])] * 2
            b_pS_l = [kb.buf()] * 2
            pO = [ps(psB, "pO%d" % i, [128, NT]) for i in range(2)]
            pD = [ps(psB, "pD%d" % i, [128, NT]) for i in range(2)]
            b_pO = [kb.buf() for _ in range(2)]
            b_pD = [kb.buf() for _ in range(2)]
            cntb = {"xb": 0, "xnb": 0, "ss": 0, "ss2": 0, "pT": 0, "xr": 0, "pS": 0, "pz": 0, "rp": 0, "gt": 0, "yn": 0}

            def lnt(src, row0, P, b_dst, col0, extra):
                i = cntb["xb"] % NXB
                cntb["xb"] += 1
                op(SP, lambda e: e.dma_start(out=xb[i][0:P, :], in_=src[row0:row0 + P, :]), reads=extra, writes=[b_xb[i]], dsem=s_xb[i])
                k = cntb["ss"] % 4
                cntb["ss"] += 1
                op(ACT, lambda e: e.activation(out=junk[0:P, :], in_=xb[i][0:P, :], func=AF.Square, accum_out=ss[0:P, 2 * k:2 * k + 1]),
                   reads=[b_xb[i]], writes=[b_junk, b_ss[k]])
                rstd_from_ss(ss[0:P, 2 * k:2 * k + 1], [b_ss[k]], b_ss[k], ss[0:P, 2 * k + 1:2 * k + 2])
                n = cntb["xnb"] % 2
                cntb["xnb"] += 1
                op(DVE, lambda e: e.tensor_scalar(out=xnb[n][0:P, :], in0=xb[i][0:P, :], scalar1=ss[0:P, 2 * k + 1:2 * k + 2], scalar2=None, op0=ALU.mult),
                   reads=[b_xb[i], b_ss[k]], writes=[b_xnb[n]])

                def tr(e):
                    ins = None
                    for kc in range(KC):
                        ins = e.transpose(out=tpb[:, kc, 0:P], in_=xnb[n][0:P, kc * 128:(kc + 1) * 128], identity=ident_b[0:P, 0:P])
                    return ins
                op(PE, tr, reads=[b_xnb[n], b_ident_b], writes=[b_tpb], c=0.7)
                op(ACT, lambda e: e.activation(out=xnT[:, :, col0:col0 + P], in_=tpb[:, :, 0:P], func=AF.Copy), reads=[b_tpb], writes=[b_dst])

            def slot_runs(base, c0, n):
                runs = []
                i = 0
                while i < n:
                    s0 = (c0 + i) % NRING
                    cntr = min(n - i, NRING - s0)
                    runs.append((base + s0, cntr, i))
                    i += cntr
                return runs

            def phaseB_tile(kind, seqs, L, row0, tidx):
                ns = len(seqs)
                N = ns * L
                src1 = x1p if kind == "p" else x1s
                dsty = yp if kind == "p" else ys
                nsub = (N + 127) // 128
                pos0 = 512 * tidx if kind == "p" else SEQ
                cosv = cos_sb[:, pos0:pos0 + N]
                sinv = sin_sb[:, pos0:pos0 + N]
                last = (kind == "s") or (tidx == SEQ // 512 - 1)
                for s_ in range(nsub):
                    P = min(128, N - 128 * s_)
                    r0 = row0 + 128 * s_
                    lnt(src1, r0, P, b_xnT[s_], PADL + 128 * s_, [x1_bufs[(kind, r0)]])
                xall = b_xnT[0:nsub]

                def proj(e, w, col, pt):
                    ins = None
                    for kc in range(KC):
                        ins = e.matmul(out=pt[:, 0:N], lhsT=w[:, kc, col:col + 128], rhs=xnT[:, kc, PADL:PADL + N],
                                       start=(kc == 0), stop=(kc == KC - 1))
                    return ins

                def proj_rope(w, col, rd):
                    i_ = cntb["pz"] % 2
                    cntb["pz"] += 1
                    pzx, b_pzx = pz_l[i_], b_pz_l[i_]
                    r_ = cntb["rp"] % 2
                    cntb["rp"] += 1
                    zb, b_zb = zb_l[r_], b_zb_l[r_]
                    t1, t2, b_t1, b_t2 = t1_l[r_], t2_l[r_], b_t1_l[r_], b_t2_l[r_]
                    op(PE, lambda e: proj(e, w, col, pzx), reads=rd + xall, writes=[b_pzx], c=2.0)
                    op(ACT, lambda e: e.activation(out=zb[:, 0:N], in_=pzx[:, 0:N], func=AF.Copy), reads=[b_pzx], writes=[b_zb])
                    op(PE, lambda e: e.matmul(out=pzr[:, 0:N], lhsT=prot_b[:, :], rhs=zb[:, 0:N], start=True, stop=True),
                       reads=[b_zb, b_prot], writes=[b_pzr], c=0.3)
                    op(DVE, lambda e: e.tensor_tensor(out=t1[:, 0:N], in0=pzx[:, 0:N], in1=cosv, op=ALU.mult), reads=[b_pzx, b_cos, b_zb], writes=[b_t1])
                    op(DVE, lambda e: e.tensor_tensor(out=t2[:, 0:N], in0=pzr[:, 0:N], in1=sinv, op=ALU.mult), reads=[b_pzr, b_sin], writes=[b_t2])
                    return t1, t2, b_t1, b_t2

                if kind == "p":
                    c_first = 8 * tidx
                    nchk = 8
                    sbase = 0
                    key_slots = None
                else:
                    c_first = 0
                    nchk = 0

                if kind == "s":
                    for s_ in range(ns):
                        sl = 3 * s_ + 2
                        op(POOL, lambda e, sl=sl: e.memset(kbd[:, sl, :, :], 0.0), writes=[b_kbd[sl]])
                        op(POOL, lambda e, sl=sl: e.memset(vbd[:, sl, :, :], 0.0), writes=[b_vbd[sl]])
                for kc2 in range(4):
                    t1, t2, b_t1, b_t2 = proj_rope(wk, kc2 * 128, b_wk)
                    if kc2 < 2:
                        gt, gb = 2 * kc2, 2 * kc2 + 1
                        kfi = kf[kc2]
                        op(POOL, lambda e, kfi=kfi, t1=t1, t2=t2: e.tensor_tensor(out=kfi[:, 0:N], in0=t1[:, 0:N], in1=t2[:, 0:N], op=ALU.add),
                           reads=[b_t1, b_t2], writes=[b_kf[kc2]])
                        srcs = (kfi, None)
                    else:
                        gt, gb = 2 * (kc2 - 2) + 1, 2 * (kc2 - 2)
                        srcs = (t1, t2)
                    for (plo, gg, clo) in ((0, gt, 0), (64, gb, 64)):
                        if kind == "p":
                            for (s0, cn, off) in slot_runs(sbase, c_first, nchk):
                                outv = kbd[plo:plo + 64, s0:s0 + cn, gg, clo:clo + 64]
                                if srcs[1] is None:
                                    inv = srcs[0][plo:plo + 64, off * 64:(off + cn) * 64].rearrange("p (c k) -> p c k", k=64)
                                    op(POOL, lambda e, outv=outv, inv=inv: e.tensor_copy(out=outv, in_=inv),
                                       reads=[b_kf[kc2]], writes=[b_kbd[s0 + i_] for i_ in range(cn)])
                                else:
                                    in0 = t1[plo:plo + 64, off * 64:(off + cn) * 64].rearrange("p (c k) -> p c k", k=64)
                                    in1 = t2[plo:plo + 64, off * 64:(off + cn) * 64].rearrange("p (c k) -> p c k", k=64)
                                    op(POOL, lambda e, outv=outv, in0=in0, in1=in1: e.tensor_tensor(out=outv, in0=in0, in1=in1, op=ALU.add),
                                       reads=[b_t1, b_t2], writes=[b_kbd[s0 + i_] for i_ in range(cn)])
                        else:
                            for s_ in range(ns):
                                sl = 3 * s_ + 2
                                outv = kbd[plo:plo + 64, sl, gg, clo:clo + L]
                                if srcs[1] is None:
                                    op(POOL, lambda e, outv=outv, s_=s_, plo=plo, kfi=srcs[0]: e.tensor_copy(out=outv, in_=kfi[plo:plo + 64, s_ * L:(s_ + 1) * L]),
                                       reads=[b_kf[kc2]], writes=[b_kbd[sl]])
                                else:
                                    op(POOL, lambda e, outv=outv, s_=s_, plo=plo, t1=t1, t2=t2: e.tensor_tensor(out=outv, in0=t1[plo:plo + 64, s_ * L:(s_ + 1) * L],
                                                                                                  in1=t2[plo:plo + 64, s_ * L:(s_ + 1) * L], op=ALU.add),
                                       reads=[b_t1, b_t2], writes=[b_kbd[sl]])
                if last:
                    for s_ in range(ns):
                        Pk = 128 if kind == "p" else L
                        c0 = N - 128 if kind == "p" else s_ * L

                        def trk(e, c0=c0, Pk=Pk):
                            ins = None
                            for kc2 in range(2):
                                ins = e.transpose(out=pz[0:Pk, kc2 * 128:(kc2 + 1) * 128], in_=kf[kc2][:, c0:c0 + Pk], identity=ident_f[:, :])
                            return ins
                        op(PE, trk, reads=b_kf + [b_ident_f], writes=[b_pz])
                        op(ACT, lambda e, Pk=Pk: e.activation(out=kout[0:Pk, :], in_=pz[0:Pk, 0:256], func=AF.Copy), reads=[b_pz], writes=[b_kout])
                        dk = (p_k if kind == "p" else s_k)[seqs[s_]]
                        op(SP, lambda e, dk=dk, Pk=Pk: e.dma_start(out=dk, in_=kout[0:Pk, :]), reads=[b_kout], dsem=s_ko)

                def vproj(e, c0, M, pt):
                    ins = None
                    for kc in range(KC):
                        ins = e.matmul(out=pt[0:M, 0:256], lhsT=xnT[:, kc, c0:c0 + M], rhs=wv[:, kc, :], start=(kc == 0), stop=(kc == KC - 1))
                    return ins
                if kind == "p":
                    for s_ in range(4):
                        op(PE, lambda e, s_=s_: vproj(e, PADL + 128 * s_, 128, pz), reads=b_wk + xall, writes=[b_pz], c=1.2)
                        ce, co = c_first + 2 * s_, c_first + 2 * s_ + 1
                        se, so = sbase + ce % NRING, sbase + co % NRING
                        pzv = pz[:, 0:256].rearrange("p (g d) -> p g d", g=4)
                        op(ACT, lambda e, se=se, pzv=pzv: e.activation(out=vbd[0:64, se, :, 0:64], in_=pzv[0:64], func=AF.Copy), reads=[b_pz], writes=[b_vbd[se]])
                        op(ACT, lambda e, so=so, pzv=pzv: e.activation(out=vbd[64:128, so, :, 64:128], in_=pzv[64:128], func=AF.Copy), reads=[b_pz], writes=[b_vbd[so]])
                        if last and s_ == 3:
                            op(ACT, lambda e: e.activation(out=vout[:, :], in_=pz[:, 0:256], func=AF.Copy), reads=[b_pz], writes=[b_vout])
                            op(SP, lambda e: e.dma_start(out=p_v[seqs[0]], in_=vout[:, :]), reads=[b_vout], dsem=s_vo)
                    for s_ in range(5):
                        op(PE, lambda e, s_=s_: vproj(e, 128 * s_, 128, pzr), reads=b_wk + xall, writes=[b_pzr], c=1.2)
                        pzv = pzr[:, 0:256].rearrange("p (g d) -> p g d", g=4)
                        if s_ >= 1:
                            so = sbase + (c_first + 2 * s_ - 1) % NRING
                            op(ACT, lambda e, so=so, pzv=pzv: e.activation(out=vbd[0:64, so, :, 0:64], in_=pzv[0:64], func=AF.Copy), reads=[b_pzr], writes=[b_vbd[so]])
                        if s_ <= 3:
                            se = sbase + (c_first + 2 * s_) % NRING
                            op(ACT, lambda e, se=se, pzv=pzv: e.activation(out=vbd[64:128, se, :, 64:128], in_=pzv[64:128], func=AF.Copy), reads=[b_pzr], writes=[b_vbd[se]])
                else:
                    for s_ in range(ns):
                        sl = 3 * s_ + 2
                        op(PE, lambda e, s_=s_: vproj(e, PADL + L * s_, L, pz), reads=b_wk + xall, writes=[b_pz])
                        pzv = pz[:, 0:256].rearrange("p (g d) -> p g d", g=4)
                        op(ACT, lambda e, sl=sl, pzv=pzv: e.activation(out=vbd[0:L, sl, :, 0:64], in_=pzv[0:L], func=AF.Copy), reads=[b_pz], writes=[b_vbd[sl]])
                        op(ACT, lambda e: e.activation(out=vout[0:L, :], in_=pz[0:L, 0:256], func=AF.Copy), reads=[b_pz], writes=[b_vout])
                        op(SP, lambda e, s_=s_: e.dma_start(out=s_v[seqs[s_]], in_=vout[0:L, :]), reads=[b_vout], dsem=s_vo)
                        op(PE, lambda e, s_=s_: vproj(e, PADL + L * s_ - 64, 64 + L, pzr), reads=b_wk + xall, writes=[b_pzr])
                        pzv2 = pzr[:, 0:256].rearrange("p (g d) -> p g d", g=4)
                        op(ACT, lambda e, sl=sl, pzv2=pzv2: e.activation(out=vbd[64:64 + L, sl, :, 64:128], in_=pzv2[64:64 + L], func=AF.Copy), reads=[b_pzr], writes=[b_vbd[sl]])
                        sq = seqs[s_]
                        op(SP, lambda e, sq=sq: e.dma_start(out=ck[:, :], in_=cache_k[sq]), writes=[b_ck], dsem=s_ck)
                        ckv = ck[:, :].rearrange("p (a b d) -> p a b d", a=2, b=2)
                        op(DVE, lambda e: e.tensor_copy(out=ckb[:, 0, :], in_=ck[:, :]), reads=[b_ck], writes=[b_ckb])
                        cks = ckb[:, 1, :].rearrange("p (a b d) -> p a b d", a=2, b=2)
                        for b2 in range(2):
                            op(DVE, lambda e, b2=b2, cks=cks, ckv=ckv: e.tensor_copy(out=cks[:, :, b2, :], in_=ckv[:, :, 1 - b2, :]), reads=[b_ck], writes=[b_ckb])

                        def trc(e):
                            ins = None
                            for j4 in range(4):
                                ins = e.transpose(out=tpb[:, j4, :], in_=ckb[:, j4 // 2, (j4 % 2) * 128:(j4 % 2 + 1) * 128], identity=ident_b[:, :])
                            return ins
                        op(PE, trc, reads=[b_ckb, b_ident_b], writes=[b_tpb])
                        for j4 in range(4):
                            jj = j4 % 2
                            if j4 < 2:
                                gt, gb = 2 * jj, 2 * jj + 1
                            else:
                                gt, gb = 2 * jj + 1, 2 * jj
                            for (plo, gg, clo) in ((0, gt, 0), (64, gb, 64)):
                                outv = kbd[plo:plo + 64, 3 * s_:3 * s_ + 2, gg, clo:clo + 64]
                                inv = tpb[plo:plo + 64, j4, :].rearrange("p (m k) -> p m k", m=2)
                                op(ACT, lambda e, outv=outv, inv=inv: e.activation(out=outv, in_=inv, func=AF.Copy),
                                   reads=[b_tpb], writes=[b_kbd[3 * s_], b_kbd[3 * s_ + 1]])
                        m_cv = kb.mark()
                        for m in range(2):
                            for hh in range(2):
                                op(SP, lambda e, sq=sq, m=m, hh=hh: e.dma_start(out=cvt[64 * hh:64 * hh + 64, m, :], in_=cache_v[sq, 64 * m:64 * m + 64, :]),
                                   writes=[b_cvt], dsem=s_cv)
                        b_cvt.w = [o_ for o_ in kb.since(m_cv) if o_.dsem is s_cv]
                        cv4 = cvt[:, :, :].rearrange("p m (g d) -> p m g d", g=4)
                        for m in range(2):
                            op(DVE, lambda e, m=m, s_=s_, cv4=cv4: e.tensor_copy(out=vbd[0:64, 3 * s_ + m, :, 0:64], in_=cv4[0:64, m]), reads=[b_cvt], writes=[b_vbd[3 * s_ + m]])
                            op(DVE, lambda e, m=m, s_=s_, cv4=cv4: e.tensor_copy(out=vbd[64:128, 3 * s_ + m, :, 64:128], in_=cv4[64:128, m]), reads=[b_cvt], writes=[b_vbd[3 * s_ + m]])

                for qc in range(KC):
                    t1, t2, b_t1, b_t2 = proj_rope(wq, qc * 128, b_wq)
                    op(POOL, lambda e, qc=qc, t1=t1, t2=t2: e.tensor_tensor(out=qT[:, qc, 0:N], in0=t1[:, 0:N], in1=t2[:, 0:N], op=ALU.add),
                       reads=[b_t1, b_t2], writes=[b_qT[qc]])

                for g in range(4):
                    jobs = []
                    if kind == "p":
                        for c in range(max(c_first - 2, 0), c_first + 8):
                            qlo = max(c, c_first)
                            qhi = min(c + 2, c_first + 7)
                            jobs.append((sbase + c % NRING, (qlo - c_first) * 64, (qhi - qlo + 1) * 64, ones_bd, b_onesbd))
                    else:
                        for s_ in range(ns):
                            for m in range(3):
                                jobs.append((3 * s_ + m, s_ * L, L, ones_bd if m < 2 else ones_pt, b_onesbd if m < 2 else b_onespt))
                    for ji, (sl, q0, nq, onesm, b_onesm) in enumerate(jobs):
                        first = (ji == 0)

                        si_ = cntb["pS"] % 2
                        cntb["pS"] += 1
                        pS = pS_l[si_]
                        b_pS = b_pS_l[si_]

                        def mms(e, sl=sl, q0=q0, nq=nq, g=g, pS=pS):
                            ins = None
                            for hpi in range(2):
                                ins = e.matmul(out=pS[:, hpi * 192:hpi * 192 + nq], lhsT=kbd[:, sl, g, :], rhs=qT[:, 2 * g + hpi, q0:q0 + nq],
                                               start=True, stop=True)
                            return ins
                        op(PE, mms, reads=[b_kbd[sl], b_qT[2 * g], b_qT[2 * g + 1]], writes=[b_pS], c=0.35)
                        pi_ = cntb["pT"] % NPT
                        cntb["pT"] += 1
                        psv = pS[:, 0:384].rearrange("p (h q) -> p h q", h=2)[:, :, 0:nq]
                        ptv = pT[pi_][:, :].rearrange("p (h q) -> p h q", h=2)[:, :, 0:nq]
                        op(ACT, lambda e, psv=psv, ptv=ptv: e.activation(out=ptv, in_=psv, func=AF.Exp, scale=0.125), reads=[b_pS], writes=[b_pT[pi_]], ts="e")

                        def mmo(e, sl=sl, q0=q0, nq=nq, g=g, pi_=pi_, onesm=onesm, first=first):
                            ins = None
                            for hpi in range(2):
                                ins = e.matmul(out=pO[hpi][:, q0:q0 + nq], lhsT=vbd[:, sl, g, :], rhs=pT[pi_][:, hpi * 192:hpi * 192 + nq],
                                               start=first, stop=False, skip_group_check=True)
                            for hpi in range(2):
                                ins = e.matmul(out=pD[hpi][:, q0:q0 + nq], lhsT=onesm[:, :], rhs=pT[pi_][:, hpi * 192:hpi * 192 + nq],
                                               start=first, stop=False, skip_group_check=True)
                            return ins
                        op(PE, mmo, reads=[b_vbd[sl], b_pT[pi_], b_onesm], writes=b_pO + b_pD, c=0.7)
                    for hpi in range(2):
                        hp = 2 * g + hpi
                        op(ACT, lambda e, hpi=hpi: e.activation(out=oraw[hpi][:, 0:N], in_=pO[hpi][:, 0:N], func=AF.Copy),
                           reads=[b_pO[hpi]], writes=[b_oraw[hpi]])
                        op(DVE, lambda e, hpi=hpi, hp=hp: e.tensor_scalar(out=rden[hpi][:, 0:N], in0=pD[hpi][:, 0:N], scalar1=esk[:, hp:hp + 1], scalar2=None, op0=ALU.add),
                           reads=[b_pD[hpi], b_esk], writes=[b_rden[hpi]])
                    for hpi in range(2):
                        hp = 2 * g + hpi
                        op(DVE, lambda e, hpi=hpi: e.reciprocal(out=rden[hpi][:, 0:N], in_=rden[hpi][:, 0:N]), reads=[b_rden[hpi]], writes=[b_rden[hpi]], c=1.8)
                        op(POOL, lambda e, hpi=hpi: e.tensor_tensor(out=oraw[hpi][:, 0:N], in0=oraw[hpi][:, 0:N], in1=rden[hpi][:, 0:N], op=ALU.mult),
                           reads=[b_oraw[hpi], b_rden[hpi]], writes=[b_oraw[hpi]])
                        i_ = cntb["pz"] % 2
                        cntb["pz"] += 1
                        pzx, b_pzx = pz_l[i_], b_pz_l[i_]
                        g_ = cntb["gt"] % 2
                        cntb["gt"] += 1
                        tgx, sgx, b_tgx, b_sgx = tg_l[g_], sg_l[g_], b_tg_l[g_], b_sg_l[g_]
                        op(PE, lambda e, hp=hp, pzx=pzx: proj(e, wgb, hp * 128, pzx), reads=b_wg + xall, writes=[b_pzx], c=2.0)
                        op(ACT, lambda e, pzx=pzx, tgx=tgx: e.activation(out=tgx[:, 0:N], in_=pzx[:, 0:N], func=AF.Tanh, scale=0.5), reads=[b_pzx], writes=[b_tgx], ts="e")
                        op(DVE, lambda e, pzx=pzx, tgx=tgx, sgx=sgx: e.scalar_tensor_tensor(out=sgx[:, 0:N], in0=tgx[:, 0:N], scalar=1.0, in1=pzx[:, 0:N], op0=ALU.add, op1=ALU.mult),
                           reads=[b_tgx, b_pzx], writes=[b_sgx], c=1.0)
                        op(POOL, lambda e, hp=hp, hpi=hpi, sgx=sgx: e.tensor_tensor(out=ogb[:, hp, 0:N], in0=oraw[hpi][:, 0:N], in1=sgx[:, 0:N], op=ALU.mult),
                           reads=[b_oraw[hpi], b_sgx], writes=[b_ogb[hp]])

                pyb = [pz, pzr]
                b_pyb = [b_pz, b_pzr]
                for s_ in range(nsub):
                    P = min(128, N - 128 * s_)
                    for h in range(2):
                        def mm_out(e, h=h, s_=s_, P=P):
                            ins = None
                            for kc in range(KC):
                                ins = e.matmul(out=pyb[h][0:P, :], lhsT=ogb[:, kc, 128 * s_:128 * s_ + P], rhs=wob[:, kc, h * 512:(h + 1) * 512],
                                               start=(kc == 0), stop=(kc == KC - 1))
                            return ins
                        op(PE, mm_out, reads=b_ogb + b_wob, writes=[b_pyb[h]], c=2.0)
                    k = cntb["ss2"] % 2
                    cntb["ss2"] += 1
                    yi_ = cntb["yn"] % 2
                    cntb["yn"] += 1
                    yn, b_yn = yn_l[yi_], b_yn_l[yi_]
                    for h in range(2):
                        op(ACT, lambda e, h=h, k=k, P=P: e.activation(out=junk2[0:P, h, :], in_=pyb[h][0:P, :], func=AF.Square,
                                                                      accum_out=ss2[0:P, 4 * k + h:4 * k + h + 1]),
                           reads=[b_pyb[h]], writes=[b_junk2[h], b_ss2[k]])
                        op(DVE, lambda e, h=h, P=P, yn=yn: e.tensor_copy(out=yn[0:P, h * 512:(h + 1) * 512], in_=pyb[h][0:P, :]),
                           reads=[b_pyb[h], b_junk2[h]], writes=[b_yn], c=0.7)
                    op(DVE, lambda e, k=k, P=P: e.tensor_tensor(out=ss2[0:P, 4 * k + 2:4 * k + 3], in0=ss2[0:P, 4 * k:4 * k + 1],
                                                                in1=ss2[0:P, 4 * k + 1:4 * k + 2], op=ALU.add), reads=[b_ss2[k]], writes=[b_ss2[k]])
                    rstd_from_ss(ss2[0:P, 4 * k + 2:4 * k + 3], [b_ss2[k]], b_ss2[k], ss2[0:P, 4 * k + 3:4 * k + 4])
                    for h in range(2):
                        op(DVE, lambda e, h=h, k=k, P=P, yn=yn: e.scalar_tensor_tensor(
                            out=yn[0:P, h * 512:(h + 1) * 512], in0=yn[0:P, h * 512:(h + 1) * 512], scalar=ss2[0:P, 4 * k + 3:4 * k + 4],
                            in1=gpost_b[0:P, h * 512:(h + 1) * 512], op0=ALU.mult, op1=ALU.mult),
                           reads=[b_yn, b_ss2[k], b_gpb], writes=[b_yn], c=0.8)
                    i = cntb["xr"] % NXR
                    cntb["xr"] += 1
                    r0 = row0 + 128 * s_
                    op(SP, lambda e, i=i, P=P, r0=r0: e.dma_start(out=xr[i][0:P, :], in_=src1[r0:r0 + P, :]),
                       reads=[x1_bufs[(kind, r0)]], writes=[b_xr[i]], dsem=s_xr[i], n=1024)
                    op(POOL, lambda e, i=i, P=P, yn=yn: e.tensor_tensor(out=xr[i][0:P, :], in0=xr[i][0:P, :], in1=yn[0:P, :], op=ALU.add),
                       reads=[b_xr[i], b_yn], writes=[b_xr[i]], n=1024)
                    op(SP, lambda e, i=i, P=P, r0=r0: e.dma_start(out=dsty[r0:r0 + P, :], in_=xr[i][0:P, :]), reads=[b_xr[i]], dsem=s_xo[i], n=1024)

            for tl in tiles:
                phaseB_tile(*tl)
            kb.emit(block, last=True)
            psB.close()
            stB.close()

        for tl in tiles:
            phaseA_tile(*tl)

        if do_b:
            gate_t = sb(stA, "gate_t", [128, 1])
            b_gate = kb.buf()
            op(POOL, lambda e: e.memset(gate_t[:], 0.0), after=b_wia, writes=[b_gate], n=1)
            ci = 0
            for kc in range(KC):
                k2 = ci % 2
                ci += 1
                gs = g_kv[:, kc:kc + 1]
                st_ = stgE[k2]
                op(SP, lambda e, kc=kc, st_=st_: e.dma_start(out=st_[:, 0:512], in_=w_kv[kc * 128:(kc + 1) * 128, :]),
                   reads=[b_gate], writes=[b_stgE[k2]], dsem=s_stg[k2], n=512)
                rd = [b_stgE[k2], b_g_kv, b_gate]
                op(DVE, lambda e, kc=kc, st_=st_, gs=gs: e.tensor_scalar(out=wk[:, kc, 0:256], in0=st_[:, 0:256], scalar1=gs, scalar2=None, op0=ALU.mult),
                   reads=rd, writes=[b_wk[kc]], n=256)
                op(ACT, lambda e, kc=kc, st_=st_, gs=gs: e.activation(out=wv[:, kc, :], in_=st_[:, 256:512], func=AF.Identity, scale=gs),
                   reads=rd, writes=[b_wk[kc]], n=256)
                kin = st_[:, 0:256].rearrange("p (a b d) -> p a b d", a=2, b=2)
                ksw = wk[:, kc, 256:512].rearrange("p (a b d) -> p a b d", a=2, b=2)
                for b2 in range(2):
                    op(DVE,
                       lambda e, kin=kin, ksw=ksw, b2=b2, gs=gs: e.tensor_scalar(out=ksw[:, :, b2, :], in0=kin[:, :, 1 - b2, :], scalar1=gs, scalar2=None, op0=ALU.mult),
                       reads=rd, writes=[b_wk[kc]], n=128)
            for kc in range(KC):
                k2 = ci % 2
                ci += 1
                gs = g_b[:, kc:kc + 1]
                st_ = stgE[k2]
                op(SP, lambda e, kc=kc, st_=st_: e.dma_start(out=st_[:, 0:1024], in_=w_in_b[kc * 128:(kc + 1) * 128, 0:1024]),
                   reads=[b_gate], writes=[b_stgE[k2]], dsem=s_stg[k2], n=1024)
                op(DVE, lambda e, kc=kc, st_=st_, gs=gs: e.tensor_scalar(out=wq[:, kc, :], in0=st_[:, 0:1024], scalar1=gs, scalar2=None, op0=ALU.mult),
                   reads=[b_stgE[k2], b_g_b, b_gate], writes=[b_wq[kc]], n=1024)

        if not do_b:
            for (kind, r0), bx in x1_bufs.items():
                P = 128 if kind == "p" else 64
                i = cnt["xb"] % NXB
                cnt["xb"] += 1
                srcd = x1p if kind == "p" else x1s
                dstd = yp if kind == "p" else ys
                op(SP, lambda e, i=i, P=P, r0=r0, srcd=srcd: e.dma_start(out=xb[i][0:P, :], in_=srcd[r0:r0 + P, :]),
                   reads=[bx], writes=[b_xb[i]], dsem=s_xb[i])
                op(SP, lambda e, i=i, P=P, r0=r0, dstd=dstd: e.dma_start(out=dstd[r0:r0 + P, :], in_=xb[i][0:P, :]),
                   reads=[b_xb[i]], dsem=s_xo[i])

        kb.emit(block, last=not do_b)
        kb.alias_pre = kb.barrier_tokens()
        psA.close()
        stA.close()
        if do_b:
            phase_b()
    return nc


_NC_CACHE = {}


def _prot_matrix():
    p = np.zeros((128, 128), np.float32)
    for m in range(128):
        if m % 64 < 32:
            p[m + 32, m] = -1.0
        else:
            p[m - 32, m] = 1.0
    return p


def _rope_tables():
    half = 32
    inv = (np.float32(10000.0) ** (-np.arange(half, dtype=np.float32) / np.float32(half))).astype(np.float32)
    pos = np.concatenate([np.arange(SEQ), PAST + np.arange(SL), PAST + np.arange(SL)]).astype(np.float32)
    ang = pos[:, None] * inv[None, :]
    cos = np.cos(ang).astype(np.float32)
    sin = np.sin(ang).astype(np.float32)
    idx = np.arange(128) % 32
    return np.ascontiguousarray(cos[:, idx].T), np.ascontiguousarray(sin[:, idx].T)


def kernel(x_prompt, x_sample, state_conv, state_rnn, cache_k, cache_v,
           norm_pre_a, w_in_a, conv_w_a, conv_b_a, w_gate_r, b_gate_r, w_gate_i, b_gate_i,
           lru_lambda, w_out_a, norm_post_a, norm_kv, w_kv,
           norm_pre_b, w_in_b, attn_sinks, w_out_b, norm_post_b, _do_b=True):
    f = lambda a: np.ascontiguousarray(np.asarray(a, dtype=np.float32))
    key = bool(_do_b)
    if key not in _NC_CACHE:
        _NC_CACHE[key] = build_program(do_b=key)
    nc = _NC_CACHE[key]
    cos_t, sin_t = _rope_tables()
    shared = {
        "norm_pre_a": f(norm_pre_a).reshape(D), "w_in_a": f(w_in_a).reshape(D, 2 * DR),
        "conv_w": f(conv_w_a).reshape(4, DR), "conv_b": f(conv_b_a).reshape(DR),
        "w_gate_r": f(w_gate_r).reshape(16, 88, 88), "b_gate_r": f(b_gate_r).reshape(DR),
        "w_gate_i": f(w_gate_i).reshape(16, 88, 88), "b_gate_i": f(b_gate_i).reshape(DR),
        "lru_lambda": f(lru_lambda).reshape(DR), "w_out_a": f(w_out_a).reshape(DR, D),
        "norm_post_a": f(norm_post_a).reshape(1, D), "norm_kv": f(norm_kv).reshape(D),
        "w_kv": f(w_kv).reshape(D, 512), "norm_pre_b": f(norm_pre_b).reshape(D),
        "w_in_b": f(w_in_b).reshape(D, 2048), "sinks": f(attn_sinks).reshape(16),
        "w_out_b": f(w_out_b).reshape(D, D), "norm_post_b": f(norm_post_b).reshape(1, D),
        "ident": np.eye(128, dtype=np.float32), "prot": _prot_matrix(), "cos_t": cos_t, "sin_t": sin_t,
    }
    xpf, xsf = f(x_prompt), f(x_sample)
    scf, srf = f(state_conv)[0], f(state_rnn)[0]
    ckf, cvf = f(cache_k).reshape(16, 128, 256), f(cache_v).reshape(16, 128, 256)
    in_maps = []
    for c in range(NCORES):
        sl = slice(NSEQ * c, NSEQ * c + NSEQ)
        m = dict(shared)
        m["xp"] = xpf[sl].reshape(NSEQ * SEQ, D)
        m["xs"] = xsf[sl].reshape(NSEQ * SL, D)
        m["st_conv"] = np.ascontiguousarray(scf[sl])
        m["st_rnn"] = np.ascontiguousarray(srf[sl])
        m["cache_k"] = np.ascontiguousarray(ckf[sl])
        m["cache_v"] = np.ascontiguousarray(cvf[sl])
        in_maps.append(m)
    res = run_bass_kernel_spmd(nc, in_maps, core_ids=list(range(NCORES)))
    R = res.results
    cat = lambda k: np.concatenate([np.asarray(r[k], dtype=np.float32) for r in R], axis=0)
    y_prompt = cat("yp").reshape(16, SEQ, D)
    y_sample = cat("ys").reshape(16, SL, D)
    pc = cat("p_conv").reshape(1, 16, 3, DR)
    prn = cat("p_rnn").reshape(1, 16, DR)
    pk = cat("p_k").reshape(16, 128, 4, 64)
    pv = cat("p_v").reshape(16, 128, 4, 64)
    sc = cat("s_conv").reshape(1, 16, 3, DR)
    srn = cat("s_rnn").reshape(1, 16, DR)
    sk = cat("s_k").reshape(16, SL, 4, 64)
    sv = cat("s_v").reshape(16, SL, 4, 64)
    return (y_prompt, y_sample, pc, prn, pk, pv, sc, srn, sk, sv)
```

```python
import contextlib
import numpy as np
import concourse.bass as bass
import concourse.mybir as mybir
from concourse.bass_utils import run_bass_kernel_spmd

F32 = mybir.dt.float32
BF16 = mybir.dt.bfloat16
AF = mybir.ActivationFunctionType
ALU = mybir.AluOpType

NCORES = 8
D = 1024
DR = 1408
NCH = 11
KC = 8
SEQ = 2048
SL = 32
NSEQ = 2
PAST = 1024
EPS = 1e-6
import os
ROT = int(os.environ.get('K_ROT', '800'))
KN = lambda k, d: int(os.environ.get(k, d))
NPOS = SEQ + 2 * SL


class Sem:
    def __init__(self, h):
        self.h = h
        self.v = 0


class Eng:
    def __init__(self, name, sem):
        self.name = name
        self.sem = sem
        self.seen = {}


class Buf:
    __slots__ = ("w", "r", "pre")

    def __init__(self, pre=None):
        self.w = []
        self.r = []
        self.pre = dict(pre) if pre else None


class Op:
    __slots__ = ("eng", "fn", "deps", "pre", "dsem", "cost", "lat", "tok", "idx", "nd", "ready", "users", "start", "fin", "ts")


class DS:
    def __init__(self, kb, name, s):
        self.kb = kb
        self.name = name
        self.s = s


COST0 = {"pe": 0.25, "act": 0.25, "dve": 0.15, "pool": 0.25, "sp": 0.05}
COSTN = {"pe": 1.0 / 2400, "act": 1.0 / 1200, "dve": 1.0 / 900, "pool": 1.0 / 420, "sp": 0.0}


class KB:
    def __init__(self, nc, stack):
        self.nc = nc
        self.stack = stack
        self.nsem = 0
        self.PE = Eng("pe", self.new_sem("pe"))
        self.ACT = Eng("act", self.new_sem("act"))
        self.DVE = Eng("dve", self.new_sem("dve"))
        self.POOL = Eng("pool", self.new_sem("pool"))
        self.SP = Eng("sp", self.new_sem("sp"))
        self.engs = [self.PE, self.ACT, self.DVE, self.POOL, self.SP]
        self.dsems = []
        self.alias_pre = {}
        self.esems = [e.sem for e in self.engs]
        self.pending = []
        self.nops = 0

    def new_sem(self, name):
        self.nsem += 1
        return Sem(self.stack.enter_context(self.nc.semaphore(name)))

    def dsem(self, name):
        s = self.new_sem(name)
        self.dsems.append(s)
        return DS(self, name, s)

    def buf(self):
        return Buf(self.alias_pre)

    def mark(self):
        return len(self.pending)

    def since(self, m):
        return self.pending[m:]

    def op(self, eng, fn, reads=(), writes=(), dsem=None, n=512, c=None, after=(), ts=None):
        o = Op()
        o.ts = ts
        o.eng = eng
        o.fn = fn
        o.dsem = dsem
        o.tok = None
        if dsem is not None:
            o.cost = 0.06
            o.lat = 2.5 + n / 290.0
        else:
            o.cost = c if c is not None else COST0[eng.name] + n * COSTN[eng.name]
            o.lat = o.cost
        deps = {}
        pre = {}
        for b in reads:
            for w in b.w:
                deps[id(w)] = w
        for b in list(writes) + list(after):
            for w in b.w:
                deps[id(w)] = w
            for w in b.r:
                deps[id(w)] = w
            if b.pre:
                for s_, v in b.pre.items():
                    if pre.get(s_, 0) < v:
                        pre[s_] = v
                b.pre = None
        o.deps = list(deps.values())
        o.pre = pre
        self.pending.append(o)
        for b in reads:
            b.r.append(o)
        for b in writes:
            b.w = [o]
            b.r = []
        return o

    def barrier_tokens(self):
        d = {}
        for sm in self.esems:
            if sm.v:
                d[sm] = sm.v
        for s in self.dsems:
            if s.v:
                d[s] = s.v
        return d

    def _schedule(self, ops, W=128):
        for i, o in enumerate(ops):
            o.idx = i
            o.users = []
            o.nd = 0
            o.ready = 0.0
            o.start = None
        inseg = set(id(o) for o in ops)
        for o in ops:
            for d in o.deps:
                if id(d) in inseg:
                    d.users.append(o)
                    o.nd += 1
        tail = {}
        for o in reversed(ops):
            t_ = 0.0
            for u in o.users:
                if tail[id(u)] > t_:
                    t_ = tail[id(u)]
            tail[id(o)] = t_ + o.lat
        PB = float(os.environ.get('K_PB', '3'))
        queues = {e.name: [] for e in self.engs}
        for o in ops:
            queues[o.eng.name].append(o)
        heads = {k: 0 for k in queues}
        free = {k: 0.0 for k in queues}
        order = {k: [] for k in queues}
        glob = []
        cur_ts = {}
        left = len(ops)
        while left:
            best = None
            for k, q in queues.items():
                h = heads[k]
                while h < len(q) and q[h].start is not None:
                    h += 1
                heads[k] = h
                cnt = 0
                i = h
                while i < len(q) and cnt < W:
                    o = q[i]
                    i += 1
                    if o.start is not None:
                        continue
                    cnt += 1
                    if o.nd:
                        continue
                    st = o.ready if o.ready > free[k] else free[k]
                    pen = 0.0
                    if o.ts is not None and o.ts != cur_ts.get(k):
                        pen = 2.0
                    if PB > 0:
                        key = (int((st + pen) / PB), -tail[id(o)], o.idx)
                    else:
                        key = (st + pen, o.idx)
                    if best is None or key < best[0]:
                        best = (key, o, k, st)
                    if PB <= 0 and st + pen <= free[k]:
                        break
            _, o, k, st = best
            if o.ts is not None and o.ts != cur_ts.get(k):
                cur_ts[k] = o.ts
                st += 1.3
            o.start = st
            free[k] = st + o.cost
            o.fin = st + o.lat
            order[k].append(o)
            glob.append(o)
            left -= 1
            for u in o.users:
                u.nd -= 1
                if u.ready < o.fin:
                    u.ready = o.fin
        return order, glob

    def emit(self, block, last=True):
        ops = self.pending
        self.pending = []
        order, glob = self._schedule(ops)
        for o in glob:
            if o.dsem is not None:
                sm = o.dsem.s
                sm.v += 16
                o.tok = (sm, sm.v)
        for e in self.engs:
            for o in order[e.name]:
                if o.dsem is None:
                    if e.sem.v >= ROT:
                        e.sem = self.new_sem(e.name + "_%d" % self.nsem)
                        self.esems.append(e.sem)
                    e.sem.v += 1
                    o.tok = (e.sem, e.sem.v)
        prog = {}
        for e in self.engs:
            lst = []
            for o in order[e.name]:
                need = dict(o.pre)
                for d in o.deps:
                    s_, v = d.tok
                    if need.get(s_, 0) < v:
                        need[s_] = v
                waits = []
                for s_, v in need.items():
                    if e.seen.get(s_, 0) < v:
                        e.seen[s_] = v
                        waits.append((s_, v))
                lst.append((waits, o.fn, o.tok[0], 16 if o.dsem is not None else 1))
            prog[e.name] = lst
        final = self.barrier_tokens() if last else {}

        def run(e, lst):
            for waits, fn, sem, inc in lst:
                for s_, v in waits:
                    e.wait_ge(s_.h, v)
                ins = fn(e)
                ins.then_inc(sem.h, inc)

        @block.tensor
        def _(e):
            run(e, prog["pe"])

        @block.scalar
        def _(e):
            run(e, prog["act"])

        @block.vector
        def _(e):
            run(e, prog["dve"])

        @block.gpsimd
        def _(e):
            run(e, prog["pool"])

        @block.sync
        def _(e):
            run(e, prog["sp"])
            for s_, v in final.items():
                e.wait_ge(s_.h, v)


def build_program(do_b=True, tile_sel=None):
    nc = bass.Bass("TRN2", target_bir_lowering=False)

    def din(name, shape):
        return nc.dram_tensor(name, list(shape), F32, kind="ExternalInput").ap()

    def dout(name, shape):
        return nc.dram_tensor(name, list(shape), F32, kind="ExternalOutput").ap()

    xp = din("xp", [NSEQ * SEQ, D])
    xs = din("xs", [NSEQ * SL, D])
    st_conv = din("st_conv", [NSEQ, 3, DR])
    st_rnn = din("st_rnn", [NSEQ, DR])
    cache_k = din("cache_k", [NSEQ, 128, 256])
    cache_v = din("cache_v", [NSEQ, 128, 256])
    norm_pre_a = din("norm_pre_a", [D])
    w_in_a = din("w_in_a", [D, 2 * DR])
    conv_w = din("conv_w", [4, DR])
    conv_b = din("conv_b", [DR])
    w_gate_r = din("w_gate_r", [16, 88, 88])
    b_gate_r = din("b_gate_r", [DR])
    w_gate_i = din("w_gate_i", [16, 88, 88])
    b_gate_i = din("b_gate_i", [DR])
    lru_lambda = din("lru_lambda", [DR])
    w_out_a = din("w_out_a", [DR, D])
    norm_post_a = din("norm_post_a", [1, D])
    norm_kv = din("norm_kv", [D])
    w_kv = din("w_kv", [D, 512])
    norm_pre_b = din("norm_pre_b", [D])
    w_in_b = din("w_in_b", [D, 2048])
    sinks = din("sinks", [16])
    w_out_b = din("w_out_b", [D, D])
    norm_post_b = din("norm_post_b", [1, D])
    ident_d = din("ident", [128, 128])
    prot_d = din("prot", [128, 128])
    cos_d = din("cos_t", [128, NPOS])
    sin_d = din("sin_t", [128, NPOS])

    yp = dout("yp", [NSEQ * SEQ, D])
    ys = dout("ys", [NSEQ * SL, D])
    p_conv = dout("p_conv", [NSEQ, 3, DR])
    p_rnn = dout("p_rnn", [NSEQ, DR])
    p_k = dout("p_k", [NSEQ, 128, 256])
    p_v = dout("p_v", [NSEQ, 128, 256])
    s_conv = dout("s_conv", [NSEQ, 3, DR])
    s_rnn = dout("s_rnn", [NSEQ, DR])
    s_k = dout("s_k", [NSEQ, SL, 256])
    s_v = dout("s_v", [NSEQ, SL, 256])

    x1p = nc.dram_tensor("x1p", [NSEQ * SEQ, D], F32).ap()
    x1s = nc.dram_tensor("x1s", [NSEQ * SL, D], F32).ap()

    with contextlib.ExitStack() as stack, nc.Block() as block:
        stack.enter_context(nc.allow_non_contiguous_dma("small strided parameter / state loads"))
        stack.enter_context(nc.allow_low_precision("bf16 matmul operands, fp32 accumulation"))
        kb = KB(nc, stack)
        op = kb.op
        PE, ACT, DVE, POOL, SP = kb.PE, kb.ACT, kb.DVE, kb.POOL, kb.SP

        def sb(st, name, shape, dt=F32):
            return st.enter_context(nc.sbuf_tensor(name, list(shape), dt))

        def ps(st, name, shape, dt=F32):
            return st.enter_context(nc.psum_tensor(name, list(shape), dt))

        tiles = []
        tiles.append(("s", [0, 1], SL, 0, 0))
        for s in range(NSEQ):
            for t in range(SEQ // 512):
                tiles.append(("p", [s], 512, s * SEQ + t * 512, t))
        if tile_sel is not None:
            tiles = [tiles[i] for i in tile_sel]

        x1_bufs = {}

        ident_f = sb(stack, "ident_f", [128, 128])
        ident_b = sb(stack, "ident_b", [128, 128], BF16)
        b_ident_f, b_ident_b = kb.buf(), kb.buf()
        s_const = kb.dsem("const")
        op(SP, lambda e: e.dma_start(out=ident_f[:], in_=ident_d), writes=[b_ident_f], dsem=s_const)
        epsb = sb(stack, "epsb", [128, 1])
        b_eps = kb.buf()
        op(POOL, lambda e: e.memset(epsb[:], EPS), writes=[b_eps])
        q25 = sb(stack, "q25", [128, 1])
        b_q25 = kb.buf()
        op(POOL, lambda e: e.memset(q25[:], 0.25), writes=[b_q25])

        def load_fm(st, name, src1d, nchunk, sem, eng=SP):
            t = sb(st, name, [128, nchunk])
            b = kb.buf()
            op(eng, lambda e: e.dma_start(out=t[:], in_=src1d.rearrange("(c p) -> p c", p=128)),
               writes=[b], dsem=sem)
            return t, b

        stA = contextlib.ExitStack()
        wbig = sb(stack, "wbig", [128, KC * 2 * DR], BF16)
        wbigv = wbig[:, :]
        wia = wbigv.rearrange("p (k n) -> p k n", k=KC)
        stgE = [wbigv[:, 0:4096].bitcast(F32), wbigv[:, 4096:8192].bitcast(F32)]
        wq = wbigv[:, 8192:16384].rearrange("p (k n) -> p k n", k=KC)
        wk = wbigv[:, 16384:20480].rearrange("p (k n) -> p k n", k=KC)
        wv = wbigv[:, 20480:22528].rearrange("p (k n) -> p k n", k=KC)
        b_wq = [kb.buf() for _ in range(KC)]
        b_wk = [kb.buf() for _ in range(KC)]
        b_stgE = [kb.buf() for _ in range(2)]
        g_kv, b_g_kv = load_fm(stack, "g_kv", norm_kv, KC, s_const)
        g_b, b_g_b = load_fm(stack, "g_b", norm_pre_b, KC, s_const)
        woa = sb(stA, "woa", [128, NCH, D], BF16)
        wgr = sb(stA, "wgr", [128, NCH, 3, 128], BF16)
        wgi = sb(stA, "wgi", [128, NCH, 3, 128], BF16)
        b_wia = [kb.buf() for _ in range(KC)]
        b_woa = [kb.buf() for _ in range(NCH)]
        b_wgr, b_wgi = kb.buf(), kb.buf()
        g_a, b_g_a = load_fm(stA, "g_a", norm_pre_a, KC, s_const)
        cb_t, b_cb = load_fm(stA, "cb_t", conv_b, NCH, s_const)
        br_t, b_br = load_fm(stA, "br_t", b_gate_r, NCH, s_const)
        bi_t, b_bi = load_fm(stA, "bi_t", b_gate_i, NCH, s_const)
        lam_t, b_lam = load_fm(stA, "lam_t", lru_lambda, NCH, s_const)
        cw_t = sb(stA, "cw_t", [128, 4, NCH])
        b_cw = kb.buf()
        op(SP, lambda e: e.dma_start(out=cw_t[:], in_=conv_w.rearrange("t (c p) -> p t c", p=128)),
           writes=[b_cw], dsem=s_const)
        gpost_a = sb(stA, "gpost_a", [128, D])
        b_gpa = kb.buf()
        op(SP, lambda e: e.dma_start(out=gpost_a[:], in_=norm_post_a.partition_broadcast(128)),
           writes=[b_gpa], dsem=s_const)

        for b_ in (b_ident_f, b_g_a, b_cb, b_br, b_bi, b_lam, b_cw, b_gpa, b_g_kv, b_g_b):
            b_.w = [o_ for o_ in kb.since(0) if o_.dsem is s_const]
        op(DVE, lambda e: e.tensor_copy(out=ident_b[:], in_=ident_f[:]), reads=[b_ident_f], writes=[b_ident_b])

        op(POOL, lambda e: e.memset(wgr[:], 0.0), writes=[b_wgr])
        op(POOL, lambda e: e.memset(wgi[:], 0.0), writes=[b_wgi])
        s_gate = kb.dsem("gate")
        for (wsrc, wdst, bw) in ((w_gate_r, wgr, b_wgr), (w_gate_i, wgi, b_wgi)):
            for blk in range(16):
                lo = 88 * blk
                hi = lo + 88
                for i in range(lo // 128, (hi - 1) // 128 + 1):
                    r0, r1 = max(lo, 128 * i), min(hi, 128 * i + 128)
                    for j in range(lo // 128, (hi - 1) // 128 + 1):
                        c0, c1 = max(lo, 128 * j), min(hi, 128 * j + 128)
                        op(POOL,
                           (lambda e, wsrc=wsrc, wdst=wdst, blk=blk, r0=r0, r1=r1, c0=c0, c1=c1, i=i, j=j, lo=lo:
                            e.dma_start(out=wdst[r0 - 128 * i:r1 - 128 * i, i, j - i + 1, c0 - 128 * j:c1 - 128 * j],
                                        in_=wsrc[blk, r0 - lo:r1 - lo, c0 - lo:c1 - lo])),
                           reads=[bw], dsem=s_gate, n=64)

        b_wgr.w = [o_ for o_ in kb.since(0) if o_.dsem is s_gate] + b_wgr.w
        b_wgi.w = list(b_wgr.w) + b_wgi.w

        hbr = sb(stA, "hbr", [128, NCH])
        hbi = sb(stA, "hbi", [128, NCH])
        cc = sb(stA, "cc", [128, NCH])
        hc = sb(stA, "hc", [128, NCH])
        xl = sb(stA, "xl", [128, NCH])
        pl = sb(stA, "pl", [128, NCH])
        b_hbr, b_hbi, b_cc, b_hc, b_xl, b_pl = (kb.buf() for _ in range(6))
        op(DVE, lambda e: e.tensor_scalar(out=hbr[:], in0=br_t[:], scalar1=0.5, scalar2=None, op0=ALU.mult),
           reads=[b_br], writes=[b_hbr])
        op(DVE, lambda e: e.tensor_scalar(out=hbi[:], in0=bi_t[:], scalar1=0.5, scalar2=None, op0=ALU.mult),
           reads=[b_bi], writes=[b_hbi])
        op(ACT, lambda e: e.activation(out=xl[:], in_=lam_t[:], func=AF.Exp, scale=-1.0), reads=[b_lam], writes=[b_xl], ts="e")
        coef = [1.0, -1.0 / 2, 1.0 / 3, -1.0 / 4, 1.0 / 5, -1.0 / 6, 1.0 / 7, -1.0 / 8]
        op(DVE, lambda e: e.tensor_scalar(out=pl[:], in0=xl[:], scalar1=coef[7], scalar2=coef[6], op0=ALU.mult, op1=ALU.add),
           reads=[b_xl], writes=[b_pl])
        for k in range(5, -1, -1):
            op(DVE, lambda e: e.tensor_tensor(out=pl[:], in0=pl[:], in1=xl[:], op=ALU.mult), reads=[b_pl, b_xl], writes=[b_pl])
            op(DVE, lambda e, k=k: e.tensor_scalar(out=pl[:], in0=pl[:], scalar1=coef[k], scalar2=None, op0=ALU.add),
               reads=[b_pl], writes=[b_pl])
        op(DVE, lambda e: e.tensor_tensor(out=pl[:], in0=pl[:], in1=xl[:], op=ALU.mult), reads=[b_pl, b_xl], writes=[b_pl])
        op(DVE, lambda e: e.tensor_scalar(out=cc[:], in0=pl[:], scalar1=-8.0, scalar2=None, op0=ALU.mult),
           reads=[b_pl], writes=[b_cc])
        op(DVE, lambda e: e.tensor_scalar(out=hc[:], in0=pl[:], scalar1=-4.0, scalar2=None, op0=ALU.mult),
           reads=[b_pl], writes=[b_hc])

        stS = contextlib.ExitStack()
        NSTG = 5
        stg = [sb(stS, "stg%d" % i, [128, 2 * DR]) for i in range(NSTG)]
        b_stg = [kb.buf() for _ in range(NSTG)]
        s_stg = [kb.dsem("stg%d" % i) for i in range(NSTG)]
        ci = 0
        for kc in range(KC):
            k2 = ci % NSTG
            dq = SP if ci % 2 == 0 else ACT
            op(dq, lambda e, kc=kc, k2=k2: e.dma_start(out=stg[k2][:], in_=w_in_a[kc * 128:(kc + 1) * 128, :]),
               writes=[b_stg[k2]], dsem=s_stg[k2])
            half = DR
            op(DVE, lambda e, kc=kc, k2=k2: e.tensor_scalar(out=wia[:, kc, 0:half], in0=stg[k2][:, 0:half],
                                                             scalar1=g_a[:, kc:kc + 1], scalar2=None, op0=ALU.mult),
               reads=[b_stg[k2], b_g_a], writes=[b_wia[kc]])
            op(ACT, lambda e, kc=kc, k2=k2: e.activation(out=wia[:, kc, half:2 * half], in_=stg[k2][:, half:2 * half],
                                                          func=AF.Identity, scale=g_a[:, kc:kc + 1]),
               reads=[b_stg[k2], b_g_a], writes=[b_wia[kc]])
            ci += 1
        for j in range(NCH):
            k2 = ci % NSTG
            dq = SP if ci % 2 == 0 else ACT
            op(dq, lambda e, j=j, k2=k2: e.dma_start(out=stg[k2][:, 0:D], in_=w_out_a[j * 128:(j + 1) * 128, :]),
               writes=[b_stg[k2]], dsem=s_stg[k2])
            if j % 2 == 0:
                op(DVE, lambda e, j=j, k2=k2: e.tensor_scalar(out=woa[:, j, :], in0=stg[k2][:, 0:D], scalar1=0.5, scalar2=None, op0=ALU.mult),
                   reads=[b_stg[k2]], writes=[b_woa[j]])
            else:
                op(ACT, lambda e, j=j, k2=k2: e.activation(out=woa[:, j, :], in_=stg[k2][:, 0:D], func=AF.Identity, scale=0.5),
                   reads=[b_stg[k2]], writes=[b_woa[j]])
            ci += 1
        kb.emit(block, last=False)
        kb.alias_pre = kb.barrier_tokens()
        stS.close()

        NT = 512
        brs = sb(stA, "brs", [128, NCH, 3 + NT])
        b_brs = [kb.buf() for _ in range(NCH)]
        hst = sb(stA, "hst", [128, NSEQ, NCH])
        b_hst = [kb.buf() for _ in range(NCH)]
        hstp = sb(stA, "hstp", [128, NSEQ, NCH])
        b_hstp = [[kb.buf() for _ in range(NCH)] for _ in range(NSEQ)]
        hsv = sb(stA, "hsv", [128, NSEQ, NCH, 3])
        b_hsv = [[kb.buf() for _ in range(NCH)] for _ in range(NSEQ)]
        NCV = 4
        cv = [sb(stA, "cv%d" % i, [128, NT]) for i in range(NCV)]
        cvb = [sb(stA, "cvb%d" % i, [128, NT], BF16) for i in range(NCV)]
        b_cv = [kb.buf() for _ in range(NCV)]
        b_cvb = [kb.buf() for _ in range(NCV)]
        tg = [sb(stA, "tg%d" % i, [128, NT]) for i in range(2)]
        b_tg = [kb.buf() for _ in range(2)]
        NSG = 4
        sg = [sb(stA, "sg%d" % i, [128, NT]) for i in range(NSG)]
        b_sg = [kb.buf() for _ in range(NSG)]
        NTS = KN('K_NTS', 2)
        tA = [sb(stA, "tA%d" % i, [128, NT]) for i in range(NTS)]
        tB = [sb(stA, "tB%d" % i, [128, NT]) for i in range(NTS)]
        tC = [sb(stA, "tC%d" % i, [128, NT]) for i in range(NTS)]
        tH = [sb(stA, "tH%d" % i, [128, NT]) for i in range(NTS)]
        b_tA = [kb.buf() for _ in range(NTS)]
        b_tB = [kb.buf() for _ in range(NTS)]
        b_tC = [kb.buf() for _ in range(NTS)]
        b_tH = [kb.buf() for _ in range(NTS)]
        NHG = KN('K_NHG', 1)
        hg_l = [sb(stA, "hg%d" % i, [128, NCH, NT], BF16) for i in range(NHG)]
        b_hg_l = [[kb.buf() for _ in range(NCH)] for _ in range(NHG)]
        NXB = KN('K_NXB', 2)
        xb = [sb(stA, "xb%d" % i, [128, D]) for i in range(NXB)]
        b_xb = [kb.buf() for _ in range(NXB)]
        s_xb = [kb.dsem("xb%d" % i) for i in range(NXB)]
        s_xo = [kb.dsem("xo%d" % i) for i in range(3)]
        NXR = KN('K_NXR', 3)
        xr = [sb(stA, "xr%d" % i, [128, D]) for i in range(NXR)]
        b_xr = [kb.buf() for _ in range(NXR)]
        s_xr = [kb.dsem("xr%d" % i) for i in range(NXR)]
        xnb = [sb(stA, "xnb%d" % i, [128, D], BF16) for i in range(2)]
        b_xnb = [kb.buf() for _ in range(2)]
        NXT = KN('K_NXT', 1)
        xnT_l = [sb(stA, "xnT%d" % i, [128, KC, NT], BF16) for i in range(NXT)]
        b_xnT_l = [[kb.buf() for _ in range(4)] for _ in range(NXT)]
        junk = sb(stA, "junk", [128, D], BF16)
        b_junk = kb.buf()
        junk2 = sb(stA, "junk2", [128, 2, 512], BF16)
        b_junk2 = [kb.buf() for _ in range(2)]
        yn_l = [sb(stA, "yn%d" % i, [128, D]) for i in range(2)]
        b_yn_l = [kb.buf() for _ in range(2)]
        ss = sb(stA, "ss", [128, 8])
        b_ss = [kb.buf() for _ in range(4)]
        ss2 = sb(stA, "ss2", [128, 8])
        b_ss2 = [kb.buf() for _ in range(2)]
        s_xb_A, s_xo_A, s_stg_A, s_const_A, s_xr_A = s_xb, s_xo, s_stg, s_const, s_xr
        s_small_ld = kb.dsem("small_ld")
        s_small_st = kb.dsem("small_st")

        psA = contextlib.ExitStack()
        ub = [ps(psA, "ub%d" % i, [128, NT]) for i in range(2)]
        b_ub = [kb.buf() for _ in range(2)]
        ug_l = [ps(psA, "ug%d" % i, [128, NT]) for i in range(2)]
        b_ug_l = [kb.buf() for _ in range(2)]
        pr = ps(psA, "pr", [128, NT])
        pi_ = ps(psA, "pi", [128, NT])
        b_pr, b_pi = kb.buf(), kb.buf()
        tp = [pr[:, :].bitcast(BF16).rearrange("p (k t) -> p k t", k=8)] * 2
        b_tp = [b_pr] * 2
        py = [ps(psA, "py%d" % i, [128, NT]) for i in range(2)]
        b_py = [kb.buf() for _ in range(2)]

        cnt = {"xb": 0, "xnb": 0, "tp": 0, "ub": 0, "cv": 0, "tg": 0, "sg": 0, "t": 0, "ss": 0, "ss2": 0, "tile": 0, "xr": 0, "yn": 0}

        def rstd_from_ss(col_ap, rd, wr_buf, out_ap):
            op(ACT, lambda e: e.activation(out=out_ap, in_=col_ap, func=AF.Sqrt, bias=epsb[0:col_ap.shape[0], :], scale=1.0 / D),
               reads=list(rd) + [b_eps], writes=[wr_buf], ts="q")
            op(DVE, lambda e: e.reciprocal(out=out_ap, in_=out_ap), reads=[wr_buf], writes=[wr_buf])

        def load_norm_transpose(src, row0, nrows, si, xnT_t, b_xnT_s, col0):
            P = nrows
            i = cnt["xb"] % NXB
            cnt["xb"] += 1
            op(SP, lambda e: e.dma_start(out=xb[i][0:P, :], in_=src[row0:row0 + P, :]), writes=[b_xb[i]], dsem=s_xb[i])
            k = cnt["ss"] % 4
            cnt["ss"] += 1
            op(ACT, lambda e: e.activation(out=junk[0:P, :], in_=xb[i][0:P, :], func=AF.Square,
                                           accum_out=ss[0:P, 2 * k:2 * k + 1]),
               reads=[b_xb[i]], writes=[b_junk, b_ss[k]])
            rstd_from_ss(ss[0:P, 2 * k:2 * k + 1], [b_ss[k]], b_ss[k], ss[0:P, 2 * k + 1:2 * k + 2])
            n = cnt["xnb"] % 2
            cnt["xnb"] += 1
            op(DVE, lambda e: e.tensor_scalar(out=xnb[n][0:P, :], in0=xb[i][0:P, :], scalar1=ss[0:P, 2 * k + 1:2 * k + 2],
                                              scalar2=None, op0=ALU.mult),
               reads=[b_xb[i], b_ss[k]], writes=[b_xnb[n]])
            q = cnt["tp"] % 2
            cnt["tp"] += 1

            def tr(e, q=q, n=n):
                ins = None
                for kc in range(KC):
                    ins = e.transpose(out=tp[q][:, kc, 0:P], in_=xnb[n][0:P, kc * 128:(kc + 1) * 128],
                                      identity=ident_b[0:P, 0:P])
                return ins
            op(PE, tr, reads=[b_xnb[n], b_ident_b], writes=[b_tp[q]], c=0.7)
            op(ACT, lambda e, q=q: e.activation(out=xnT_t[:, :, col0:col0 + P], in_=tp[q][:, :, 0:P], func=AF.Copy),
               reads=[b_tp[q]], writes=[b_xnT_s])

        def phaseA_tile(kind, seqs, L, row0, tidx):
            ns = len(seqs)
            N = ns * L
            xi_ = cnt["tile"] % NXT
            cnt["tile"] += 1
            xnT = xnT_l[xi_]
            b_xnT = b_xnT_l[xi_]
            hg = hg_l[(cnt["tile"] - 1) % NHG]
            b_hg = b_hg_l[(cnt["tile"] - 1) % NHG]
            src = xp if kind == "p" else xs
            dst1 = x1p if kind == "p" else x1s
            nsub = (N + 127) // 128
            if kind == "p":
                sq0 = seqs[0]
                for j in range(NCH):
                    if tidx == 0:
                        op(POOL, lambda e, j=j: e.memset(brs[:, j, 0:3], 0.0), writes=[b_brs[j]], n=3)
                        op(POOL, lambda e, j=j, sq0=sq0: e.memset(hstp[:, sq0, j:j + 1], 0.0), writes=[b_hstp[sq0][j]], n=1)
                    else:
                        op(POOL, lambda e, j=j, sq0=sq0: e.tensor_copy(out=brs[:, j, 0:3], in_=hsv[:, sq0, j, :]),
                           reads=[b_hsv[sq0][j]], writes=[b_brs[j]], n=3)
            if kind == "s":
                m_ld = kb.mark()
                for j in range(NCH):
                    bv = brs[:, j, 0:ns * (3 + L)].rearrange("p (s l) -> p s l", s=ns)
                    for s_ in range(ns):
                        op(SP, lambda e, j=j, s_=s_, bv=bv: e.dma_start(
                            out=bv[:, s_, 0:3],
                            in_=st_conv[s_, :, j * 128:(j + 1) * 128].rearrange("r p -> p r")),
                           after=[b_brs[j]], dsem=s_small_ld, n=8)
                    op(SP, lambda e, j=j: e.dma_start(
                        out=hst[:, :, j], in_=st_rnn[:, j * 128:(j + 1) * 128].rearrange("s p -> p s")),
                       after=[b_hst[j]], dsem=s_small_ld, n=8)
                lds = [o_ for o_ in kb.since(m_ld) if o_.dsem is s_small_ld]
                for j in range(NCH):
                    b_brs[j].w = list(lds)
                    b_brs[j].r = []
                    b_hst[j].w = list(lds)
                    b_hst[j].r = []
            for s_ in range(nsub):
                P = min(128, N - 128 * s_)
                load_norm_transpose(src, row0 + 128 * s_, P, s_, xnT, b_xnT[s_], 128 * s_)

            def brv(j, lo, hi):
                return brs[:, j, 0:ns * (3 + L)].rearrange("p (s l) -> p s l", s=ns)[:, :, lo:hi]

            def v3(ap):
                return ap[:, 0:N].rearrange("p (s l) -> p s l", s=ns)

            slot_cv = {}
            slot_sg = {}

            def gates_chain(j):
                iis = [i for i in (j - 1, j, j + 1) if 0 <= i < NCH and blocks_overlap(i, j)]

                def mmr(e, w=wgr, pt=pr):
                    ins = None
                    for n_, i in enumerate(iis):
                        ins = e.matmul(out=pt[:, 0:N], lhsT=w[:, i, j - i + 1, :], rhs=cvb[slot_cv[i]][:, 0:N],
                                       start=(n_ == 0), stop=(n_ == len(iis) - 1))
                    return ins
                op(PE, mmr, reads=[b_wgr] + [b_cvb[slot_cv[i]] for i in iis], writes=[b_pr], c=0.75)
                op(PE, lambda e: mmr(e, wgi, pi_), reads=[b_wgi] + [b_cvb[slot_cv[i]] for i in iis], writes=[b_pi], c=0.75)
                k = cnt["t"] % NTS
                cnt["t"] += 1
                op(ACT, lambda e: e.activation(out=tA[k][:, 0:N], in_=pr[:, 0:N], func=AF.Tanh,
                                               bias=hbr[:, j:j + 1], scale=0.5),
                   reads=[b_pr, b_hbr], writes=[b_tA[k]], ts="e")
                op(ACT, lambda e: e.activation(out=tC[k][:, 0:N], in_=pi_[:, 0:N], func=AF.Tanh,
                                               bias=hbi[:, j:j + 1], scale=0.5),
                   reads=[b_pi, b_hbi], writes=[b_tC[k]], ts="e")
                op(ACT, lambda e: e.activation(out=tB[k][:, 0:N], in_=tA[k][:, 0:N], func=AF.Exp,
                                               bias=hc[:, j:j + 1], scale=hc[:, j:j + 1]),
                   reads=[b_tA[k], b_hc], writes=[b_tB[k]], ts="e")
                if KN('K_A2', 0) == 0:
                    op(POOL, lambda e: e.tensor_tensor(out=tA[k][:, 0:N], in0=tB[k][:, 0:N], in1=tB[k][:, 0:N], op=ALU.mult),
                       reads=[b_tB[k]], writes=[b_tA[k]])
                else:
                    op(ACT, lambda e: e.activation(out=tA[k][:, 0:N], in_=tB[k][:, 0:N], func=AF.Square),
                       reads=[b_tB[k]], writes=[b_tA[k]])
                op(ACT, lambda e: e.activation(out=tA[k][:, 0:N], in_=tA[k][:, 0:N], func=AF.Sqrt, bias=q25[:, 0:1], scale=-0.25),
                   reads=[b_tA[k], b_q25], writes=[b_tA[k]], ts="q")
                c_ = slot_cv[j]
                op(DVE, lambda e: e.scalar_tensor_tensor(out=tC[k][:, 0:N], in0=tC[k][:, 0:N], scalar=1.0,
                                                          in1=cv[c_][:, 0:N], op0=ALU.add, op1=ALU.mult),
                   reads=[b_tC[k], b_cv[c_]], writes=[b_tC[k]])
                op(POOL if KN('K_BE', 0) == 0 else DVE, lambda e: e.tensor_tensor(out=tC[k][:, 0:N], in0=tA[k][:, 0:N], in1=tC[k][:, 0:N], op=ALU.mult),
                   reads=[b_tA[k], b_tC[k]], writes=[b_tC[k]])
                for s_ in range(ns):
                    if kind == "s":
                        hcol = hst[:, seqs[s_], j:j + 1]
                        b_hcol = b_hst[j]
                    else:
                        hcol = hstp[:, seqs[0], j:j + 1]
                        b_hcol = b_hstp[seqs[0]][j]
                    op(DVE, lambda e, s_=s_, hcol=hcol: e.tensor_tensor_scan(
                        out=tH[k][:, s_ * L:(s_ + 1) * L], data0=tB[k][:, s_ * L:(s_ + 1) * L],
                        data1=tC[k][:, s_ * L:(s_ + 1) * L], initial=hcol, op0=ALU.mult, op1=ALU.add),
                       reads=[b_tB[k], b_tC[k], b_hcol], writes=[b_tH[k]], c=1.6 * L / 512 + 0.2)
                    op(POOL, lambda e, s_=s_, hcol=hcol: e.tensor_copy(out=hcol, in_=tH[k][:, (s_ + 1) * L - 1:(s_ + 1) * L]),
                       reads=[b_tH[k]], writes=[b_hcol], n=1)
                g_ = slot_sg[j]
                op(POOL if KN('K_HG', 0) == 0 else DVE, lambda e: e.tensor_tensor(out=hg[:, j, 0:N], in0=tH[k][:, 0:N], in1=sg[g_][:, 0:N], op=ALU.mult),
                   reads=[b_tH[k], b_sg[g_]], writes=[b_hg[j]])

            for j in range(NCH):
                u = cnt["ub"] % 2
                cnt["ub"] += 1
                ug, b_ug = ug_l[u], b_ug_l[u]

                def mm_in(e, col, pt):
                    ins = None
                    for kc in range(KC):
                        ins = e.matmul(out=pt[:, 0:N], lhsT=wia[:, kc, col:col + 128], rhs=xnT[:, kc, 0:N],
                                       start=(kc == 0), stop=(kc == KC - 1))
                    return ins
                op(PE, lambda e, j=j, u=u: mm_in(e, j * 128, ub[u]), reads=b_wia + b_xnT[0:nsub], writes=[b_ub[u]], c=2.0)
                op(PE, lambda e, j=j, ug=ug: mm_in(e, DR + j * 128, ug), reads=b_wia + b_xnT[0:nsub], writes=[b_ug], c=2.0)
                op(ACT, lambda e, j=j, u=u: e.activation(out=brv(j, 3, 3 + L), in_=v3(ub[u]), func=AF.Copy),
                   reads=[b_ub[u]], writes=[b_brs[j]])
                c_ = cnt["cv"] % NCV
                cnt["cv"] += 1
                slot_cv[j] = c_
                op(ACT, lambda e, j=j, c_=c_, u=u: e.activation(out=v3(cv[c_]), in_=v3(ub[u]), func=AF.Identity,
                                                                 scale=cw_t[:, 3, j:j + 1], bias=cb_t[:, j:j + 1]),
                   reads=[b_ub[u], b_cw, b_cb], writes=[b_cv[c_]])
                for tap in (2, 1, 0):
                    ce = DVE
                    op(ce, lambda e, j=j, c_=c_, tap=tap: e.scalar_tensor_tensor(
                        out=v3(cv[c_]), in0=brv(j, tap, tap + L), scalar=cw_t[:, tap, j:j + 1], in1=v3(cv[c_]),
                        op0=ALU.mult, op1=ALU.add),
                       reads=[b_brs[j], b_cw, b_cv[c_]], writes=[b_cv[c_]], c=0.9)
                op(ACT, lambda e, c_=c_: e.activation(out=cvb[c_][:, 0:N], in_=cv[c_][:, 0:N], func=AF.Copy),
                   reads=[b_cv[c_]], writes=[b_cvb[c_]])
                if kind == "p":
                    op(POOL, lambda e, j=j, sq0=seqs[0]: e.tensor_copy(out=hsv[:, sq0, j, :], in_=brs[:, j, L:L + 3]),
                       reads=[b_brs[j]], writes=[b_hsv[seqs[0]][j]], n=3)
                t_ = cnt["tg"] % 2
                cnt["tg"] += 1
                op(ACT, lambda e, t_=t_, ug=ug: e.activation(out=tg[t_][:, 0:N], in_=ug[:, 0:N], func=AF.Tanh, scale=0.5),
                   reads=[b_ug], writes=[b_tg[t_]], ts="e")
                g_ = cnt["sg"] % NSG
                cnt["sg"] += 1
                slot_sg[j] = g_
                op(DVE, lambda e, t_=t_, g_=g_, ug=ug: e.scalar_tensor_tensor(out=sg[g_][:, 0:N], in0=tg[t_][:, 0:N], scalar=1.0,
                                                                       in1=ug[:, 0:N], op0=ALU.add, op1=ALU.mult),
                   reads=[b_tg[t_], b_ug], writes=[b_sg[g_]], c=1.0)
                if j >= 1:
                    gates_chain(j - 1)
            gates_chain(NCH - 1)

            last = (kind == "s") or (tidx == SEQ // 512 - 1)
            if last:
                m_st = kb.mark()
                for s_ in range(ns):
                    oc = (p_conv if kind == "p" else s_conv)[seqs[s_]]
                    orr = (p_rnn if kind == "p" else s_rnn)[seqs[s_]]
                    for j in range(NCH):
                        if kind == "p":
                            srcv = hsv[:, seqs[0], j, :]
                            rb = b_hsv[seqs[0]][j]
                        else:
                            srcv = brv(j, L, L + 3)[:, s_, :]
                            rb = b_brs[j]
                        op(SP, lambda e, j=j, oc=oc, srcv=srcv: e.dma_start(
                            out=oc[:, j * 128:(j + 1) * 128].rearrange("r p -> p r"), in_=srcv),
                           reads=[rb], dsem=s_small_st, n=8)
                    if kind == "s":
                        op(SP, lambda e, orr=orr, hs=seqs[s_]: e.dma_start(out=orr.rearrange("(c p) -> p c", p=128), in_=hst[:, hs, :]),
                           reads=b_hst, dsem=s_small_st, n=16)
                    else:
                        op(SP, lambda e, orr=orr, hs=seqs[0]: e.dma_start(out=orr.rearrange("(c p) -> p c", p=128), in_=hstp[:, hs, :]),
                           reads=b_hstp[seqs[0]], dsem=s_small_st, n=16)
                sts = [o_ for o_ in kb.since(m_st) if o_.dsem is s_small_st]
                for j in range(NCH):
                    b_brs[j].r = b_brs[j].r + sts
                    b_hst[j].r = b_hst[j].r + sts

            for s_ in range(nsub):
                P = min(128, N - 128 * s_)
                for h in range(2):
                    def mm_out(e, h=h, s_=s_, P=P):
                        ins = None
                        for j in range(NCH):
                            ins = e.matmul(out=py[h][0:P, :], lhsT=hg[:, j, 128 * s_:128 * s_ + P],
                                           rhs=woa[:, j, h * 512:(h + 1) * 512], start=(j == 0), stop=(j == NCH - 1))
                        return ins
                    op(PE, mm_out, reads=b_hg + b_woa, writes=[b_py[h]], c=2.8)
                k = cnt["ss2"] % 2
                cnt["ss2"] += 1
                yi_ = cnt["yn"] % 2
                cnt["yn"] += 1
                yn, b_yn = yn_l[yi_], b_yn_l[yi_]
                for h in range(2):
                    op(ACT, lambda e, h=h, k=k, P=P: e.activation(out=junk2[0:P, h, :], in_=py[h][0:P, :], func=AF.Square,
                                                                  accum_out=ss2[0:P, 4 * k + h:4 * k + h + 1]),
                       reads=[b_py[h]], writes=[b_junk2[h], b_ss2[k]])
                    op(DVE, lambda e, h=h, P=P, yn=yn: e.tensor_copy(out=yn[0:P, h * 512:(h + 1) * 512], in_=py[h][0:P, :]),
                       reads=[b_py[h], b_junk2[h]], writes=[b_yn], c=0.7)
                op(DVE, lambda e, k=k, P=P: e.tensor_tensor(out=ss2[0:P, 4 * k + 2:4 * k + 3], in0=ss2[0:P, 4 * k:4 * k + 1],
                                                            in1=ss2[0:P, 4 * k + 1:4 * k + 2], op=ALU.add),
                   reads=[b_ss2[k]], writes=[b_ss2[k]])
                rstd_from_ss(ss2[0:P, 4 * k + 2:4 * k + 3], [b_ss2[k]], b_ss2[k], ss2[0:P, 4 * k + 3:4 * k + 4])
                for h in range(2):
                    op(DVE, lambda e, h=h, k=k, P=P, yn=yn: e.scalar_tensor_tensor(
                        out=yn[0:P, h * 512:(h + 1) * 512], in0=yn[0:P, h * 512:(h + 1) * 512], scalar=ss2[0:P, 4 * k + 3:4 * k + 4],
                        in1=gpost_a[0:P, h * 512:(h + 1) * 512], op0=ALU.mult, op1=ALU.mult),
                       reads=[b_yn, b_ss2[k], b_gpa], writes=[b_yn], c=0.8)
                i = cnt["xr"] % NXR
                cnt["xr"] += 1
                r0 = row0 + 128 * s_
                op(SP, lambda e, i=i, P=P, r0=r0: e.dma_start(out=xr[i][0:P, :], in_=src[r0:r0 + P, :]),
                   writes=[b_xr[i]], dsem=s_xr[i], n=1024)
                op(POOL, lambda e, i=i, P=P, yn=yn: e.tensor_tensor(out=xr[i][0:P, :], in0=xr[i][0:P, :], in1=yn[0:P, :], op=ALU.add),
                   reads=[b_xr[i], b_yn], writes=[b_xr[i]], n=1024)
                bx = kb.buf()
                x1_bufs[(kind, r0)] = bx
                op(SP, lambda e, i=i, P=P, r0=r0: e.dma_start(out=dst1[r0:r0 + P, :], in_=xr[i][0:P, :]),
                   reads=[b_xr[i]], writes=[bx], dsem=s_xo[i], n=1024)

        def blocks_overlap(i, j):
            for blk in range(16):
                lo, hi = 88 * blk, 88 * blk + 88
                if max(lo, 128 * i) < min(hi, 128 * i + 128) and max(lo, 128 * j) < min(hi, 128 * j + 128):
                    return True
            return False


        def phase_b():
            NRING = 10
            NSLOT = NRING
            stB = contextlib.ExitStack()
            psB = contextlib.ExitStack()
            wgb = sb(stB, "wgb", [128, KC, 1024], BF16)
            wob = sb(stB, "wob", [128, KC, 1024], BF16)
            b_wg = [kb.buf() for _ in range(KC)]
            b_wob = [kb.buf() for _ in range(KC)]
            s_cB = s_const_A
            m_cB = kb.mark()
            gpost_b = sb(stB, "gpost_b", [128, D])
            b_gpb = kb.buf()
            op(SP, lambda e: e.dma_start(out=gpost_b[:], in_=norm_post_b.partition_broadcast(128)), writes=[b_gpb], dsem=s_cB)
            cos_sb = sb(stB, "cos_sb", [128, NPOS])
            sin_sb = sb(stB, "sin_sb", [128, NPOS])
            b_cos, b_sin = kb.buf(), kb.buf()
            op(SP, lambda e: e.dma_start(out=cos_sb[:], in_=cos_d), writes=[b_cos], dsem=s_cB)
            op(SP, lambda e: e.dma_start(out=sin_sb[:], in_=sin_d), writes=[b_sin], dsem=s_cB)
            prot_f = sb(stB, "prot_f", [128, 128])
            b_protf = kb.buf()
            op(SP, lambda e: e.dma_start(out=prot_f[:], in_=prot_d), writes=[b_protf], dsem=s_cB)
            skb = sb(stB, "skb", [128, 8])
            b_skb = kb.buf()
            sk2 = sinks.rearrange("(hp two) -> two hp", two=2)
            op(SP, lambda e: e.dma_start(out=skb[0:64, :], in_=sk2[0:1, :].partition_broadcast(64)), writes=[b_skb], dsem=s_cB)
            op(SP, lambda e: e.dma_start(out=skb[64:128, :], in_=sk2[1:2, :].partition_broadcast(64)), dsem=s_cB)
            for b_ in (b_gpb, b_cos, b_sin, b_skb, b_protf):
                b_.w = [o_ for o_ in kb.since(m_cB) if o_.dsem is s_cB]
            prot_b = sb(stB, "prot_b", [128, 128], BF16)
            b_prot = kb.buf()
            op(DVE, lambda e: e.tensor_copy(out=prot_b[:], in_=prot_f[:]), reads=[b_protf], writes=[b_prot])
            esk = sb(stB, "esk", [128, 8])
            b_esk = kb.buf()
            op(ACT, lambda e: e.activation(out=esk[:], in_=skb[:], func=AF.Exp), reads=[b_skb], writes=[b_esk], ts="e")
            ones_bd = sb(stB, "ones_bd", [128, 128], BF16)
            ones_pt = sb(stB, "ones_pt", [128, 128], BF16)
            b_onesbd, b_onespt = kb.buf(), kb.buf()
            op(POOL, lambda e: e.memset(ones_bd[:], 0.0), writes=[b_onesbd])
            op(POOL, lambda e: e.memset(ones_bd[0:64, 0:64], 1.0), writes=[b_onesbd])
            op(POOL, lambda e: e.memset(ones_bd[64:128, 64:128], 1.0), writes=[b_onesbd])
            op(POOL, lambda e: e.memset(ones_pt[:], 0.0), writes=[b_onespt])
            op(POOL, lambda e: e.memset(ones_pt[0:32, 0:64], 1.0), writes=[b_onespt])
            op(POOL, lambda e: e.memset(ones_pt[64:96, 64:128], 1.0), writes=[b_onespt])

            NSB = 2
            stg2 = [sb(stB, "stgb%d" % i, [128, 1024]) for i in range(NSB)]
            b_stg2 = [kb.buf() for _ in range(NSB)]
            s_stg2 = s_stg_A
            ci = 0
            for kc in range(KC):
                k2 = ci % NSB
                ci += 1
                gs = g_b[:, kc:kc + 1]
                st_ = stg2[k2]
                op(SP if ci % 2 else ACT, lambda e, kc=kc, st_=st_: e.dma_start(out=st_[:, 0:1024], in_=w_in_b[kc * 128:(kc + 1) * 128, 1024:2048]),
                   writes=[b_stg2[k2]], dsem=s_stg2[k2], n=1024)
                op(ACT, lambda e, kc=kc, st_=st_, gs=gs: e.activation(out=wgb[:, kc, :], in_=st_[:, 0:1024], func=AF.Identity, scale=gs),
                   reads=[b_stg2[k2], b_g_b], writes=[b_wg[kc]], n=1024)
            for kc in range(KC):
                k2 = ci % NSB
                ci += 1
                st_ = stg2[k2]
                op(SP if ci % 2 else ACT, lambda e, kc=kc, st_=st_: e.dma_start(out=st_[:, 0:1024], in_=w_out_b[kc * 128:(kc + 1) * 128, :]),
                   writes=[b_stg2[k2]], dsem=s_stg2[k2])
                op(DVE,
                   lambda e, kc=kc, st_=st_: e.tensor_scalar(out=wob[:, kc, :], in0=st_[:, 0:1024], scalar1=0.5, scalar2=None, op0=ALU.mult),
                   reads=[b_stg2[k2]], writes=[b_wob[kc]])

            NT = 512
            PADL = 64
            kbd = sb(stB, "kbd", [128, NSLOT, 4, 128], BF16)
            vbd = sb(stB, "vbd", [128, NSLOT, 4, 128], BF16)
            b_kbd = [kb.buf() for _ in range(NSLOT)]
            b_vbd = [kb.buf() for _ in range(NSLOT)]
            op(POOL, lambda e: e.memset(kbd[:], 0.0), writes=b_kbd)
            op(POOL, lambda e: e.memset(vbd[:], 0.0), writes=b_vbd)
            xnT = sb(stB, "xnTb", [128, KC, PADL + NT + 64], BF16)
            b_xnT = [kb.buf() for _ in range(4)]
            op(POOL, lambda e: e.memset(xnT[:], 0.0), writes=b_xnT)
            qT = sb(stB, "qT", [128, KC, NT], BF16)
            b_qT = [kb.buf() for _ in range(KC)]
            ogb = sb(stB, "ogb", [128, KC, NT], BF16)
            b_ogb = [kb.buf() for _ in range(KC)]
            NXB = 2
            xb = [sb(stB, "xbb%d" % i, [128, D]) for i in range(NXB)]
            b_xb = [kb.buf() for _ in range(NXB)]
            s_xb = s_xb_A
            s_xo = s_xo_A
            s_xr = s_xr_A
            NXR = 2
            xr = [sb(stB, "xrb%d" % i, [128, D]) for i in range(NXR)]
            b_xr = [kb.buf() for _ in range(NXR)]
            xnb = [sb(stB, "xnbb%d" % i, [128, D], BF16) for i in range(2)]
            b_xnb = [kb.buf() for _ in range(2)]
            junk = sb(stB, "junkb", [128, D], BF16)
            b_junk = kb.buf()
            junk2 = sb(stB, "junkb2", [128, 2, 512], BF16)
            b_junk2 = [kb.buf() for _ in range(2)]
            yn_l = [sb(stB, "ynb%d" % i, [128, D]) for i in range(2)]
            b_yn_l = [kb.buf() for _ in range(2)]
            ss = sb(stB, "ssb", [128, 8])
            b_ss = [kb.buf() for _ in range(4)]
            ss2 = sb(stB, "ss2b", [128, 8])
            b_ss2 = [kb.buf() for _ in range(2)]
            zb_l = [sb(stB, "zb%d" % i, [128, NT], BF16) for i in range(2)]
            b_zb_l = [kb.buf() for _ in range(2)]
            wtmp = wbigv[:, 0:8192].bitcast(F32).rearrange("p (k n) -> p k n", k=8)
            t1_l = [wtmp[:, i, :] for i in range(2)]
            t2_l = [wtmp[:, 2 + i, :] for i in range(2)]
            b_t1_l = [kb.buf() for _ in range(2)]
            b_t2_l = [kb.buf() for _ in range(2)]
            kf = [sb(stB, "kf%d" % i, [128, NT]) for i in range(2)]
            b_kf = [kb.buf() for _ in range(2)]
            NPT = 3
            pT = [sb(stB, "pT%d" % i, [128, 384], BF16) for i in range(NPT)]
            b_pT = [kb.buf() for _ in range(NPT)]
            tg_l = [sb(stB, "tgb%d" % i, [128, NT]) for i in range(2)]
            sg_l = [sb(stB, "sgb%d" % i, [128, NT]) for i in range(2)]
            b_tg_l = [kb.buf() for _ in range(2)]
            b_sg_l = [kb.buf() for _ in range(2)]
            rden = [wtmp[:, 4 + i, :] for i in range(2)]
            oraw = [wtmp[:, 6 + i, :] for i in range(2)]
            b_rden = [kb.buf() for _ in range(2)]
            b_oraw = [kb.buf() for _ in range(2)]
            vout = sb(stB, "vout", [128, 256])
            kout = sb(stB, "kout", [128, 256])
            b_vout, b_kout = kb.buf(), kb.buf()
            s_vo, s_ko = kb.dsem("vo"), kb.dsem("ko")
            ck = sb(stB, "ck", [128, 256])
            ckb = sb(stB, "ckb", [128, 2, 256], BF16)
            cvt = sb(stB, "cvt", [128, 2, 256])
            b_ck, b_ckb, b_cvt = kb.buf(), kb.buf(), kb.buf()
            s_ck, s_cv = kb.dsem("ckl"), kb.dsem("cvl")

            pz_l = [ps(psB, "pz%d" % i, [128, NT]) for i in range(2)]
            b_pz_l = [kb.buf() for _ in range(2)]
            pz, b_pz = pz_l[0], b_pz_l[0]
            pzr = ps(psB, "pzr", [128, NT])
            b_pzr = kb.buf()
            tpb = pzr[:, :].bitcast(BF16).rearrange("p (k t) -> p k t", k=8)
            b_tpb = b_pzr
            pS_l = [ps(psB, "pS0", [128, NT])] * 2
            b_pS_l = [kb.buf()] * 2
            pO = [ps(psB, "pO%d" % i, [128, NT]) for i in range(2)]
            pD = [ps(psB, "pD%d" % i, [128, NT]) for i in range(2)]
            b_pO = [kb.buf() for _ in range(2)]
            b_pD = [kb.buf() for _ in range(2)]
            cntb = {"xb": 0, "xnb": 0, "ss": 0, "ss2": 0, "pT": 0, "xr": 0, "pS": 0, "pz": 0, "rp": 0, "gt": 0, "yn": 0}

            def lnt(src, row0, P, b_dst, col0, extra):
                i = cntb["xb"] % NXB
                cntb["xb"] += 1
                op(SP, lambda e: e.dma_start(out=xb[i][0:P, :], in_=src[row0:row0 + P, :]), reads=extra, writes=[b_xb[i]], dsem=s_xb[i])
                k = cntb["ss"] % 4
                cntb["ss"] += 1
                op(ACT, lambda e: e.activation(out=junk[0:P, :], in_=xb[i][0:P, :], func=AF.Square, accum_out=ss[0:P, 2 * k:2 * k + 1]),
                   reads=[b_xb[i]], writes=[b_junk, b_ss[k]])
                rstd_from_ss(ss[0:P, 2 * k:2 * k + 1], [b_ss[k]], b_ss[k], ss[0:P, 2 * k + 1:2 * k + 2])
                n = cntb["xnb"] % 2
                cntb["xnb"] += 1
                op(DVE, lambda e: e.tensor_scalar(out=xnb[n][0:P, :], in0=xb[i][0:P, :], scalar1=ss[0:P, 2 * k + 1:2 * k + 2], scalar2=None, op0=ALU.mult),
                   reads=[b_xb[i], b_ss[k]], writes=[b_xnb[n]])

                def tr(e):
                    ins = None
                    for kc in range(KC):
                        ins = e.transpose(out=tpb[:, kc, 0:P], in_=xnb[n][0:P, kc * 128:(kc + 1) * 128], identity=ident_b[0:P, 0:P])
                    return ins
                op(PE, tr, reads=[b_xnb[n], b_ident_b], writes=[b_tpb], c=0.7)
                op(ACT, lambda e: e.activation(out=xnT[:, :, col0:col0 + P], in_=tpb[:, :, 0:P], func=AF.Copy), reads=[b_tpb], writes=[b_dst])

            def slot_runs(base, c0, n):
                runs = []
                i = 0
                while i < n:
                    s0 = (c0 + i) % NRING
                    cntr = min(n - i, NRING - s0)
                    runs.append((base + s0, cntr, i))
                    i += cntr
                return runs

            def phaseB_tile(kind, seqs, L, row0, tidx):
                ns = len(seqs)
                N = ns * L
                src1 = x1p if kind == "p" else x1s
                dsty = yp if kind == "p" else ys
                nsub = (N + 127) // 128
                pos0 = 512 * tidx if kind == "p" else SEQ
                cosv = cos_sb[:, pos0:pos0 + N]
                sinv = sin_sb[:, pos0:pos0 + N]
                last = (kind == "s") or (tidx == SEQ // 512 - 1)
                for s_ in range(nsub):
                    P = min(128, N - 128 * s_)
                    r0 = row0 + 128 * s_
                    lnt(src1, r0, P, b_xnT[s_], PADL + 128 * s_, [x1_bufs[(kind, r0)]])
                xall = b_xnT[0:nsub]

                def proj(e, w, col, pt):
                    ins = None
                    for kc in range(KC):
                        ins = e.matmul(out=pt[:, 0:N], lhsT=w[:, kc, col:col + 128], rhs=xnT[:, kc, PADL:PADL + N],
                                       start=(kc == 0), stop=(kc == KC - 1))
                    return ins

                def proj_rope(w, col, rd):
                    i_ = cntb["pz"] % 2
                    cntb["pz"] += 1
                    pzx, b_pzx = pz_l[i_], b_pz_l[i_]
                    r_ = cntb["rp"] % 2
                    cntb["rp"] += 1
                    zb, b_zb = zb_l[r_], b_zb_l[r_]
                    t1, t2, b_t1, b_t2 = t1_l[r_], t2_l[r_], b_t1_l[r_], b_t2_l[r_]
                    op(PE, lambda e: proj(e, w, col, pzx), reads=rd + xall, writes=[b_pzx], c=2.0)
                    op(ACT, lambda e: e.activation(out=zb[:, 0:N], in_=pzx[:, 0:N], func=AF.Copy), reads=[b_pzx], writes=[b_zb])
                    op(PE, lambda e: e.matmul(out=pzr[:, 0:N], lhsT=prot_b[:, :], rhs=zb[:, 0:N], start=True, stop=True),
                       reads=[b_zb, b_prot], writes=[b_pzr], c=0.3)
                    op(DVE, lambda e: e.tensor_tensor(out=t1[:, 0:N], in0=pzx[:, 0:N], in1=cosv, op=ALU.mult), reads=[b_pzx, b_cos, b_zb], writes=[b_t1])
                    op(DVE, lambda e: e.tensor_tensor(out=t2[:, 0:N], in0=pzr[:, 0:N], in1=sinv, op=ALU.mult), reads=[b_pzr, b_sin], writes=[b_t2])
                    return t1, t2, b_t1, b_t2

                if kind == "p":
                    c_first = 8 * tidx
                    nchk = 8
                    sbase = 0
                    key_slots = None
                else:
                    c_first = 0
                    nchk = 0

                if kind == "s":
                    for s_ in range(ns):
                        sl = 3 * s_ + 2
                        op(POOL, lambda e, sl=sl: e.memset(kbd[:, sl, :, :], 0.0), writes=[b_kbd[sl]])
                        op(POOL, lambda e, sl=sl: e.memset(vbd[:, sl, :, :], 0.0), writes=[b_vbd[sl]])
                for kc2 in range(4):
                    t1, t2, b_t1, b_t2 = proj_rope(wk, kc2 * 128, b_wk)
                    if kc2 < 2:
                        gt, gb = 2 * kc2, 2 * kc2 + 1
                        kfi = kf[kc2]
                        op(POOL, lambda e, kfi=kfi, t1=t1, t2=t2: e.tensor_tensor(out=kfi[:, 0:N], in0=t1[:, 0:N], in1=t2[:, 0:N], op=ALU.add),
                           reads=[b_t1, b_t2], writes=[b_kf[kc2]])
                        srcs = (kfi, None)
                    else:
                        gt, gb = 2 * (kc2 - 2) + 1, 2 * (kc2 - 2)
                        srcs = (t1, t2)
                    for (plo, gg, clo) in ((0, gt, 0), (64, gb, 64)):
                        if kind == "p":
                            for (s0, cn, off) in slot_runs(sbase, c_first, nchk):
                                outv = kbd[plo:plo + 64, s0:s0 + cn, gg, clo:clo + 64]
                                if srcs[1] is None:
                                    inv = srcs[0][plo:plo + 64, off * 64:(off + cn) * 64].rearrange("p (c k) -> p c k", k=64)
                                    op(POOL, lambda e, outv=outv, inv=inv: e.tensor_copy(out=outv, in_=inv),
                                       reads=[b_kf[kc2]], writes=[b_kbd[s0 + i_] for i_ in range(cn)])
                                else:
                                    in0 = t1[plo:plo + 64, off * 64:(off + cn) * 64].rearrange("p (c k) -> p c k", k=64)
                                    in1 = t2[plo:plo + 64, off * 64:(off + cn) * 64].rearrange("p (c k) -> p c k", k=64)
                                    op(POOL, lambda e, outv=outv, in0=in0, in1=in1: e.tensor_tensor(out=outv, in0=in0, in1=in1, op=ALU.add),
                                       reads=[b_t1, b_t2], writes=[b_kbd[s0 + i_] for i_ in range(cn)])
                        else:
                            for s_ in range(ns):
                                sl = 3 * s_ + 2
                                outv = kbd[plo:plo + 64, sl, gg, clo:clo + L]
                                if srcs[1] is None:
                                    op(POOL, lambda e, outv=outv, s_=s_, plo=plo, kfi=srcs[0]: e.tensor_copy(out=outv, in_=kfi[plo:plo + 64, s_ * L:(s_ + 1) * L]),
                                       reads=[b_kf[kc2]], writes=[b_kbd[sl]])
                                else:
                                    op(POOL, lambda e, outv=outv, s_=s_, plo=plo, t1=t1, t2=t2: e.tensor_tensor(out=outv, in0=t1[plo:plo + 64, s_ * L:(s_ + 1) * L],
                                                                                                  in1=t2[plo:plo + 64, s_ * L:(s_ + 1) * L], op=ALU.add),
                                       reads=[b_t1, b_t2], writes=[b_kbd[sl]])
                if last:
                    for s_ in range(ns):
                        Pk = 128 if kind == "p" else L
                        c0 = N - 128 if kind == "p" else s_ * L

                        def trk(e, c0=c0, Pk=Pk):
                            ins = None
                            for kc2 in range(2):
                                ins = e.transpose(out=pz[0:Pk, kc2 * 128:(kc2 + 1) * 128], in_=kf[kc2][:, c0:c0 + Pk], identity=ident_f[:, :])
                            return ins
                        op(PE, trk, reads=b_kf + [b_ident_f], writes=[b_pz])
                        op(ACT, lambda e, Pk=Pk: e.activation(out=kout[0:Pk, :], in_=pz[0:Pk, 0:256], func=AF.Copy), reads=[b_pz], writes=[b_kout])
                        dk = (p_k if kind == "p" else s_k)[seqs[s_]]
                        op(SP, lambda e, dk=dk, Pk=Pk: e.dma_start(out=dk, in_=kout[0:Pk, :]), reads=[b_kout], dsem=s_ko)

                def vproj(e, c0, M, pt):
                    ins = None
                    for kc in range(KC):
                        ins = e.matmul(out=pt[0:M, 0:256], lhsT=xnT[:, kc, c0:c0 + M], rhs=wv[:, kc, :], start=(kc == 0), stop=(kc == KC - 1))
                    return ins
                if kind == "p":
                    for s_ in range(4):
                        op(PE, lambda e, s_=s_: vproj(e, PADL + 128 * s_, 128, pz), reads=b_wk + xall, writes=[b_pz], c=1.2)
                        ce, co = c_first + 2 * s_, c_first + 2 * s_ + 1
                        se, so = sbase + ce % NRING, sbase + co % NRING
                        pzv = pz[:, 0:256].rearrange("p (g d) -> p g d", g=4)
                        op(ACT, lambda e, se=se, pzv=pzv: e.activation(out=vbd[0:64, se, :, 0:64], in_=pzv[0:64], func=AF.Copy), reads=[b_pz], writes=[b_vbd[se]])
                        op(ACT, lambda e, so=so, pzv=pzv: e.activation(out=vbd[64:128, so, :, 64:128], in_=pzv[64:128], func=AF.Copy), reads=[b_pz], writes=[b_vbd[so]])
                        if last and s_ == 3:
                            op(ACT, lambda e: e.activation(out=vout[:, :], in_=pz[:, 0:256], func=AF.Copy), reads=[b_pz], writes=[b_vout])
                            op(SP, lambda e: e.dma_start(out=p_v[seqs[0]], in_=vout[:, :]), reads=[b_vout], dsem=s_vo)
                    for s_ in range(5):
                        op(PE, lambda e, s_=s_: vproj(e, 128 * s_, 128, pzr), reads=b_wk + xall, writes=[b_pzr], c=1.2)
                        pzv = pzr[:, 0:256].rearrange("p (g d) -> p g d", g=4)
                        if s_ >= 1:
                            so = sbase + (c_first + 2 * s_ - 1) % NRING
                            op(ACT, lambda e, so=so, pzv=pzv: e.activation(out=vbd[0:64, so, :, 0:64], in_=pzv[0:64], func=AF.Copy), reads=[b_pzr], writes=[b_vbd[so]])
                        if s_ <= 3:
                            se = sbase + (c_first + 2 * s_) % NRING
                            op(ACT, lambda e, se=se, pzv=pzv: e.activation(out=vbd[64:128, se, :, 64:128], in_=pzv[64:128], func=AF.Copy), reads=[b_pzr], writes=[b_vbd[se]])
                else:
                    for s_ in range(ns):
                        sl = 3 * s_ + 2
                        op(PE, lambda e, s_=s_: vproj(e, PADL + L * s_, L, pz), reads=b_wk + xall, writes=[b_pz])
                        pzv = pz[:, 0:256].rearrange("p (g d) -> p g d", g=4)
                        op(ACT, lambda e, sl=sl, pzv=pzv: e.activation(out=vbd[0:L, sl, :, 0:64], in_=pzv[0:L], func=AF.Copy), reads=[b_pz], writes=[b_vbd[sl]])
                        op(ACT, lambda e: e.activation(out=vout[0:L, :], in_=pz[0:L, 0:256], func=AF.Copy), reads=[b_pz], writes=[b_vout])
                        op(SP, lambda e, s_=s_: e.dma_start(out=s_v[seqs[s_]], in_=vout[0:L, :]), reads=[b_vout], dsem=s_vo)
                        op(PE, lambda e, s_=s_: vproj(e, PADL + L * s_ - 64, 64 + L, pzr), reads=b_wk + xall, writes=[b_pzr])
                        pzv2 = pzr[:, 0:256].rearrange("p (g d) -> p g d", g=4)
                        op(ACT, lambda e, sl=sl, pzv2=pzv2: e.activation(out=vbd[64:64 + L, sl, :, 64:128], in_=pzv2[64:64 + L], func=AF.Copy), reads=[b_pzr], writes=[b_vbd[sl]])
                        sq = seqs[s_]
                        op(SP, lambda e, sq=sq: e.dma_start(out=ck[:, :], in_=cache_k[sq]), writes=[b_ck], dsem=s_ck)
                        ckv = ck[:, :].rearrange("p (a b d) -> p a b d", a=2, b=2)
                        op(DVE, lambda e: e.tensor_copy(out=ckb[:, 0, :], in_=ck[:, :]), reads=[b_ck], writes=[b_ckb])
                        cks = ckb[:, 1, :].rearrange("p (a b d) -> p a b d", a=2, b=2)
                        for b2 in range(2):
                            op(DVE, lambda e, b2=b2, cks=cks, ckv=ckv: e.tensor_copy(out=cks[:, :, b2, :], in_=ckv[:, :, 1 - b2, :]), reads=[b_ck], writes=[b_ckb])

                        def trc(e):
                            ins = None
                            for j4 in range(4):
                                ins = e.transpose(out=tpb[:, j4, :], in_=ckb[:, j4 // 2, (j4 % 2) * 128:(j4 % 2 + 1) * 128], identity=ident_b[:, :])
                            return ins
                        op(PE, trc, reads=[b_ckb, b_ident_b], writes=[b_tpb])
                        for j4 in range(4):
                            jj = j4 % 2
                            if j4 < 2:
                                gt, gb = 2 * jj, 2 * jj + 1
                            else:
                                gt, gb = 2 * jj + 1, 2 * jj
                            for (plo, gg, clo) in ((0, gt, 0), (64, gb, 64)):
                                outv = kbd[plo:plo + 64, 3 * s_:3 * s_ + 2, gg, clo:clo + 64]
                                inv = tpb[plo:plo + 64, j4, :].rearrange("p (m k) -> p m k", m=2)
                                op(ACT, lambda e, outv=outv, inv=inv: e.activation(out=outv, in_=inv, func=AF.Copy),
                                   reads=[b_tpb], writes=[b_kbd[3 * s_], b_kbd[3 * s_ + 1]])
                        m_cv = kb.mark()
                        for m in range(2):
                            for hh in range(2):
                                op(SP, lambda e, sq=sq, m=m, hh=hh: e.dma_start(out=cvt[64 * hh:64 * hh + 64, m, :], in_=cache_v[sq, 64 * m:64 * m + 64, :]),
                                   writes=[b_cvt], dsem=s_cv)
                        b_cvt.w = [o_ for o_ in kb.since(m_cv) if o_.dsem is s_cv]
                        cv4 = cvt[:, :, :].rearrange("p m (g d) -> p m g d", g=4)
                        for m in range(2):
                            op(DVE, lambda e, m=m, s_=s_, cv4=cv4: e.tensor_copy(out=vbd[0:64, 3 * s_ + m, :, 0:64], in_=cv4[0:64, m]), reads=[b_cvt], writes=[b_vbd[3 * s_ + m]])
                            op(DVE, lambda e, m=m, s_=s_, cv4=cv4: e.tensor_copy(out=vbd[64:128, 3 * s_ + m, :, 64:128], in_=cv4[64:128, m]), reads=[b_cvt], writes=[b_vbd[3 * s_ + m]])

                for qc in range(KC):
                    t1, t2, b_t1, b_t2 = proj_rope(wq, qc * 128, b_wq)
                    op(DVE if (KN('K_QADD', 0) == 1 or (KN('K_QADD', 0) == 2 and qc % 2 == 0)) else POOL, lambda e, qc=qc, t1=t1, t2=t2: e.tensor_tensor(out=qT[:, qc, 0:N], in0=t1[:, 0:N], in1=t2[:, 0:N], op=ALU.add),
                       reads=[b_t1, b_t2], writes=[b_qT[qc]])

                for g in range(4):
                    jobs = []
                    if kind == "p":
                        for c in range(max(c_first - 2, 0), c_first + 8):
                            qlo = max(c, c_first)
                            qhi = min(c + 2, c_first + 7)
                            jobs.append((sbase + c % NRING, (qlo - c_first) * 64, (qhi - qlo + 1) * 64, ones_bd, b_onesbd))
                    else:
                        for s_ in range(ns):
                            for m in range(3):
                                jobs.append((3 * s_ + m, s_ * L, L, ones_bd if m < 2 else ones_pt, b_onesbd if m < 2 else b_onespt))
                    for ji, (sl, q0, nq, onesm, b_onesm) in enumerate(jobs):
                        first = (ji == 0)

                        si_ = cntb["pS"] % 2
                        cntb["pS"] += 1
                        pS = pS_l[si_]
                        b_pS = b_pS_l[si_]

                        def mms(e, sl=sl, q0=q0, nq=nq, g=g, pS=pS):
                            ins = None
                            for hpi in range(2):
                                ins = e.matmul(out=pS[:, hpi * 192:hpi * 192 + nq], lhsT=kbd[:, sl, g, :], rhs=qT[:, 2 * g + hpi, q0:q0 + nq],
                                               start=True, stop=True)
                            return ins
                        op(PE, mms, reads=[b_kbd[sl], b_qT[2 * g], b_qT[2 * g + 1]], writes=[b_pS], c=0.35)
                        pi_ = cntb["pT"] % NPT
                        cntb["pT"] += 1
                        psv = pS[:, 0:384].rearrange("p (h q) -> p h q", h=2)[:, :, 0:nq]
                        ptv = pT[pi_][:, :].rearrange("p (h q) -> p h q", h=2)[:, :, 0:nq]
                        op(ACT, lambda e, psv=psv, ptv=ptv: e.activation(out=ptv, in_=psv, func=AF.Exp, scale=0.125), reads=[b_pS], writes=[b_pT[pi_]], ts="e")

                        def mmo(e, sl=sl, q0=q0, nq=nq, g=g, pi_=pi_, onesm=onesm, first=first):
                            ins = None
                            for hpi in range(2):
                                ins = e.matmul(out=pO[hpi][:, q0:q0 + nq], lhsT=vbd[:, sl, g, :], rhs=pT[pi_][:, hpi * 192:hpi * 192 + nq],
                                               start=first, stop=False, skip_group_check=True)
                            for hpi in range(2):
                                ins = e.matmul(out=pD[hpi][:, q0:q0 + nq], lhsT=onesm[:, :], rhs=pT[pi_][:, hpi * 192:hpi * 192 + nq],
                                               start=first, stop=False, skip_group_check=True)
                            return ins
                        op(PE, mmo, reads=[b_vbd[sl], b_pT[pi_], b_onesm], writes=b_pO + b_pD, c=0.7)
                    for hpi in range(2):
                        hp = 2 * g + hpi
                        op(ACT, lambda e, hpi=hpi: e.activation(out=oraw[hpi][:, 0:N], in_=pO[hpi][:, 0:N], func=AF.Copy),
                           reads=[b_pO[hpi]], writes=[b_oraw[hpi]])
                        op(DVE, lambda e, hpi=hpi, hp=hp: e.tensor_scalar(out=rden[hpi][:, 0:N], in0=pD[hpi][:, 0:N], scalar1=esk[:, hp:hp + 1], scalar2=None, op0=ALU.add),
                           reads=[b_pD[hpi], b_esk], writes=[b_rden[hpi]])
                    for hpi in range(2):
                        hp = 2 * g + hpi
                        op(DVE, lambda e, hpi=hpi: e.reciprocal(out=rden[hpi][:, 0:N], in_=rden[hpi][:, 0:N]), reads=[b_rden[hpi]], writes=[b_rden[hpi]], c=1.8)
                        op(POOL, lambda e, hpi=hpi: e.tensor_tensor(out=oraw[hpi][:, 0:N], in0=oraw[hpi][:, 0:N], in1=rden[hpi][:, 0:N], op=ALU.mult),
                           reads=[b_oraw[hpi], b_rden[hpi]], writes=[b_oraw[hpi]])
                        i_ = cntb["pz"] % 2
                        cntb["pz"] += 1
                        pzx, b_pzx = pz_l[i_], b_pz_l[i_]
                        g_ = cntb["gt"] % 2
                        cntb["gt"] += 1
                        tgx, sgx, b_tgx, b_sgx = tg_l[g_], sg_l[g_], b_tg_l[g_], b_sg_l[g_]
                        op(PE, lambda e, hp=hp, pzx=pzx: proj(e, wgb, hp * 128, pzx), reads=b_wg + xall, writes=[b_pzx], c=2.0)
                        op(ACT, lambda e, pzx=pzx, tgx=tgx: e.activation(out=tgx[:, 0:N], in_=pzx[:, 0:N], func=AF.Tanh, scale=0.5), reads=[b_pzx], writes=[b_tgx], ts="e")
                        op(DVE, lambda e, pzx=pzx, tgx=tgx, sgx=sgx: e.scalar_tensor_tensor(out=sgx[:, 0:N], in0=tgx[:, 0:N], scalar=1.0, in1=pzx[:, 0:N], op0=ALU.add, op1=ALU.mult),
                           reads=[b_tgx, b_pzx], writes=[b_sgx], c=1.0)
                        op(POOL, lambda e, hp=hp, hpi=hpi, sgx=sgx: e.tensor_tensor(out=ogb[:, hp, 0:N], in0=oraw[hpi][:, 0:N], in1=sgx[:, 0:N], op=ALU.mult),
                           reads=[b_oraw[hpi], b_sgx], writes=[b_ogb[hp]])

                if KN('K_PYB', 1) == 1:
                    pyb = [pz_l[0], pz_l[1]]
                    b_pyb = [b_pz_l[0], b_pz_l[1]]
                else:
                    pyb = [pz, pzr]
                    b_pyb = [b_pz, b_pzr]
                for s_ in range(nsub):
                    P = min(128, N - 128 * s_)
                    for h in range(2):
                        def mm_out(e, h=h, s_=s_, P=P):
                            ins = None
                            for kc in range(KC):
                                ins = e.matmul(out=pyb[h][0:P, :], lhsT=ogb[:, kc, 128 * s_:128 * s_ + P], rhs=wob[:, kc, h * 512:(h + 1) * 512],
                                               start=(kc == 0), stop=(kc == KC - 1))
                            return ins
                        op(PE, mm_out, reads=b_ogb + b_wob, writes=[b_pyb[h]], c=2.0)
                    k = cntb["ss2"] % 2
                    cntb["ss2"] += 1
                    yi_ = cntb["yn"] % 2
                    cntb["yn"] += 1
                    yn, b_yn = yn_l[yi_], b_yn_l[yi_]
                    for h in range(2):
                        op(ACT, lambda e, h=h, k=k, P=P: e.activation(out=junk2[0:P, h, :], in_=pyb[h][0:P, :], func=AF.Square,
                                                                      accum_out=ss2[0:P, 4 * k + h:4 * k + h + 1]),
                           reads=[b_pyb[h]], writes=[b_junk2[h], b_ss2[k]])
                        op(DVE, lambda e, h=h, P=P, yn=yn: e.tensor_copy(out=yn[0:P, h * 512:(h + 1) * 512], in_=pyb[h][0:P, :]),
                           reads=[b_pyb[h], b_junk2[h]], writes=[b_yn], c=0.7)
                    op(DVE, lambda e, k=k, P=P: e.tensor_tensor(out=ss2[0:P, 4 * k + 2:4 * k + 3], in0=ss2[0:P, 4 * k:4 * k + 1],
                                                                in1=ss2[0:P, 4 * k + 1:4 * k + 2], op=ALU.add), reads=[b_ss2[k]], writes=[b_ss2[k]])
                    rstd_from_ss(ss2[0:P, 4 * k + 2:4 * k + 3], [b_ss2[k]], b_ss2[k], ss2[0:P, 4 * k + 3:4 * k + 4])
                    for h in range(2):
                        op(DVE, lambda e, h=h, k=k, P=P, yn=yn: e.scalar_tensor_tensor(
                            out=yn[0:P, h * 512:(h + 1) * 512], in0=yn[0:P, h * 512:(h + 1) * 512], scalar=ss2[0:P, 4 * k + 3:4 * k + 4],
                            in1=gpost_b[0:P, h * 512:(h + 1) * 512], op0=ALU.mult, op1=ALU.mult),
                           reads=[b_yn, b_ss2[k], b_gpb], writes=[b_yn], c=0.8)
                    i = cntb["xr"] % NXR
                    cntb["xr"] += 1
                    r0 = row0 + 128 * s_
                    op(SP, lambda e, i=i, P=P, r0=r0: e.dma_start(out=xr[i][0:P, :], in_=src1[r0:r0 + P, :]),
                       reads=[x1_bufs[(kind, r0)]], writes=[b_xr[i]], dsem=s_xr[i], n=1024)
                    op(POOL, lambda e, i=i, P=P, yn=yn: e.tensor_tensor(out=xr[i][0:P, :], in0=xr[i][0:P, :], in1=yn[0:P, :], op=ALU.add),
                       reads=[b_xr[i], b_yn], writes=[b_xr[i]], n=1024)
                    op(SP, lambda e, i=i, P=P, r0=r0: e.dma_start(out=dsty[r0:r0 + P, :], in_=xr[i][0:P, :]), reads=[b_xr[i]], dsem=s_xo[i], n=1024)

            for tl in tiles:
                phaseB_tile(*tl)
            kb.emit(block, last=True)
            psB.close()
            stB.close()

        for tl in tiles:
            phaseA_tile(*tl)

        if do_b:
            gate_t = sb(stA, "gate_t", [128, 1])
            b_gate = kb.buf()
            op(POOL, lambda e: e.memset(gate_t[:], 0.0), after=b_wia, writes=[b_gate], n=1)
            ci = 0
            for kc in range(KC):
                k2 = ci % 2
                ci += 1
                gs = g_kv[:, kc:kc + 1]
                st_ = stgE[k2]
                op(SP, lambda e, kc=kc, st_=st_: e.dma_start(out=st_[:, 0:512], in_=w_kv[kc * 128:(kc + 1) * 128, :]),
                   reads=[b_gate], writes=[b_stgE[k2]], dsem=s_stg[k2], n=512)
                rd = [b_stgE[k2], b_g_kv, b_gate]
                op(DVE, lambda e, kc=kc, st_=st_, gs=gs: e.tensor_scalar(out=wk[:, kc, 0:256], in0=st_[:, 0:256], scalar1=gs, scalar2=None, op0=ALU.mult),
                   reads=rd, writes=[b_wk[kc]], n=256)
                op(ACT, lambda e, kc=kc, st_=st_, gs=gs: e.activation(out=wv[:, kc, :], in_=st_[:, 256:512], func=AF.Identity, scale=gs),
                   reads=rd, writes=[b_wk[kc]], n=256)
                kin = st_[:, 0:256].rearrange("p (a b d) -> p a b d", a=2, b=2)
                ksw = wk[:, kc, 256:512].rearrange("p (a b d) -> p a b d", a=2, b=2)
                for b2 in range(2):
                    op(DVE,
                       lambda e, kin=kin, ksw=ksw, b2=b2, gs=gs: e.tensor_scalar(out=ksw[:, :, b2, :], in0=kin[:, :, 1 - b2, :], scalar1=gs, scalar2=None, op0=ALU.mult),
                       reads=rd, writes=[b_wk[kc]], n=128)
            for kc in range(KC):
                k2 = ci % 2
                ci += 1
                gs = g_b[:, kc:kc + 1]
                st_ = stgE[k2]
                op(SP, lambda e, kc=kc, st_=st_: e.dma_start(out=st_[:, 0:1024], in_=w_in_b[kc * 128:(kc + 1) * 128, 0:1024]),
                   reads=[b_gate], writes=[b_stgE[k2]], dsem=s_stg[k2], n=1024)
                op(DVE, lambda e, kc=kc, st_=st_, gs=gs: e.tensor_scalar(out=wq[:, kc, :], in0=st_[:, 0:1024], scalar1=gs, scalar2=None, op0=ALU.mult),
                   reads=[b_stgE[k2], b_g_b, b_gate], writes=[b_wq[kc]], n=1024)

        if not do_b:
            for (kind, r0), bx in x1_bufs.items():
                P = 128 if kind == "p" else 64
                i = cnt["xb"] % NXB
                cnt["xb"] += 1
                srcd = x1p if kind == "p" else x1s
                dstd = yp if kind == "p" else ys
                op(SP, lambda e, i=i, P=P, r0=r0, srcd=srcd: e.dma_start(out=xb[i][0:P, :], in_=srcd[r0:r0 + P, :]),
                   reads=[bx], writes=[b_xb[i]], dsem=s_xb[i])
                op(SP, lambda e, i=i, P=P, r0=r0, dstd=dstd: e.dma_start(out=dstd[r0:r0 + P, :], in_=xb[i][0:P, :]),
                   reads=[b_xb[i]], dsem=s_xo[i])

        kb.emit(block, last=not do_b)
        kb.alias_pre = kb.barrier_tokens()
        psA.close()
        stA.close()
        if do_b:
            phase_b()
    return nc


_NC_CACHE = {}


def _prot_matrix():
    p = np.zeros((128, 128), np.float32)
    for m in range(128):
        if m % 64 < 32:
            p[m + 32, m] = -1.0
        else:
            p[m - 32, m] = 1.0
    return p


def _rope_tables():
    half = 32
    inv = (np.float32(10000.0) ** (-np.arange(half, dtype=np.float32) / np.float32(half))).astype(np.float32)
    pos = np.concatenate([np.arange(SEQ), PAST + np.arange(SL), PAST + np.arange(SL)]).astype(np.float32)
    ang = pos[:, None] * inv[None, :]
    cos = np.cos(ang).astype(np.float32)
    sin = np.sin(ang).astype(np.float32)
    idx = np.arange(128) % 32
    return np.ascontiguousarray(cos[:, idx].T), np.ascontiguousarray(sin[:, idx].T)


def kernel(x_prompt, x_sample, state_conv, state_rnn, cache_k, cache_v,
           norm_pre_a, w_in_a, conv_w_a, conv_b_a, w_gate_r, b_gate_r, w_gate_i, b_gate_i,
           lru_lambda, w_out_a, norm_post_a, norm_kv, w_kv,
           norm_pre_b, w_in_b, attn_sinks, w_out_b, norm_post_b, _do_b=True):
    f = lambda a: np.ascontiguousarray(np.asarray(a, dtype=np.float32))
    key = bool(_do_b)
    if key not in _NC_CACHE:
        _NC_CACHE[key] = build_program(do_b=key)
    nc = _NC_CACHE[key]
    cos_t, sin_t = _rope_tables()
    shared = {
        "norm_pre_a": f(norm_pre_a).reshape(D), "w_in_a": f(w_in_a).reshape(D, 2 * DR),
        "conv_w": f(conv_w_a).reshape(4, DR), "conv_b": f(conv_b_a).reshape(DR),
        "w_gate_r": f(w_gate_r).reshape(16, 88, 88), "b_gate_r": f(b_gate_r).reshape(DR),
        "w_gate_i": f(w_gate_i).reshape(16, 88, 88), "b_gate_i": f(b_gate_i).reshape(DR),
        "lru_lambda": f(lru_lambda).reshape(DR), "w_out_a": f(w_out_a).reshape(DR, D),
        "norm_post_a": f(norm_post_a).reshape(1, D), "norm_kv": f(norm_kv).reshape(D),
        "w_kv": f(w_kv).reshape(D, 512), "norm_pre_b": f(norm_pre_b).reshape(D),
        "w_in_b": f(w_in_b).reshape(D, 2048), "sinks": f(attn_sinks).reshape(16),
        "w_out_b": f(w_out_b).reshape(D, D), "norm_post_b": f(norm_post_b).reshape(1, D),
        "ident": np.eye(128, dtype=np.float32), "prot": _prot_matrix(), "cos_t": cos_t, "sin_t": sin_t,
    }
    xpf, xsf = f(x_prompt), f(x_sample)
    scf, srf = f(state_conv)[0], f(state_rnn)[0]
    ckf, cvf = f(cache_k).reshape(16, 128, 256), f(cache_v).reshape(16, 128, 256)
    in_maps = []
    for c in range(NCORES):
        sl = slice(NSEQ * c, NSEQ * c + NSEQ)
        m = dict(shared)
        m["xp"] = xpf[sl].reshape(NSEQ * SEQ, D)
        m["xs"] = xsf[sl].reshape(NSEQ * SL, D)
        m["st_conv"] = np.ascontiguousarray(scf[sl])
        m["st_rnn"] = np.ascontiguousarray(srf[sl])
        m["cache_k"] = np.ascontiguousarray(ckf[sl])
        m["cache_v"] = np.ascontiguousarray(cvf[sl])
        in_maps.append(m)
    res = run_bass_kernel_spmd(nc, in_maps, core_ids=list(range(NCORES)))
    R = res.results
    cat = lambda k: np.concatenate([np.asarray(r[k], dtype=np.float32) for r in R], axis=0)
    y_prompt = cat("yp").reshape(16, SEQ, D)
    y_sample = cat("ys").reshape(16, SL, D)
    pc = cat("p_conv").reshape(1, 16, 3, DR)
    prn = cat("p_rnn").reshape(1, 16, DR)
    pk = cat("p_k").reshape(16, 128, 4, 64)
    pv = cat("p_v").reshape(16, 128, 4, 64)
    sc = cat("s_conv").reshape(1, 16, 3, DR)
    srn = cat("s_rnn").reshape(1, 16, DR)
    sk = cat("s_k").reshape(16, SL, 4, 64)
    sv = cat("s_v").reshape(16, SL, 4, 64)
    return (y_prompt, y_sample, pc, prn, pk, pv, sc, srn, sk, sv)
```

```python
import contextlib
import numpy as np
import concourse.bass as bass
import concourse.mybir as mybir
from concourse.bass_utils import run_bass_kernel_spmd

F32 = mybir.dt.float32
BF16 = mybir.dt.bfloat16
AF = mybir.ActivationFunctionType
ALU = mybir.AluOpType

NCORES = 8
D = 1024
DR = 1408
NCH = 11
KC = 8
SEQ = 2048
SL = 32
NSEQ = 2
PAST = 1024
EPS = 1e-6
import os
ROT = int(os.environ.get('K_ROT', '800'))
KN = lambda k, d: int(os.environ.get(k, d))
NPOS = SEQ + 2 * SL


class Sem:
    def __init__(self, h):
        self.h = h
        self.v = 0


class Eng:
    def __init__(self, name, sem):
        self.name = name
        self.sem = sem
        self.seen = {}


class Buf:
    __slots__ = ("w", "r", "pre")

    def __init__(self, pre=None):
        self.w = []
        self.r = []
        self.pre = dict(pre) if pre else None


class Op:
    __slots__ = ("eng", "fn", "deps", "pre", "dsem", "cost", "lat", "tok", "idx", "nd", "ready", "users", "start", "fin", "ts")


class DS:
    def __init__(self, kb, name, s):
        self.kb = kb
        self.name = name
        self.s = s


COST0 = {"pe": 0.25, "act": 0.25, "dve": 0.15, "pool": 0.25, "sp": 0.05}
COSTN = {"pe": 1.0 / 2400, "act": 1.0 / 1200, "dve": 1.0 / 900, "pool": 1.0 / 420, "sp": 0.0}


class KB:
    def __init__(self, nc, stack):
        self.nc = nc
        self.stack = stack
        self.nsem = 0
        self.PE = Eng("pe", self.new_sem("pe"))
        self.ACT = Eng("act", self.new_sem("act"))
        self.DVE = Eng("dve", self.new_sem("dve"))
        self.POOL = Eng("pool", self.new_sem("pool"))
        self.SP = Eng("sp", self.new_sem("sp"))
        self.engs = [self.PE, self.ACT, self.DVE, self.POOL, self.SP]
        self.dsems = []
        self.alias_pre = {}
        self.esems = [e.sem for e in self.engs]
        self.pending = []
        self.nops = 0

    def new_sem(self, name):
        self.nsem += 1
        return Sem(self.stack.enter_context(self.nc.semaphore(name)))

    def dsem(self, name):
        s = self.new_sem(name)
        self.dsems.append(s)
        return DS(self, name, s)

    def buf(self):
        return Buf(self.alias_pre)

    def mark(self):
        return len(self.pending)

    def since(self, m):
        return self.pending[m:]

    def op(self, eng, fn, reads=(), writes=(), dsem=None, n=512, c=None, after=(), ts=None):
        o = Op()
        o.ts = ts
        o.eng = eng
        o.fn = fn
        o.dsem = dsem
        o.tok = None
        if dsem is not None:
            o.cost = 0.06
            o.lat = 2.5 + n / 290.0
        else:
            o.cost = c if c is not None else COST0[eng.name] + n * COSTN[eng.name]
            o.lat = o.cost
        deps = {}
        pre = {}
        for b in reads:
            for w in b.w:
                deps[id(w)] = w
        for b in list(writes) + list(after):
            for w in b.w:
                deps[id(w)] = w
            for w in b.r:
                deps[id(w)] = w
            if b.pre:
                for s_, v in b.pre.items():
                    if pre.get(s_, 0) < v:
                        pre[s_] = v
                b.pre = None
        o.deps = list(deps.values())
        o.pre = pre
        self.pending.append(o)
        for b in reads:
            b.r.append(o)
        for b in writes:
            b.w = [o]
            b.r = []
        return o

    def barrier_tokens(self):
        d = {}
        for sm in self.esems:
            if sm.v:
                d[sm] = sm.v
        for s in self.dsems:
            if s.v:
                d[s] = s.v
        return d

    def _schedule(self, ops, W=128):
        for i, o in enumerate(ops):
            o.idx = i
            o.users = []
            o.nd = 0
            o.ready = 0.0
            o.start = None
        inseg = set(id(o) for o in ops)
        for o in ops:
            for d in o.deps:
                if id(d) in inseg:
                    d.users.append(o)
                    o.nd += 1
        tail = {}
        for o in reversed(ops):
            t_ = 0.0
            for u in o.users:
                if tail[id(u)] > t_:
                    t_ = tail[id(u)]
            tail[id(o)] = t_ + o.lat
        PB = float(os.environ.get('K_PB', '5'))
        queues = {e.name: [] for e in self.engs}
        for o in ops:
            queues[o.eng.name].append(o)
        heads = {k: 0 for k in queues}
        free = {k: 0.0 for k in queues}
        order = {k: [] for k in queues}
        glob = []
        cur_ts = {}
        left = len(ops)
        while left:
            best = None
            for k, q in queues.items():
                h = heads[k]
                while h < len(q) and q[h].start is not None:
                    h += 1
                heads[k] = h
                cnt = 0
                i = h
                while i < len(q) and cnt < W:
                    o = q[i]
                    i += 1
                    if o.start is not None:
                        continue
                    cnt += 1
                    if o.nd:
                        continue
                    st = o.ready if o.ready > free[k] else free[k]
                    pen = 0.0
                    if o.ts is not None and o.ts != cur_ts.get(k):
                        pen = 2.0
                    if PB > 0:
                        key = (int((st + pen) / PB), -tail[id(o)], o.idx)
                    else:
                        key = (st + pen, o.idx)
                    if best is None or key < best[0]:
                        best = (key, o, k, st)
                    if PB <= 0 and st + pen <= free[k]:
                        break
            _, o, k, st = best
            if o.ts is not None and o.ts != cur_ts.get(k):
                cur_ts[k] = o.ts
                st += 1.3
            o.start = st
            free[k] = st + o.cost
            o.fin = st + o.lat
            order[k].append(o)
            glob.append(o)
            left -= 1
            for u in o.users:
                u.nd -= 1
                if u.ready < o.fin:
                    u.ready = o.fin
        return order, glob

    def emit(self, block, last=True):
        ops = self.pending
        self.pending = []
        order, glob = self._schedule(ops)
        for o in glob:
            if o.dsem is not None:
                sm = o.dsem.s
                sm.v += 16
                o.tok = (sm, sm.v)
        for e in self.engs:
            for o in order[e.name]:
                if o.dsem is None:
                    if e.sem.v >= ROT:
                        e.sem = self.new_sem(e.name + "_%d" % self.nsem)
                        self.esems.append(e.sem)
                    e.sem.v += 1
                    o.tok = (e.sem, e.sem.v)
        prog = {}
        for e in self.engs:
            lst = []
            for o in order[e.name]:
                need = dict(o.pre)
                for d in o.deps:
                    s_, v = d.tok
                    if need.get(s_, 0) < v:
                        need[s_] = v
                waits = []
                for s_, v in need.items():
                    if e.seen.get(s_, 0) < v:
                        e.seen[s_] = v
                        waits.append((s_, v))
                lst.append((waits, o.fn, o.tok[0], 16 if o.dsem is not None else 1))
            prog[e.name] = lst
        final = self.barrier_tokens() if last else {}

        def run(e, lst):
            for waits, fn, sem, inc in lst:
                for s_, v in waits:
                    e.wait_ge(s_.h, v)
                ins = fn(e)
                ins.then_inc(sem.h, inc)

        @block.tensor
        def _(e):
            run(e, prog["pe"])

        @block.scalar
        def _(e):
            run(e, prog["act"])

        @block.vector
        def _(e):
            run(e, prog["dve"])

        @block.gpsimd
        def _(e):
            run(e, prog["pool"])

        @block.sync
        def _(e):
            run(e, prog["sp"])
            for s_, v in final.items():
                e.wait_ge(s_.h, v)


def build_program(do_b=True, tile_sel=None):
    nc = bass.Bass("TRN2", target_bir_lowering=False)

    def din(name, shape):
        return nc.dram_tensor(name, list(shape), F32, kind="ExternalInput").ap()

    def dout(name, shape):
        return nc.dram_tensor(name, list(shape), F32, kind="ExternalOutput").ap()

    xp = din("xp", [NSEQ * SEQ, D])
    xs = din("xs", [NSEQ * SL, D])
    st_conv = din("st_conv", [NSEQ, 3, DR])
    st_rnn = din("st_rnn", [NSEQ, DR])
    cache_k = din("cache_k", [NSEQ, 128, 256])
    cache_v = din("cache_v", [NSEQ, 128, 256])
    norm_pre_a = din("norm_pre_a", [D])
    w_in_a = din("w_in_a", [D, 2 * DR])
    conv_w = din("conv_w", [4, DR])
    conv_b = din("conv_b", [DR])
    w_gate_r = din("w_gate_r", [16, 88, 88])
    b_gate_r = din("b_gate_r", [DR])
    w_gate_i = din("w_gate_i", [16, 88, 88])
    b_gate_i = din("b_gate_i", [DR])
    lru_lambda = din("lru_lambda", [DR])
    w_out_a = din("w_out_a", [DR, D])
    norm_post_a = din("norm_post_a", [1, D])
    norm_kv = din("norm_kv", [D])
    w_kv = din("w_kv", [D, 512])
    norm_pre_b = din("norm_pre_b", [D])
    w_in_b = din("w_in_b", [D, 2048])
    sinks = din("sinks", [16])
    w_out_b = din("w_out_b", [D, D])
    norm_post_b = din("norm_post_b", [1, D])
    ident_d = din("ident", [128, 128])
    prot_d = din("prot", [128, 128])
    cos_d = din("cos_t", [128, NPOS])
    sin_d = din("sin_t", [128, NPOS])

    yp = dout("yp", [NSEQ * SEQ, D])
    ys = dout("ys", [NSEQ * SL, D])
    p_conv = dout("p_conv", [NSEQ, 3, DR])
    p_rnn = dout("p_rnn", [NSEQ, DR])
    p_k = dout("p_k", [NSEQ, 128, 256])
    p_v = dout("p_v", [NSEQ, 128, 256])
    s_conv = dout("s_conv", [NSEQ, 3, DR])
    s_rnn = dout("s_rnn", [NSEQ, DR])
    s_k = dout("s_k", [NSEQ, SL, 256])
    s_v = dout("s_v", [NSEQ, SL, 256])

    x1p = nc.dram_tensor("x1p", [NSEQ * SEQ, D], F32).ap()
    x1s = nc.dram_tensor("x1s", [NSEQ * SL, D], F32).ap()

    with contextlib.ExitStack() as stack, nc.Block() as block:
        stack.enter_context(nc.allow_non_contiguous_dma("small strided parameter / state loads"))
        stack.enter_context(nc.allow_low_precision("bf16 matmul operands, fp32 accumulation"))
        kb = KB(nc, stack)
        op = kb.op
        PE, ACT, DVE, POOL, SP = kb.PE, kb.ACT, kb.DVE, kb.POOL, kb.SP

        def sb(st, name, shape, dt=F32):
            return st.enter_context(nc.sbuf_tensor(name, list(shape), dt))

        def ps(st, name, shape, dt=F32):
            return st.enter_context(nc.psum_tensor(name, list(shape), dt))

        tiles = []
        tiles.append(("s", [0, 1], SL, 0, 0))
        for s in range(NSEQ):
            for t in range(SEQ // 512):
                tiles.append(("p", [s], 512, s * SEQ + t * 512, t))
        if tile_sel is not None:
            tiles = [tiles[i] for i in tile_sel]

        x1_bufs = {}

        ident_f = sb(stack, "ident_f", [128, 128])
        ident_b = sb(stack, "ident_b", [128, 128], BF16)
        b_ident_f, b_ident_b = kb.buf(), kb.buf()
        s_const = kb.dsem("const")
        op(SP, lambda e: e.dma_start(out=ident_f[:], in_=ident_d), writes=[b_ident_f], dsem=s_const)
        epsb = sb(stack, "epsb", [128, 1])
        b_eps = kb.buf()
        op(POOL, lambda e: e.memset(epsb[:], EPS), writes=[b_eps])
        q25 = sb(stack, "q25", [128, 1])
        b_q25 = kb.buf()
        op(POOL, lambda e: e.memset(q25[:], 0.25), writes=[b_q25])

        def load_fm(st, name, src1d, nchunk, sem, eng=SP):
            t = sb(st, name, [128, nchunk])
            b = kb.buf()
            op(eng, lambda e: e.dma_start(out=t[:], in_=src1d.rearrange("(c p) -> p c", p=128)),
               writes=[b], dsem=sem)
            return t, b

        stA = contextlib.ExitStack()
        wbig = sb(stack, "wbig", [128, KC * 2 * DR], BF16)
        wbigv = wbig[:, :]
        wia = wbigv.rearrange("p (k n) -> p k n", k=KC)
        stgE = [wbigv[:, 0:4096].bitcast(F32), wbigv[:, 4096:8192].bitcast(F32)]
        wq = wbigv[:, 8192:16384].rearrange("p (k n) -> p k n", k=KC)
        wk = wbigv[:, 16384:20480].rearrange("p (k n) -> p k n", k=KC)
        wv = wbigv[:, 20480:22528].rearrange("p (k n) -> p k n", k=KC)
        b_wq = [kb.buf() for _ in range(KC)]
        b_wk = [kb.buf() for _ in range(KC)]
        b_stgE = [kb.buf() for _ in range(2)]
        g_kv, b_g_kv = load_fm(stack, "g_kv", norm_kv, KC, s_const)
        g_b, b_g_b = load_fm(stack, "g_b", norm_pre_b, KC, s_const)
        woa = sb(stA, "woa", [128, NCH, D], BF16)
        wgr = sb(stA, "wgr", [128, NCH, 3, 128], BF16)
        wgi = sb(stA, "wgi", [128, NCH, 3, 128], BF16)
        b_wia = [kb.buf() for _ in range(KC)]
        b_woa = [kb.buf() for _ in range(NCH)]
        b_wgr, b_wgi = kb.buf(), kb.buf()
        g_a, b_g_a = load_fm(stA, "g_a", norm_pre_a, KC, s_const)
        cb_t, b_cb = load_fm(stA, "cb_t", conv_b, NCH, s_const)
        br_t, b_br = load_fm(stA, "br_t", b_gate_r, NCH, s_const)
        bi_t, b_bi = load_fm(stA, "bi_t", b_gate_i, NCH, s_const)
        lam_t, b_lam = load_fm(stA, "lam_t", lru_lambda, NCH, s_const)
        cw_t = sb(stA, "cw_t", [128, 4, NCH])
        b_cw = kb.buf()
        op(SP, lambda e: e.dma_start(out=cw_t[:], in_=conv_w.rearrange("t (c p) -> p t c", p=128)),
           writes=[b_cw], dsem=s_const)
        gpost_a = sb(stA, "gpost_a", [128, D])
        b_gpa = kb.buf()
        op(SP, lambda e: e.dma_start(out=gpost_a[:], in_=norm_post_a.partition_broadcast(128)),
           writes=[b_gpa], dsem=s_const)

        for b_ in (b_ident_f, b_g_a, b_cb, b_br, b_bi, b_lam, b_cw, b_gpa, b_g_kv, b_g_b):
            b_.w = [o_ for o_ in kb.since(0) if o_.dsem is s_const]
        op(DVE, lambda e: e.tensor_copy(out=ident_b[:], in_=ident_f[:]), reads=[b_ident_f], writes=[b_ident_b])

        op(POOL, lambda e: e.memset(wgr[:], 0.0), writes=[b_wgr])
        op(POOL, lambda e: e.memset(wgi[:], 0.0), writes=[b_wgi])
        s_gate = kb.dsem("gate")
        for (wsrc, wdst, bw) in ((w_gate_r, wgr, b_wgr), (w_gate_i, wgi, b_wgi)):
            for blk in range(16):
                lo = 88 * blk
                hi = lo + 88
                for i in range(lo // 128, (hi - 1) // 128 + 1):
                    r0, r1 = max(lo, 128 * i), min(hi, 128 * i + 128)
                    for j in range(lo // 128, (hi - 1) // 128 + 1):
                        c0, c1 = max(lo, 128 * j), min(hi, 128 * j + 128)
                        op(POOL,
                           (lambda e, wsrc=wsrc, wdst=wdst, blk=blk, r0=r0, r1=r1, c0=c0, c1=c1, i=i, j=j, lo=lo:
                            e.dma_start(out=wdst[r0 - 128 * i:r1 - 128 * i, i, j - i + 1, c0 - 128 * j:c1 - 128 * j],
                                        in_=wsrc[blk, r0 - lo:r1 - lo, c0 - lo:c1 - lo])),
                           reads=[bw], dsem=s_gate, n=64)

        b_wgr.w = [o_ for o_ in kb.since(0) if o_.dsem is s_gate] + b_wgr.w
        b_wgi.w = list(b_wgr.w) + b_wgi.w

        hbr = sb(stA, "hbr", [128, NCH])
        hbi = sb(stA, "hbi", [128, NCH])
        cc = sb(stA, "cc", [128, NCH])
        hc = sb(stA, "hc", [128, NCH])
        xl = sb(stA, "xl", [128, NCH])
        pl = sb(stA, "pl", [128, NCH])
        b_hbr, b_hbi, b_cc, b_hc, b_xl, b_pl = (kb.buf() for _ in range(6))
        op(DVE, lambda e: e.tensor_scalar(out=hbr[:], in0=br_t[:], scalar1=0.5, scalar2=None, op0=ALU.mult),
           reads=[b_br], writes=[b_hbr])
        op(DVE, lambda e: e.tensor_scalar(out=hbi[:], in0=bi_t[:], scalar1=0.5, scalar2=None, op0=ALU.mult),
           reads=[b_bi], writes=[b_hbi])
        op(ACT, lambda e: e.activation(out=xl[:], in_=lam_t[:], func=AF.Exp, scale=-1.0), reads=[b_lam], writes=[b_xl], ts="e")
        coef = [1.0, -1.0 / 2, 1.0 / 3, -1.0 / 4, 1.0 / 5, -1.0 / 6, 1.0 / 7, -1.0 / 8]
        op(DVE, lambda e: e.tensor_scalar(out=pl[:], in0=xl[:], scalar1=coef[7], scalar2=coef[6], op0=ALU.mult, op1=ALU.add),
           reads=[b_xl], writes=[b_pl])
        for k in range(5, -1, -1):
            op(DVE, lambda e: e.tensor_tensor(out=pl[:], in0=pl[:], in1=xl[:], op=ALU.mult), reads=[b_pl, b_xl], writes=[b_pl])
            op(DVE, lambda e, k=k: e.tensor_scalar(out=pl[:], in0=pl[:], scalar1=coef[k], scalar2=None, op0=ALU.add),
               reads=[b_pl], writes=[b_pl])
        op(DVE, lambda e: e.tensor_tensor(out=pl[:], in0=pl[:], in1=xl[:], op=ALU.mult), reads=[b_pl, b_xl], writes=[b_pl])
        op(DVE, lambda e: e.tensor_scalar(out=cc[:], in0=pl[:], scalar1=-8.0, scalar2=None, op0=ALU.mult),
           reads=[b_pl], writes=[b_cc])
        op(DVE, lambda e: e.tensor_scalar(out=hc[:], in0=pl[:], scalar1=-4.0, scalar2=None, op0=ALU.mult),
           reads=[b_pl], writes=[b_hc])

        stS = contextlib.ExitStack()
        NSTG = 5
        stg = [sb(stS, "stg%d" % i, [128, 2 * DR]) for i in range(NSTG)]
        b_stg = [kb.buf() for _ in range(NSTG)]
        s_stg = [kb.dsem("stg%d" % i) for i in range(NSTG)]
        ci = 0
        for kc in range(KC):
            k2 = ci % NSTG
            dq = SP if ci % 2 == 0 else ACT
            op(dq, lambda e, kc=kc, k2=k2: e.dma_start(out=stg[k2][:], in_=w_in_a[kc * 128:(kc + 1) * 128, :]),
               writes=[b_stg[k2]], dsem=s_stg[k2])
            half = DR
            op(DVE, lambda e, kc=kc, k2=k2: e.tensor_scalar(out=wia[:, kc, 0:half], in0=stg[k2][:, 0:half],
                                                             scalar1=g_a[:, kc:kc + 1], scalar2=None, op0=ALU.mult),
               reads=[b_stg[k2], b_g_a], writes=[b_wia[kc]])
            op(ACT, lambda e, kc=kc, k2=k2: e.activation(out=wia[:, kc, half:2 * half], in_=stg[k2][:, half:2 * half],
                                                          func=AF.Identity, scale=g_a[:, kc:kc + 1]),
               reads=[b_stg[k2], b_g_a], writes=[b_wia[kc]])
            ci += 1
        for j in range(NCH):
            k2 = ci % NSTG
            dq = SP if ci % 2 == 0 else ACT
            op(dq, lambda e, j=j, k2=k2: e.dma_start(out=stg[k2][:, 0:D], in_=w_out_a[j * 128:(j + 1) * 128, :]),
               writes=[b_stg[k2]], dsem=s_stg[k2])
            if j % 2 == 0:
                op(DVE, lambda e, j=j, k2=k2: e.tensor_scalar(out=woa[:, j, :], in0=stg[k2][:, 0:D], scalar1=0.5, scalar2=None, op0=ALU.mult),
                   reads=[b_stg[k2]], writes=[b_woa[j]])
            else:
                op(ACT, lambda e, j=j, k2=k2: e.activation(out=woa[:, j, :], in_=stg[k2][:, 0:D], func=AF.Identity, scale=0.5),
                   reads=[b_stg[k2]], writes=[b_woa[j]])
            ci += 1
        kb.emit(block, last=False)
        kb.alias_pre = kb.barrier_tokens()
        stS.close()

        NT = 512
        brs = sb(stA, "brs", [128, NCH, 3 + NT])
        b_brs = [kb.buf() for _ in range(NCH)]
        hst = sb(stA, "hst", [128, NSEQ, NCH])
        b_hst = [kb.buf() for _ in range(NCH)]
        hstp = sb(stA, "hstp", [128, NSEQ, NCH])
        b_hstp = [[kb.buf() for _ in range(NCH)] for _ in range(NSEQ)]
        hsv = sb(stA, "hsv", [128, NSEQ, NCH, 3])
        b_hsv = [[kb.buf() for _ in range(NCH)] for _ in range(NSEQ)]
        NCV = 4
        cv = [sb(stA, "cv%d" % i, [128, NT]) for i in range(NCV)]
        cvb = [sb(stA, "cvb%d" % i, [128, NT], BF16) for i in range(NCV)]
        b_cv = [kb.buf() for _ in range(NCV)]
        b_cvb = [kb.buf() for _ in range(NCV)]
        tg = [sb(stA, "tg%d" % i, [128, NT]) for i in range(2)]
        b_tg = [kb.buf() for _ in range(2)]
        NSG = 4
        sg = [sb(stA, "sg%d" % i, [128, NT]) for i in range(NSG)]
        b_sg = [kb.buf() for _ in range(NSG)]
        NTS = KN('K_NTS', 2)
        tA = [sb(stA, "tA%d" % i, [128, NT]) for i in range(NTS)]
        tB = [sb(stA, "tB%d" % i, [128, NT]) for i in range(NTS)]
        tC = [sb(stA, "tC%d" % i, [128, NT]) for i in range(NTS)]
        tH = [sb(stA, "tH%d" % i, [128, NT]) for i in range(NTS)]
        b_tA = [kb.buf() for _ in range(NTS)]
        b_tB = [kb.buf() for _ in range(NTS)]
        b_tC = [kb.buf() for _ in range(NTS)]
        b_tH = [kb.buf() for _ in range(NTS)]
        NHG = KN('K_NHG', 1)
        hg_l = [sb(stA, "hg%d" % i, [128, NCH, NT], BF16) for i in range(NHG)]
        b_hg_l = [[kb.buf() for _ in range(NCH)] for _ in range(NHG)]
        NXB = KN('K_NXB', 2)
        xb = [sb(stA, "xb%d" % i, [128, D]) for i in range(NXB)]
        b_xb = [kb.buf() for _ in range(NXB)]
        s_xb = [kb.dsem("xb%d" % i) for i in range(NXB)]
        s_xo = [kb.dsem("xo%d" % i) for i in range(3)]
        NXR = KN('K_NXR', 3)
        xr = [sb(stA, "xr%d" % i, [128, D]) for i in range(NXR)]
        b_xr = [kb.buf() for _ in range(NXR)]
        s_xr = [kb.dsem("xr%d" % i) for i in range(NXR)]
        xnb = [sb(stA, "xnb%d" % i, [128, D], BF16) for i in range(2)]
        b_xnb = [kb.buf() for _ in range(2)]
        NXT = KN('K_NXT', 1)
        xnT_l = [sb(stA, "xnT%d" % i, [128, KC, NT], BF16) for i in range(NXT)]
        b_xnT_l = [[kb.buf() for _ in range(4)] for _ in range(NXT)]
        junk = sb(stA, "junk", [128, D], BF16)
        b_junk = kb.buf()
        junk2 = sb(stA, "junk2", [128, 2, 512], BF16)
        b_junk2 = [kb.buf() for _ in range(2)]
        yn_l = [sb(stA, "yn%d" % i, [128, D]) for i in range(2)]
        b_yn_l = [kb.buf() for _ in range(2)]
        ss = sb(stA, "ss", [128, 8])
        b_ss = [kb.buf() for _ in range(4)]
        ss2 = sb(stA, "ss2", [128, 8])
        b_ss2 = [kb.buf() for _ in range(2)]
        s_xb_A, s_xo_A, s_stg_A, s_const_A, s_xr_A = s_xb, s_xo, s_stg, s_const, s_xr
        s_small_ld = kb.dsem("small_ld")
        s_small_st = kb.dsem("small_st")

        psA = contextlib.ExitStack()
        ub = [ps(psA, "ub%d" % i, [128, NT]) for i in range(2)]
        b_ub = [kb.buf() for _ in range(2)]
        ug_l = [ps(psA, "ug%d" % i, [128, NT]) for i in range(2)]
        b_ug_l = [kb.buf() for _ in range(2)]
        pr = ps(psA, "pr", [128, NT])
        pi_ = ps(psA, "pi", [128, NT])
        b_pr, b_pi = kb.buf(), kb.buf()
        tp = [pr[:, :].bitcast(BF16).rearrange("p (k t) -> p k t", k=8)] * 2
        b_tp = [b_pr] * 2
        py = [ps(psA, "py%d" % i, [128, NT]) for i in range(2)]
        b_py = [kb.buf() for _ in range(2)]

        cnt = {"xb": 0, "xnb": 0, "tp": 0, "ub": 0, "cv": 0, "tg": 0, "sg": 0, "t": 0, "ss": 0, "ss2": 0, "tile": 0, "xr": 0, "yn": 0}

        def rstd_from_ss(col_ap, rd, wr_buf, out_ap):
            op(ACT, lambda e: e.activation(out=out_ap, in_=col_ap, func=AF.Sqrt, bias=epsb[0:col_ap.shape[0], :], scale=1.0 / D),
               reads=list(rd) + [b_eps], writes=[wr_buf], ts="q")
            op(DVE, lambda e: e.reciprocal(out=out_ap, in_=out_ap), reads=[wr_buf], writes=[wr_buf])

        def load_norm_transpose(src, row0, nrows, si, xnT_t, b_xnT_s, col0):
            P = nrows
            i = cnt["xb"] % NXB
            cnt["xb"] += 1
            op(SP, lambda e: e.dma_start(out=xb[i][0:P, :], in_=src[row0:row0 + P, :]), writes=[b_xb[i]], dsem=s_xb[i])
            k = cnt["ss"] % 4
            cnt["ss"] += 1
            op(ACT, lambda e: e.activation(out=junk[0:P, :], in_=xb[i][0:P, :], func=AF.Square,
                                           accum_out=ss[0:P, 2 * k:2 * k + 1]),
               reads=[b_xb[i]], writes=[b_junk, b_ss[k]])
            rstd_from_ss(ss[0:P, 2 * k:2 * k + 1], [b_ss[k]], b_ss[k], ss[0:P, 2 * k + 1:2 * k + 2])
            n = cnt["xnb"] % 2
            cnt["xnb"] += 1
            op(DVE, lambda e: e.tensor_scalar(out=xnb[n][0:P, :], in0=xb[i][0:P, :], scalar1=ss[0:P, 2 * k + 1:2 * k + 2],
                                              scalar2=None, op0=ALU.mult),
               reads=[b_xb[i], b_ss[k]], writes=[b_xnb[n]])
            q = cnt["tp"] % 2
            cnt["tp"] += 1

            def tr(e, q=q, n=n):
                ins = None
                for kc in range(KC):
                    ins = e.transpose(out=tp[q][:, kc, 0:P], in_=xnb[n][0:P, kc * 128:(kc + 1) * 128],
                                      identity=ident_b[0:P, 0:P])
                return ins
            op(PE, tr, reads=[b_xnb[n], b_ident_b], writes=[b_tp[q]], c=0.7)
            op(ACT, lambda e, q=q: e.activation(out=xnT_t[:, :, col0:col0 + P], in_=tp[q][:, :, 0:P], func=AF.Copy),
               reads=[b_tp[q]], writes=[b_xnT_s])

        def phaseA_tile(kind, seqs, L, row0, tidx):
            ns = len(seqs)
            N = ns * L
            xi_ = cnt["tile"] % NXT
            cnt["tile"] += 1
            xnT = xnT_l[xi_]
            b_xnT = b_xnT_l[xi_]
            hg = hg_l[(cnt["tile"] - 1) % NHG]
            b_hg = b_hg_l[(cnt["tile"] - 1) % NHG]
            src = xp if kind == "p" else xs
            dst1 = x1p if kind == "p" else x1s
            nsub = (N + 127) // 128
            if kind == "p":
                sq0 = seqs[0]
                for j in range(NCH):
                    if tidx == 0:
                        op(POOL, lambda e, j=j: e.memset(brs[:, j, 0:3], 0.0), writes=[b_brs[j]], n=3)
                        op(POOL, lambda e, j=j, sq0=sq0: e.memset(hstp[:, sq0, j:j + 1], 0.0), writes=[b_hstp[sq0][j]], n=1)
                    else:
                        op(POOL, lambda e, j=j, sq0=sq0: e.tensor_copy(out=brs[:, j, 0:3], in_=hsv[:, sq0, j, :]),
                           reads=[b_hsv[sq0][j]], writes=[b_brs[j]], n=3)
            if kind == "s":
                m_ld = kb.mark()
                for j in range(NCH):
                    bv = brs[:, j, 0:ns * (3 + L)].rearrange("p (s l) -> p s l", s=ns)
                    for s_ in range(ns):
                        op(SP, lambda e, j=j, s_=s_, bv=bv: e.dma_start(
                            out=bv[:, s_, 0:3],
                            in_=st_conv[s_, :, j * 128:(j + 1) * 128].rearrange("r p -> p r")),
                           after=[b_brs[j]], dsem=s_small_ld, n=8)
                    op(SP, lambda e, j=j: e.dma_start(
                        out=hst[:, :, j], in_=st_rnn[:, j * 128:(j + 1) * 128].rearrange("s p -> p s")),
                       after=[b_hst[j]], dsem=s_small_ld, n=8)
                lds = [o_ for o_ in kb.since(m_ld) if o_.dsem is s_small_ld]
                for j in range(NCH):
                    b_brs[j].w = list(lds)
                    b_brs[j].r = []
                    b_hst[j].w = list(lds)
                    b_hst[j].r = []
            for s_ in range(nsub):
                P = min(128, N - 128 * s_)
                load_norm_transpose(src, row0 + 128 * s_, P, s_, xnT, b_xnT[s_], 128 * s_)

            def brv(j, lo, hi):
                return brs[:, j, 0:ns * (3 + L)].rearrange("p (s l) -> p s l", s=ns)[:, :, lo:hi]

            def v3(ap):
                return ap[:, 0:N].rearrange("p (s l) -> p s l", s=ns)

            slot_cv = {}
            slot_sg = {}

            def gates_chain(j):
                iis = [i for i in (j - 1, j, j + 1) if 0 <= i < NCH and blocks_overlap(i, j)]

                def mmr(e, w=wgr, pt=pr):
                    ins = None
                    for n_, i in enumerate(iis):
                        ins = e.matmul(out=pt[:, 0:N], lhsT=w[:, i, j - i + 1, :], rhs=cvb[slot_cv[i]][:, 0:N],
                                       start=(n_ == 0), stop=(n_ == len(iis) - 1))
                    return ins
                op(PE, mmr, reads=[b_wgr] + [b_cvb[slot_cv[i]] for i in iis], writes=[b_pr], c=0.75)
                op(PE, lambda e: mmr(e, wgi, pi_), reads=[b_wgi] + [b_cvb[slot_cv[i]] for i in iis], writes=[b_pi], c=0.75)
                k = cnt["t"] % NTS
                cnt["t"] += 1
                op(ACT, lambda e: e.activation(out=tA[k][:, 0:N], in_=pr[:, 0:N], func=AF.Tanh,
                                               bias=hbr[:, j:j + 1], scale=0.5),
                   reads=[b_pr, b_hbr], writes=[b_tA[k]], ts="e")
                op(ACT, lambda e: e.activation(out=tC[k][:, 0:N], in_=pi_[:, 0:N], func=AF.Tanh,
                                               bias=hbi[:, j:j + 1], scale=0.5),
                   reads=[b_pi, b_hbi], writes=[b_tC[k]], ts="e")
                op(ACT, lambda e: e.activation(out=tB[k][:, 0:N], in_=tA[k][:, 0:N], func=AF.Exp,
                                               bias=hc[:, j:j + 1], scale=hc[:, j:j + 1]),
                   reads=[b_tA[k], b_hc], writes=[b_tB[k]], ts="e")
                if KN('K_A2', 0) == 0:
                    op(POOL, lambda e: e.tensor_tensor(out=tA[k][:, 0:N], in0=tB[k][:, 0:N], in1=tB[k][:, 0:N], op=ALU.mult),
                       reads=[b_tB[k]], writes=[b_tA[k]])
                else:
                    op(ACT, lambda e: e.activation(out=tA[k][:, 0:N], in_=tB[k][:, 0:N], func=AF.Square),
                       reads=[b_tB[k]], writes=[b_tA[k]])
                op(ACT, lambda e: e.activation(out=tA[k][:, 0:N], in_=tA[k][:, 0:N], func=AF.Sqrt, bias=q25[:, 0:1], scale=-0.25),
                   reads=[b_tA[k], b_q25], writes=[b_tA[k]], ts="q")
                c_ = slot_cv[j]
                op(DVE, lambda e: e.scalar_tensor_tensor(out=tC[k][:, 0:N], in0=tC[k][:, 0:N], scalar=1.0,
                                                          in1=cv[c_][:, 0:N], op0=ALU.add, op1=ALU.mult),
                   reads=[b_tC[k], b_cv[c_]], writes=[b_tC[k]])
                op(POOL if KN('K_BE', 0) == 0 else DVE, lambda e: e.tensor_tensor(out=tC[k][:, 0:N], in0=tA[k][:, 0:N], in1=tC[k][:, 0:N], op=ALU.mult),
                   reads=[b_tA[k], b_tC[k]], writes=[b_tC[k]])
                for s_ in range(ns):
                    if kind == "s":
                        hcol = hst[:, seqs[s_], j:j + 1]
                        b_hcol = b_hst[j]
                    else:
                        hcol = hstp[:, seqs[0], j:j + 1]
                        b_hcol = b_hstp[seqs[0]][j]
                    op(DVE, lambda e, s_=s_, hcol=hcol: e.tensor_tensor_scan(
                        out=tH[k][:, s_ * L:(s_ + 1) * L], data0=tB[k][:, s_ * L:(s_ + 1) * L],
                        data1=tC[k][:, s_ * L:(s_ + 1) * L], initial=hcol, op0=ALU.mult, op1=ALU.add),
                       reads=[b_tB[k], b_tC[k], b_hcol], writes=[b_tH[k]], c=1.6 * L / 512 + 0.2)
                    op(POOL, lambda e, s_=s_, hcol=hcol: e.tensor_copy(out=hcol, in_=tH[k][:, (s_ + 1) * L - 1:(s_ + 1) * L]),
                       reads=[b_tH[k]], writes=[b_hcol], n=1)
                g_ = slot_sg[j]
                op(POOL if KN('K_HG', 0) == 0 else DVE, lambda e: e.tensor_tensor(out=hg[:, j, 0:N], in0=tH[k][:, 0:N], in1=sg[g_][:, 0:N], op=ALU.mult),
                   reads=[b_tH[k], b_sg[g_]], writes=[b_hg[j]])

            for j in range(NCH):
                u = cnt["ub"] % 2
                cnt["ub"] += 1
                ug, b_ug = ug_l[u], b_ug_l[u]

                def mm_in(e, col, pt):
                    ins = None
                    for kc in range(KC):
                        ins = e.matmul(out=pt[:, 0:N], lhsT=wia[:, kc, col:col + 128], rhs=xnT[:, kc, 0:N],
                                       start=(kc == 0), stop=(kc == KC - 1))
                    return ins
                op(PE, lambda e, j=j, u=u: mm_in(e, j * 128, ub[u]), reads=b_wia + b_xnT[0:nsub], writes=[b_ub[u]], c=2.0)
                op(PE, lambda e, j=j, ug=ug: mm_in(e, DR + j * 128, ug), reads=b_wia + b_xnT[0:nsub], writes=[b_ug], c=2.0)
                op(ACT, lambda e, j=j, u=u: e.activation(out=brv(j, 3, 3 + L), in_=v3(ub[u]), func=AF.Copy),
                   reads=[b_ub[u]], writes=[b_brs[j]])
                c_ = cnt["cv"] % NCV
                cnt["cv"] += 1
                slot_cv[j] = c_
                op(ACT, lambda e, j=j, c_=c_, u=u: e.activation(out=v3(cv[c_]), in_=v3(ub[u]), func=AF.Identity,
                                                                 scale=cw_t[:, 3, j:j + 1], bias=cb_t[:, j:j + 1]),
                   reads=[b_ub[u], b_cw, b_cb], writes=[b_cv[c_]])
                for tap in (2, 1, 0):
                    ce = DVE
                    op(ce, lambda e, j=j, c_=c_, tap=tap: e.scalar_tensor_tensor(
                        out=v3(cv[c_]), in0=brv(j, tap, tap + L), scalar=cw_t[:, tap, j:j + 1], in1=v3(cv[c_]),
                        op0=ALU.mult, op1=ALU.add),
                       reads=[b_brs[j], b_cw, b_cv[c_]], writes=[b_cv[c_]], c=0.9)
                op(ACT, lambda e, c_=c_: e.activation(out=cvb[c_][:, 0:N], in_=cv[c_][:, 0:N], func=AF.Copy),
                   reads=[b_cv[c_]], writes=[b_cvb[c_]])
                if kind == "p":
                    op(POOL, lambda e, j=j, sq0=seqs[0]: e.tensor_copy(out=hsv[:, sq0, j, :], in_=brs[:, j, L:L + 3]),
                       reads=[b_brs[j]], writes=[b_hsv[seqs[0]][j]], n=3)
                t_ = cnt["tg"] % 2
                cnt["tg"] += 1
                op(ACT, lambda e, t_=t_, ug=ug: e.activation(out=tg[t_][:, 0:N], in_=ug[:, 0:N], func=AF.Tanh, scale=0.5),
                   reads=[b_ug], writes=[b_tg[t_]], ts="e")
                g_ = cnt["sg"] % NSG
                cnt["sg"] += 1
                slot_sg[j] = g_
                op(DVE, lambda e, t_=t_, g_=g_, ug=ug: e.scalar_tensor_tensor(out=sg[g_][:, 0:N], in0=tg[t_][:, 0:N], scalar=1.0,
                                                                       in1=ug[:, 0:N], op0=ALU.add, op1=ALU.mult),
                   reads=[b_tg[t_], b_ug], writes=[b_sg[g_]], c=1.0)
                if j >= 1:
                    gates_chain(j - 1)
            gates_chain(NCH - 1)

            last = (kind == "s") or (tidx == SEQ // 512 - 1)
            if last:
                m_st = kb.mark()
                for s_ in range(ns):
                    oc = (p_conv if kind == "p" else s_conv)[seqs[s_]]
                    orr = (p_rnn if kind == "p" else s_rnn)[seqs[s_]]
                    for j in range(NCH):
                        if kind == "p":
                            srcv = hsv[:, seqs[0], j, :]
                            rb = b_hsv[seqs[0]][j]
                        else:
                            srcv = brv(j, L, L + 3)[:, s_, :]
                            rb = b_brs[j]
                        op(SP, lambda e, j=j, oc=oc, srcv=srcv: e.dma_start(
                            out=oc[:, j * 128:(j + 1) * 128].rearrange("r p -> p r"), in_=srcv),
                           reads=[rb], dsem=s_small_st, n=8)
                    if kind == "s":
                        op(SP, lambda e, orr=orr, hs=seqs[s_]: e.dma_start(out=orr.rearrange("(c p) -> p c", p=128), in_=hst[:, hs, :]),
                           reads=b_hst, dsem=s_small_st, n=16)
                    else:
                        op(SP, lambda e, orr=orr, hs=seqs[0]: e.dma_start(out=orr.rearrange("(c p) -> p c", p=128), in_=hstp[:, hs, :]),
                           reads=b_hstp[seqs[0]], dsem=s_small_st, n=16)
                sts = [o_ for o_ in kb.since(m_st) if o_.dsem is s_small_st]
                for j in range(NCH):
                    b_brs[j].r = b_brs[j].r + sts
                    b_hst[j].r = b_hst[j].r + sts

            for s_ in range(nsub):
                P = min(128, N - 128 * s_)
                for h in range(2):
                    def mm_out(e, h=h, s_=s_, P=P):
                        ins = None
                        for j in range(NCH):
                            ins = e.matmul(out=py[h][0:P, :], lhsT=hg[:, j, 128 * s_:128 * s_ + P],
                                           rhs=woa[:, j, h * 512:(h + 1) * 512], start=(j == 0), stop=(j == NCH - 1))
                        return ins
                    op(PE, mm_out, reads=b_hg + b_woa, writes=[b_py[h]], c=2.8)
                k = cnt["ss2"] % 2
                cnt["ss2"] += 1
                yi_ = cnt["yn"] % 2
                cnt["yn"] += 1
                yn, b_yn = yn_l[yi_], b_yn_l[yi_]
                for h in range(2):
                    op(ACT, lambda e, h=h, k=k, P=P: e.activation(out=junk2[0:P, h, :], in_=py[h][0:P, :], func=AF.Square,
                                                                  accum_out=ss2[0:P, 4 * k + h:4 * k + h + 1]),
                       reads=[b_py[h]], writes=[b_junk2[h], b_ss2[k]])
                    op(DVE, lambda e, h=h, P=P, yn=yn: e.tensor_copy(out=yn[0:P, h * 512:(h + 1) * 512], in_=py[h][0:P, :]),
                       reads=[b_py[h], b_junk2[h]], writes=[b_yn], c=0.7)
                op(DVE, lambda e, k=k, P=P: e.tensor_tensor(out=ss2[0:P, 4 * k + 2:4 * k + 3], in0=ss2[0:P, 4 * k:4 * k + 1],
                                                            in1=ss2[0:P, 4 * k + 1:4 * k + 2], op=ALU.add),
                   reads=[b_ss2[k]], writes=[b_ss2[k]])
                rstd_from_ss(ss2[0:P, 4 * k + 2:4 * k + 3], [b_ss2[k]], b_ss2[k], ss2[0:P, 4 * k + 3:4 * k + 4])
                for h in range(2):
                    op(DVE, lambda e, h=h, k=k, P=P, yn=yn: e.scalar_tensor_tensor(
                        out=yn[0:P, h * 512:(h + 1) * 512], in0=yn[0:P, h * 512:(h + 1) * 512], scalar=ss2[0:P, 4 * k + 3:4 * k + 4],
                        in1=gpost_a[0:P, h * 512:(h + 1) * 512], op0=ALU.mult, op1=ALU.mult),
                       reads=[b_yn, b_ss2[k], b_gpa], writes=[b_yn], c=0.8)
                i = cnt["xr"] % NXR
                cnt["xr"] += 1
                r0 = row0 + 128 * s_
                op(SP, lambda e, i=i, P=P, r0=r0: e.dma_start(out=xr[i][0:P, :], in_=src[r0:r0 + P, :]),
                   writes=[b_xr[i]], dsem=s_xr[i], n=1024)
                op(POOL, lambda e, i=i, P=P, yn=yn: e.tensor_tensor(out=xr[i][0:P, :], in0=xr[i][0:P, :], in1=yn[0:P, :], op=ALU.add),
                   reads=[b_xr[i], b_yn], writes=[b_xr[i]], n=1024)
                bx = kb.buf()
                x1_bufs[(kind, r0)] = bx
                op(SP, lambda e, i=i, P=P, r0=r0: e.dma_start(out=dst1[r0:r0 + P, :], in_=xr[i][0:P, :]),
                   reads=[b_xr[i]], writes=[bx], dsem=s_xo[i], n=1024)

        def blocks_overlap(i, j):
            for blk in range(16):
                lo, hi = 88 * blk, 88 * blk + 88
                if max(lo, 128 * i) < min(hi, 128 * i + 128) and max(lo, 128 * j) < min(hi, 128 * j + 128):
                    return True
            return False


        def phase_b():
            NRING = 10
            NSLOT = NRING
            stB = contextlib.ExitStack()
            psB = contextlib.ExitStack()
            wgb = sb(stB, "wgb", [128, KC, 1024], BF16)
            wob = sb(stB, "wob", [128, KC, 1024], BF16)
            b_wg = [kb.buf() for _ in range(KC)]
            b_wob = [kb.buf() for _ in range(KC)]
            s_cB = s_const_A
            m_cB = kb.mark()
            gpost_b = sb(stB, "gpost_b", [128, D])
            b_gpb = kb.buf()
            op(SP, lambda e: e.dma_start(out=gpost_b[:], in_=norm_post_b.partition_broadcast(128)), writes=[b_gpb], dsem=s_cB)
            cos_sb = sb(stB, "cos_sb", [128, NPOS])
            sin_sb = sb(stB, "sin_sb", [128, NPOS])
            b_cos, b_sin = kb.buf(), kb.buf()
            op(SP, lambda e: e.dma_start(out=cos_sb[:], in_=cos_d), writes=[b_cos], dsem=s_cB)
            op(SP, lambda e: e.dma_start(out=sin_sb[:], in_=sin_d), writes=[b_sin], dsem=s_cB)
            prot_f = sb(stB, "prot_f", [128, 128])
            b_protf = kb.buf()
            op(SP, lambda e: e.dma_start(out=prot_f[:], in_=prot_d), writes=[b_protf], dsem=s_cB)
            skb = sb(stB, "skb", [128, 8])
            b_skb = kb.buf()
            sk2 = sinks.rearrange("(hp two) -> two hp", two=2)
            op(SP, lambda e: e.dma_start(out=skb[0:64, :], in_=sk2[0:1, :].partition_broadcast(64)), writes=[b_skb], dsem=s_cB)
            op(SP, lambda e: e.dma_start(out=skb[64:128, :], in_=sk2[1:2, :].partition_broadcast(64)), dsem=s_cB)
            for b_ in (b_gpb, b_cos, b_sin, b_skb, b_protf):
                b_.w = [o_ for o_ in kb.since(m_cB) if o_.dsem is s_cB]
            prot_b = sb(stB, "prot_b", [128, 128], BF16)
            b_prot = kb.buf()
            op(DVE, lambda e: e.tensor_copy(out=prot_b[:], in_=prot_f[:]), reads=[b_protf], writes=[b_prot])
            esk = sb(stB, "esk", [128, 8])
            b_esk = kb.buf()
            op(ACT, lambda e: e.activation(out=esk[:], in_=skb[:], func=AF.Exp), reads=[b_skb], writes=[b_esk], ts="e")
            ones_bd = sb(stB, "ones_bd", [128, 128], BF16)
            ones_pt = sb(stB, "ones_pt", [128, 128], BF16)
            b_onesbd, b_onespt = kb.buf(), kb.buf()
            op(POOL, lambda e: e.memset(ones_bd[:], 0.0), writes=[b_onesbd])
            op(POOL, lambda e: e.memset(ones_bd[0:64, 0:64], 1.0), writes=[b_onesbd])
            op(POOL, lambda e: e.memset(ones_bd[64:128, 64:128], 1.0), writes=[b_onesbd])
            op(POOL, lambda e: e.memset(ones_pt[:], 0.0), writes=[b_onespt])
            op(POOL, lambda e: e.memset(ones_pt[0:32, 0:64], 1.0), writes=[b_onespt])
            op(POOL, lambda e: e.memset(ones_pt[64:96, 64:128], 1.0), writes=[b_onespt])

            NSB = 2
            stg2 = [sb(stB, "stgb%d" % i, [128, 1024]) for i in range(NSB)]
            b_stg2 = [kb.buf() for _ in range(NSB)]
            s_stg2 = s_stg_A
            ci = 0
            for kc in range(KC):
                k2 = ci % NSB
                ci += 1
                gs = g_b[:, kc:kc + 1]
                st_ = stg2[k2]
                op(SP if ci % 2 else ACT, lambda e, kc=kc, st_=st_: e.dma_start(out=st_[:, 0:1024], in_=w_in_b[kc * 128:(kc + 1) * 128, 1024:2048]),
                   writes=[b_stg2[k2]], dsem=s_stg2[k2], n=1024)
                op(ACT, lambda e, kc=kc, st_=st_, gs=gs: e.activation(out=wgb[:, kc, :], in_=st_[:, 0:1024], func=AF.Identity, scale=gs),
                   reads=[b_stg2[k2], b_g_b], writes=[b_wg[kc]], n=1024)
            for kc in range(KC):
                k2 = ci % NSB
                ci += 1
                st_ = stg2[k2]
                op(SP if ci % 2 else ACT, lambda e, kc=kc, st_=st_: e.dma_start(out=st_[:, 0:1024], in_=w_out_b[kc * 128:(kc + 1) * 128, :]),
                   writes=[b_stg2[k2]], dsem=s_stg2[k2])
                op(DVE,
                   lambda e, kc=kc, st_=st_: e.tensor_scalar(out=wob[:, kc, :], in0=st_[:, 0:1024], scalar1=0.5, scalar2=None, op0=ALU.mult),
                   reads=[b_stg2[k2]], writes=[b_wob[kc]])

            NT = 512
            PADL = 64
            kbd = sb(stB, "kbd", [128, NSLOT, 4, 128], BF16)
            vbd = sb(stB, "vbd", [128, NSLOT, 4, 128], BF16)
            b_kbd = [kb.buf() for _ in range(NSLOT)]
            b_vbd = [kb.buf() for _ in range(NSLOT)]
            op(POOL, lambda e: e.memset(kbd[:], 0.0), writes=b_kbd)
            op(POOL, lambda e: e.memset(vbd[:], 0.0), writes=b_vbd)
            xnT = sb(stB, "xnTb", [128, KC, PADL + NT + 64], BF16)
            b_xnT = [kb.buf() for _ in range(4)]
            op(POOL, lambda e: e.memset(xnT[:], 0.0), writes=b_xnT)
            qT = sb(stB, "qT", [128, KC, NT], BF16)
            b_qT = [kb.buf() for _ in range(KC)]
            ogb = sb(stB, "ogb", [128, KC, NT], BF16)
            b_ogb = [kb.buf() for _ in range(KC)]
            NXB = 2
            xb = [sb(stB, "xbb%d" % i, [128, D]) for i in range(NXB)]
            b_xb = [kb.buf() for _ in range(NXB)]
            s_xb = s_xb_A
            s_xo = s_xo_A
            s_xr = s_xr_A
            NXR = 2
            xr = [sb(stB, "xrb%d" % i, [128, D]) for i in range(NXR)]
            b_xr = [kb.buf() for _ in range(NXR)]
            xnb = [sb(stB, "xnbb%d" % i, [128, D], BF16) for i in range(2)]
            b_xnb = [kb.buf() for _ in range(2)]
            junk = sb(stB, "junkb", [128, D], BF16)
            b_junk = kb.buf()
            junk2 = sb(stB, "junkb2", [128, 2, 512], BF16)
            b_junk2 = [kb.buf() for _ in range(2)]
            yn_l = [sb(stB, "ynb%d" % i, [128, D]) for i in range(2)]
            b_yn_l = [kb.buf() for _ in range(2)]
            ss = sb(stB, "ssb", [128, 8])
            b_ss = [kb.buf() for _ in range(4)]
            ss2 = sb(stB, "ss2b", [128, 8])
            b_ss2 = [kb.buf() for _ in range(2)]
            zb_l = [sb(stB, "zb%d" % i, [128, NT], BF16) for i in range(2)]
            b_zb_l = [kb.buf() for _ in range(2)]
            wtmp = wbigv[:, 0:8192].bitcast(F32).rearrange("p (k n) -> p k n", k=8)
            t1_l = [wtmp[:, i, :] for i in range(2)]
            t2_l = [wtmp[:, 2 + i, :] for i in range(2)]
            b_t1_l = [kb.buf() for _ in range(2)]
            b_t2_l = [kb.buf() for _ in range(2)]
            kf = [sb(stB, "kf%d" % i, [128, NT]) for i in range(2)]
            b_kf = [kb.buf() for _ in range(2)]
            NPT = 3
            pT = [sb(stB, "pT%d" % i, [128, 384], BF16) for i in range(NPT)]
            b_pT = [kb.buf() for _ in range(NPT)]
            tg_l = [sb(stB, "tgb%d" % i, [128, NT]) for i in range(2)]
            sg_l = [sb(stB, "sgb%d" % i, [128, NT]) for i in range(2)]
            b_tg_l = [kb.buf() for _ in range(2)]
            b_sg_l = [kb.buf() for _ in range(2)]
            rden = [wtmp[:, 4 + i, :] for i in range(2)]
            oraw = [wtmp[:, 6 + i, :] for i in range(2)]
            b_rden = [kb.buf() for _ in range(2)]
            b_oraw = [kb.buf() for _ in range(2)]
            vout = sb(stB, "vout", [128, 256])
            kout = sb(stB, "kout", [128, 256])
            b_vout, b_kout = kb.buf(), kb.buf()
            s_vo, s_ko = kb.dsem("vo"), kb.dsem("ko")
            ck = sb(stB, "ck", [128, 256])
            ckb = sb(stB, "ckb", [128, 2, 256], BF16)
            cvt = sb(stB, "cvt", [128, 2, 256])
            b_ck, b_ckb, b_cvt = kb.buf(), kb.buf(), kb.buf()
            s_ck, s_cv = kb.dsem("ckl"), kb.dsem("cvl")

            pz_l = [ps(psB, "pz%d" % i, [128, NT]) for i in range(2)]
            b_pz_l = [kb.buf() for _ in range(2)]
            pz, b_pz = pz_l[0], b_pz_l[0]
            pzr = ps(psB, "pzr", [128, NT])
            b_pzr = kb.buf()
            tpb = pzr[:, :].bitcast(BF16).rearrange("p (k t) -> p k t", k=8)
            b_tpb = b_pzr
            pS_l = [ps(psB, "pS0", [128, NT])] * 2
            b_pS_l = [kb.buf()] * 2
            pO = [ps(psB, "pO%d" % i, [128, NT]) for i in range(2)]
            pD = [ps(psB, "pD%d" % i, [128, NT]) for i in range(2)]
            b_pO = [kb.buf() for _ in range(2)]
            b_pD = [kb.buf() for _ in range(2)]
            cntb = {"xb": 0, "xnb": 0, "ss": 0, "ss2": 0, "pT": 0, "xr": 0, "pS": 0, "pz": 0, "rp": 0, "gt": 0, "yn": 0}

            def lnt(src, row0, P, b_dst, col0, extra):
                i = cntb["xb"] % NXB
                cntb["xb"] += 1
                op(SP, lambda e: e.dma_start(out=xb[i][0:P, :], in_=src[row0:row0 + P, :]), reads=extra, writes=[b_xb[i]], dsem=s_xb[i])
                k = cntb["ss"] % 4
                cntb["ss"] += 1
                op(ACT, lambda e: e.activation(out=junk[0:P, :], in_=xb[i][0:P, :], func=AF.Square, accum_out=ss[0:P, 2 * k:2 * k + 1]),
                   reads=[b_xb[i]], writes=[b_junk, b_ss[k]])
                rstd_from_ss(ss[0:P, 2 * k:2 * k + 1], [b_ss[k]], b_ss[k], ss[0:P, 2 * k + 1:2 * k + 2])
                n = cntb["xnb"] % 2
                cntb["xnb"] += 1
                op(DVE, lambda e: e.tensor_scalar(out=xnb[n][0:P, :], in0=xb[i][0:P, :], scalar1=ss[0:P, 2 * k + 1:2 * k + 2], scalar2=None, op0=ALU.mult),
                   reads=[b_xb[i], b_ss[k]], writes=[b_xnb[n]])

                def tr(e):
                    ins = None
                    for kc in range(KC):
                        ins = e.transpose(out=tpb[:, kc, 0:P], in_=xnb[n][0:P, kc * 128:(kc + 1) * 128], identity=ident_b[0:P, 0:P])
                    return ins
                op(PE, tr, reads=[b_xnb[n], b_ident_b], writes=[b_tpb], c=0.7)
                op(ACT, lambda e: e.activation(out=xnT[:, :, col0:col0 + P], in_=tpb[:, :, 0:P], func=AF.Copy), reads=[b_tpb], writes=[b_dst])

            def slot_runs(base, c0, n):
                runs = []
                i = 0
                while i < n:
                    s0 = (c0 + i) % NRING
                    cntr = min(n - i, NRING - s0)
                    runs.append((base + s0, cntr, i))
                    i += cntr
                return runs

            def phaseB_tile(kind, seqs, L, row0, tidx):
                ns = len(seqs)
                N = ns * L
                src1 = x1p if kind == "p" else x1s
                dsty = yp if kind == "p" else ys
                nsub = (N + 127) // 128
                pos0 = 512 * tidx if kind == "p" else SEQ
                cosv = cos_sb[:, pos0:pos0 + N]
                sinv = sin_sb[:, pos0:pos0 + N]
                last = (kind == "s") or (tidx == SEQ // 512 - 1)
                for s_ in range(nsub):
                    P = min(128, N - 128 * s_)
                    r0 = row0 + 128 * s_
                    lnt(src1, r0, P, b_xnT[s_], PADL + 128 * s_, [x1_bufs[(kind, r0)]])
                xall = b_xnT[0:nsub]

                def proj(e, w, col, pt):
                    ins = None
                    for kc in range(KC):
                        ins = e.matmul(out=pt[:, 0:N], lhsT=w[:, kc, col:col + 128], rhs=xnT[:, kc, PADL:PADL + N],
                                       start=(kc == 0), stop=(kc == KC - 1))
                    return ins

                def proj_rope(w, col, rd):
                    i_ = cntb["pz"] % 2
                    cntb["pz"] += 1
                    pzx, b_pzx = pz_l[i_], b_pz_l[i_]
                    r_ = cntb["rp"] % 2
                    cntb["rp"] += 1
                    zb, b_zb = zb_l[r_], b_zb_l[r_]
                    t1, t2, b_t1, b_t2 = t1_l[r_], t2_l[r_], b_t1_l[r_], b_t2_l[r_]
                    op(PE, lambda e: proj(e, w, col, pzx), reads=rd + xall, writes=[b_pzx], c=2.0)
                    op(ACT, lambda e: e.activation(out=zb[:, 0:N], in_=pzx[:, 0:N], func=AF.Copy), reads=[b_pzx], writes=[b_zb])
                    op(PE, lambda e: e.matmul(out=pzr[:, 0:N], lhsT=prot_b[:, :], rhs=zb[:, 0:N], start=True, stop=True),
                       reads=[b_zb, b_prot], writes=[b_pzr], c=0.3)
                    op(DVE, lambda e: e.tensor_tensor(out=t1[:, 0:N], in0=pzx[:, 0:N], in1=cosv, op=ALU.mult), reads=[b_pzx, b_cos, b_zb], writes=[b_t1])
                    op(DVE, lambda e: e.tensor_tensor(out=t2[:, 0:N], in0=pzr[:, 0:N], in1=sinv, op=ALU.mult), reads=[b_pzr, b_sin], writes=[b_t2])
                    return t1, t2, b_t1, b_t2

                if kind == "p":
                    c_first = 8 * tidx
                    nchk = 8
                    sbase = 0
                    key_slots = None
                else:
                    c_first = 0
                    nchk = 0

                if kind == "s":
                    for s_ in range(ns):
                        sl = 3 * s_ + 2
                        op(POOL, lambda e, sl=sl: e.memset(kbd[:, sl, :, :], 0.0), writes=[b_kbd[sl]])
                        op(POOL, lambda e, sl=sl: e.memset(vbd[:, sl, :, :], 0.0), writes=[b_vbd[sl]])
                for kc2 in range(4):
                    t1, t2, b_t1, b_t2 = proj_rope(wk, kc2 * 128, b_wk)
                    if kc2 < 2:
                        gt, gb = 2 * kc2, 2 * kc2 + 1
                        kfi = kf[kc2]
                        op(POOL, lambda e, kfi=kfi, t1=t1, t2=t2: e.tensor_tensor(out=kfi[:, 0:N], in0=t1[:, 0:N], in1=t2[:, 0:N], op=ALU.add),
                           reads=[b_t1, b_t2], writes=[b_kf[kc2]])
                        srcs = (kfi, None)
                    else:
                        gt, gb = 2 * (kc2 - 2) + 1, 2 * (kc2 - 2)
                        srcs = (t1, t2)
                    for (plo, gg, clo) in ((0, gt, 0), (64, gb, 64)):
                        if kind == "p":
                            for (s0, cn, off) in slot_runs(sbase, c_first, nchk):
                                outv = kbd[plo:plo + 64, s0:s0 + cn, gg, clo:clo + 64]
                                if srcs[1] is None:
                                    inv = srcs[0][plo:plo + 64, off * 64:(off + cn) * 64].rearrange("p (c k) -> p c k", k=64)
                                    op(POOL, lambda e, outv=outv, inv=inv: e.tensor_copy(out=outv, in_=inv),
                                       reads=[b_kf[kc2]], writes=[b_kbd[s0 + i_] for i_ in range(cn)])
                                else:
                                    in0 = t1[plo:plo + 64, off * 64:(off + cn) * 64].rearrange("p (c k) -> p c k", k=64)
                                    in1 = t2[plo:plo + 64, off * 64:(off + cn) * 64].rearrange("p (c k) -> p c k", k=64)
                                    op(POOL, lambda e, outv=outv, in0=in0, in1=in1: e.tensor_tensor(out=outv, in0=in0, in1=in1, op=ALU.add),
                                       reads=[b_t1, b_t2], writes=[b_kbd[s0 + i_] for i_ in range(cn)])
                        else:
                            for s_ in range(ns):
                                sl = 3 * s_ + 2
                                outv = kbd[plo:plo + 64, sl, gg, clo:clo + L]
                                if srcs[1] is None:
                                    op(POOL, lambda e, outv=outv, s_=s_, plo=plo, kfi=srcs[0]: e.tensor_copy(out=outv, in_=kfi[plo:plo + 64, s_ * L:(s_ + 1) * L]),
                                       reads=[b_kf[kc2]], writes=[b_kbd[sl]])
                                else:
                                    op(POOL, lambda e, outv=outv, s_=s_, plo=plo, t1=t1, t2=t2: e.tensor_tensor(out=outv, in0=t1[plo:plo + 64, s_ * L:(s_ + 1) * L],
                                                                                                  in1=t2[plo:plo + 64, s_ * L:(s_ + 1) * L], op=ALU.add),
                                       reads=[b_t1, b_t2], writes=[b_kbd[sl]])
                if last:
                    for s_ in range(ns):
                        Pk = 128 if kind == "p" else L
                        c0 = N - 128 if kind == "p" else s_ * L

                        def trk(e, c0=c0, Pk=Pk):
                            ins = None
                            for kc2 in range(2):
                                ins = e.transpose(out=pz[0:Pk, kc2 * 128:(kc2 + 1) * 128], in_=kf[kc2][:, c0:c0 + Pk], identity=ident_f[:, :])
                            return ins
                        op(PE, trk, reads=b_kf + [b_ident_f], writes=[b_pz])
                        op(ACT, lambda e, Pk=Pk: e.activation(out=kout[0:Pk, :], in_=pz[0:Pk, 0:256], func=AF.Copy), reads=[b_pz], writes=[b_kout])
                        dk = (p_k if kind == "p" else s_k)[seqs[s_]]
                        op(SP, lambda e, dk=dk, Pk=Pk: e.dma_start(out=dk, in_=kout[0:Pk, :]), reads=[b_kout], dsem=s_ko)

                def vproj(e, c0, M, pt):
                    ins = None
                    for kc in range(KC):
                        ins = e.matmul(out=pt[0:M, 0:256], lhsT=xnT[:, kc, c0:c0 + M], rhs=wv[:, kc, :], start=(kc == 0), stop=(kc == KC - 1))
                    return ins
                if kind == "p":
                    for s_ in range(4):
                        op(PE, lambda e, s_=s_: vproj(e, PADL + 128 * s_, 128, pz), reads=b_wk + xall, writes=[b_pz], c=1.2)
                        ce, co = c_first + 2 * s_, c_first + 2 * s_ + 1
                        se, so = sbase + ce % NRING, sbase + co % NRING
                        pzv = pz[:, 0:256].rearrange("p (g d) -> p g d", g=4)
                        op(ACT, lambda e, se=se, pzv=pzv: e.activation(out=vbd[0:64, se, :, 0:64], in_=pzv[0:64], func=AF.Copy), reads=[b_pz], writes=[b_vbd[se]])
                        op(ACT, lambda e, so=so, pzv=pzv: e.activation(out=vbd[64:128, so, :, 64:128], in_=pzv[64:128], func=AF.Copy), reads=[b_pz], writes=[b_vbd[so]])
                        if last and s_ == 3:
                            op(ACT, lambda e: e.activation(out=vout[:, :], in_=pz[:, 0:256], func=AF.Copy), reads=[b_pz], writes=[b_vout])
                            op(SP, lambda e: e.dma_start(out=p_v[seqs[0]], in_=vout[:, :]), reads=[b_vout], dsem=s_vo)
                    for s_ in range(5):
                        op(PE, lambda e, s_=s_: vproj(e, 128 * s_, 128, pzr), reads=b_wk + xall, writes=[b_pzr], c=1.2)
                        pzv = pzr[:, 0:256].rearrange("p (g d) -> p g d", g=4)
                        if s_ >= 1:
                            so = sbase + (c_first + 2 * s_ - 1) % NRING
                            op(ACT, lambda e, so=so, pzv=pzv: e.activation(out=vbd[0:64, so, :, 0:64], in_=pzv[0:64], func=AF.Copy), reads=[b_pzr], writes=[b_vbd[so]])
                        if s_ <= 3:
                            se = sbase + (c_first + 2 * s_) % NRING
                            op(ACT, lambda e, se=se, pzv=pzv: e.activation(out=vbd[64:128, se, :, 64:128], in_=pzv[64:128], func=AF.Copy), reads=[b_pzr], writes=[b_vbd[se]])
                else:
                    for s_ in range(ns):
                        sl = 3 * s_ + 2
                        op(PE, lambda e, s_=s_: vproj(e, PADL + L * s_, L, pz), reads=b_wk + xall, writes=[b_pz])
                        pzv = pz[:, 0:256].rearrange("p (g d) -> p g d", g=4)
                        op(ACT, lambda e, sl=sl, pzv=pzv: e.activation(out=vbd[0:L, sl, :, 0:64], in_=pzv[0:L], func=AF.Copy), reads=[b_pz], writes=[b_vbd[sl]])
                        op(ACT, lambda e: e.activation(out=vout[0:L, :], in_=pz[0:L, 0:256], func=AF.Copy), reads=[b_pz], writes=[b_vout])
                        op(SP, lambda e, s_=s_: e.dma_start(out=s_v[seqs[s_]], in_=vout[0:L, :]), reads=[b_vout], dsem=s_vo)
                        op(PE, lambda e, s_=s_: vproj(e, PADL + L * s_ - 64, 64 + L, pzr), reads=b_wk + xall, writes=[b_pzr])
                        pzv2 = pzr[:, 0:256].rearrange("p (g d) -> p g d", g=4)
                        op(ACT, lambda e, sl=sl, pzv2=pzv2: e.activation(out=vbd[64:64 + L, sl, :, 64:128], in_=pzv2[64:64 + L], func=AF.Copy), reads=[b_pzr], writes=[b_vbd[sl]])
                        sq = seqs[s_]
                        op(SP, lambda e, sq=sq: e.dma_start(out=ck[:, :], in_=cache_k[sq]), writes=[b_ck], dsem=s_ck)
                        ckv = ck[:, :].rearrange("p (a b d) -> p a b d", a=2, b=2)
                        op(DVE, lambda e: e.tensor_copy(out=ckb[:, 0, :], in_=ck[:, :]), reads=[b_ck], writes=[b_ckb])
                        cks = ckb[:, 1, :].rearrange("p (a b d) -> p a b d", a=2, b=2)
                        for b2 in range(2):
                            op(DVE, lambda e, b2=b2, cks=cks, ckv=ckv: e.tensor_copy(out=cks[:, :, b2, :], in_=ckv[:, :, 1 - b2, :]), reads=[b_ck], writes=[b_ckb])

                        def trc(e):
                            ins = None
                            for j4 in range(4):
                                ins = e.transpose(out=tpb[:, j4, :], in_=ckb[:, j4 // 2, (j4 % 2) * 128:(j4 % 2 + 1) * 128], identity=ident_b[:, :])
                            return ins
                        op(PE, trc, reads=[b_ckb, b_ident_b], writes=[b_tpb])
                        for j4 in range(4):
                            jj = j4 % 2
                            if j4 < 2:
                                gt, gb = 2 * jj, 2 * jj + 1
                            else:
                                gt, gb = 2 * jj + 1, 2 * jj
                            for (plo, gg, clo) in ((0, gt, 0), (64, gb, 64)):
                                outv = kbd[plo:plo + 64, 3 * s_:3 * s_ + 2, gg, clo:clo + 64]
                                inv = tpb[plo:plo + 64, j4, :].rearrange("p (m k) -> p m k", m=2)
                                op(ACT, lambda e, outv=outv, inv=inv: e.activation(out=outv, in_=inv, func=AF.Copy),
                                   reads=[b_tpb], writes=[b_kbd[3 * s_], b_kbd[3 * s_ + 1]])
                        m_cv = kb.mark()
                        for m in range(2):
                            for hh in range(2):
                                op(SP, lambda e, sq=sq, m=m, hh=hh: e.dma_start(out=cvt[64 * hh:64 * hh + 64, m, :], in_=cache_v[sq, 64 * m:64 * m + 64, :]),
                                   writes=[b_cvt], dsem=s_cv)
                        b_cvt.w = [o_ for o_ in kb.since(m_cv) if o_.dsem is s_cv]
                        cv4 = cvt[:, :, :].rearrange("p m (g d) -> p m g d", g=4)
                        for m in range(2):
                            op(DVE, lambda e, m=m, s_=s_, cv4=cv4: e.tensor_copy(out=vbd[0:64, 3 * s_ + m, :, 0:64], in_=cv4[0:64, m]), reads=[b_cvt], writes=[b_vbd[3 * s_ + m]])
                            op(DVE, lambda e, m=m, s_=s_, cv4=cv4: e.tensor_copy(out=vbd[64:128, 3 * s_ + m, :, 64:128], in_=cv4[64:128, m]), reads=[b_cvt], writes=[b_vbd[3 * s_ + m]])

                for qc in range(KC):
                    t1, t2, b_t1, b_t2 = proj_rope(wq, qc * 128, b_wq)
                    op(DVE if (KN('K_QADD', 0) == 1 or (KN('K_QADD', 0) == 2 and qc % 2 == 0)) else POOL, lambda e, qc=qc, t1=t1, t2=t2: e.tensor_tensor(out=qT[:, qc, 0:N], in0=t1[:, 0:N], in1=t2[:, 0:N], op=ALU.add),
                       reads=[b_t1, b_t2], writes=[b_qT[qc]])

                for g in range(4):
                    jobs = []
                    if kind == "p":
                        for c in range(max(c_first - 2, 0), c_first + 8):
                            qlo = max(c, c_first)
                            qhi = min(c + 2, c_first + 7)
                            jobs.append((sbase + c % NRING, (qlo - c_first) * 64, (qhi - qlo + 1) * 64, ones_bd, b_onesbd))
                    else:
                        for s_ in range(ns):
                            for m in range(3):
                                jobs.append((3 * s_ + m, s_ * L, L, ones_bd if m < 2 else ones_pt, b_onesbd if m < 2 else b_onespt))
                    for ji, (sl, q0, nq, onesm, b_onesm) in enumerate(jobs):
                        first = (ji == 0)

                        si_ = cntb["pS"] % 2
                        cntb["pS"] += 1
                        pS = pS_l[si_]
                        b_pS = b_pS_l[si_]

                        def mms(e, sl=sl, q0=q0, nq=nq, g=g, pS=pS):
                            ins = None
                            for hpi in range(2):
                                ins = e.matmul(out=pS[:, hpi * 192:hpi * 192 + nq], lhsT=kbd[:, sl, g, :], rhs=qT[:, 2 * g + hpi, q0:q0 + nq],
                                               start=True, stop=True)
                            return ins
                        op(PE, mms, reads=[b_kbd[sl], b_qT[2 * g], b_qT[2 * g + 1]], writes=[b_pS], c=0.35)
                        pi_ = cntb["pT"] % NPT
                        cntb["pT"] += 1
                        psv = pS[:, 0:384].rearrange("p (h q) -> p h q", h=2)[:, :, 0:nq]
                        ptv = pT[pi_][:, :].rearrange("p (h q) -> p h q", h=2)[:, :, 0:nq]
                        op(ACT, lambda e, psv=psv, ptv=ptv: e.activation(out=ptv, in_=psv, func=AF.Exp, scale=0.125), reads=[b_pS], writes=[b_pT[pi_]], ts="e")

                        def mmo(e, sl=sl, q0=q0, nq=nq, g=g, pi_=pi_, onesm=onesm, first=first):
                            ins = None
                            for hpi in range(2):
                                ins = e.matmul(out=pO[hpi][:, q0:q0 + nq], lhsT=vbd[:, sl, g, :], rhs=pT[pi_][:, hpi * 192:hpi * 192 + nq],
                                               start=first, stop=False, skip_group_check=True)
                            for hpi in range(2):
                                ins = e.matmul(out=pD[hpi][:, q0:q0 + nq], lhsT=onesm[:, :], rhs=pT[pi_][:, hpi * 192:hpi * 192 + nq],
                                               start=first, stop=False, skip_group_check=True)
                            return ins
                        op(PE, mmo, reads=[b_vbd[sl], b_pT[pi_], b_onesm], writes=b_pO + b_pD, c=0.7)
                    for hpi in range(2):
                        hp = 2 * g + hpi
                        op(ACT, lambda e, hpi=hpi: e.activation(out=oraw[hpi][:, 0:N], in_=pO[hpi][:, 0:N], func=AF.Copy),
                           reads=[b_pO[hpi]], writes=[b_oraw[hpi]])
                        op(DVE, lambda e, hpi=hpi, hp=hp: e.tensor_scalar(out=rden[hpi][:, 0:N], in0=pD[hpi][:, 0:N], scalar1=esk[:, hp:hp + 1], scalar2=None, op0=ALU.add),
                           reads=[b_pD[hpi], b_esk], writes=[b_rden[hpi]])
                    for hpi in range(2):
                        hp = 2 * g + hpi
                        op(DVE, lambda e, hpi=hpi: e.reciprocal(out=rden[hpi][:, 0:N], in_=rden[hpi][:, 0:N]), reads=[b_rden[hpi]], writes=[b_rden[hpi]], c=1.8)
                        op(POOL, lambda e, hpi=hpi: e.tensor_tensor(out=oraw[hpi][:, 0:N], in0=oraw[hpi][:, 0:N], in1=rden[hpi][:, 0:N], op=ALU.mult),
                           reads=[b_oraw[hpi], b_rden[hpi]], writes=[b_oraw[hpi]])
                        i_ = cntb["pz"] % 2
                        cntb["pz"] += 1
                        pzx, b_pzx = pz_l[i_], b_pz_l[i_]
                        g_ = cntb["gt"] % 2
                        cntb["gt"] += 1
                        tgx, sgx, b_tgx, b_sgx = tg_l[g_], sg_l[g_], b_tg_l[g_], b_sg_l[g_]
                        op(PE, lambda e, hp=hp, pzx=pzx: proj(e, wgb, hp * 128, pzx), reads=b_wg + xall, writes=[b_pzx], c=2.0)
                        op(ACT, lambda e, pzx=pzx, tgx=tgx: e.activation(out=tgx[:, 0:N], in_=pzx[:, 0:N], func=AF.Tanh, scale=0.5), reads=[b_pzx], writes=[b_tgx], ts="e")
                        op(DVE, lambda e, pzx=pzx, tgx=tgx, sgx=sgx: e.scalar_tensor_tensor(out=sgx[:, 0:N], in0=tgx[:, 0:N], scalar=1.0, in1=pzx[:, 0:N], op0=ALU.add, op1=ALU.mult),
                           reads=[b_tgx, b_pzx], writes=[b_sgx], c=1.0)
                        op(POOL, lambda e, hp=hp, hpi=hpi, sgx=sgx: e.tensor_tensor(out=ogb[:, hp, 0:N], in0=oraw[hpi][:, 0:N], in1=sgx[:, 0:N], op=ALU.mult),
                           reads=[b_oraw[hpi], b_sgx], writes=[b_ogb[hp]])

                if KN('K_PYB', 1) == 1:
                    pyb = [pz_l[0], pz_l[1]]
                    b_pyb = [b_pz_l[0], b_pz_l[1]]
                else:
                    pyb = [pz, pzr]
                    b_pyb = [b_pz, b_pzr]
                for s_ in range(nsub):
                    P = min(128, N - 128 * s_)
                    for h in range(2):
                        def mm_out(e, h=h, s_=s_, P=P):
                            ins = None
                            for kc in range(KC):
                                ins = e.matmul(out=pyb[h][0:P, :], lhsT=ogb[:, kc, 128 * s_:128 * s_ + P], rhs=wob[:, kc, h * 512:(h + 1) * 512],
                                               start=(kc == 0), stop=(kc == KC - 1))
                            return ins
                        op(PE, mm_out, reads=b_ogb + b_wob, writes=[b_pyb[h]], c=2.0)
                    k = cntb["ss2"] % 2
                    cntb["ss2"] += 1
                    yi_ = cntb["yn"] % 2
                    cntb["yn"] += 1
                    yn, b_yn = yn_l[yi_], b_yn_l[yi_]
                    for h in range(2):
                        op(ACT, lambda e, h=h, k=k, P=P: e.activation(out=junk2[0:P, h, :], in_=pyb[h][0:P, :], func=AF.Square,
                                                                      accum_out=ss2[0:P, 4 * k + h:4 * k + h + 1]),
                           reads=[b_pyb[h]], writes=[b_junk2[h], b_ss2[k]])
                        op(DVE, lambda e, h=h, P=P, yn=yn: e.tensor_copy(out=yn[0:P, h * 512:(h + 1) * 512], in_=pyb[h][0:P, :]),
                           reads=[b_pyb[h], b_junk2[h]], writes=[b_yn], c=0.7)
                    op(DVE, lambda e, k=k, P=P: e.tensor_tensor(out=ss2[0:P, 4 * k + 2:4 * k + 3], in0=ss2[0:P, 4 * k:4 * k + 1],
                                                                in1=ss2[0:P, 4 * k + 1:4 * k + 2], op=ALU.add), reads=[b_ss2[k]], writes=[b_ss2[k]])
                    rstd_from_ss(ss2[0:P, 4 * k + 2:4 * k + 3], [b_ss2[k]], b_ss2[k], ss2[0:P, 4 * k + 3:4 * k + 4])
                    for h in range(2):
                        op(DVE, lambda e, h=h, k=k, P=P, yn=yn: e.scalar_tensor_tensor(
                            out=yn[0:P, h * 512:(h + 1) * 512], in0=yn[0:P, h * 512:(h + 1) * 512], scalar=ss2[0:P, 4 * k + 3:4 * k + 4],
                            in1=gpost_b[0:P, h * 512:(h + 1) * 512], op0=ALU.mult, op1=ALU.mult),
                           reads=[b_yn, b_ss2[k], b_gpb], writes=[b_yn], c=0.8)
                    i = cntb["xr"] % NXR
                    cntb["xr"] += 1
                    r0 = row0 + 128 * s_
                    op(SP, lambda e, i=i, P=P, r0=r0: e.dma_start(out=xr[i][0:P, :], in_=src1[r0:r0 + P, :]),
                       reads=[x1_bufs[(kind, r0)]], writes=[b_xr[i]], dsem=s_xr[i], n=1024)
                    op(POOL, lambda e, i=i, P=P, yn=yn: e.tensor_tensor(out=xr[i][0:P, :], in0=xr[i][0:P, :], in1=yn[0:P, :], op=ALU.add),
                       reads=[b_xr[i], b_yn], writes=[b_xr[i]], n=1024)
                    op(SP, lambda e, i=i, P=P, r0=r0: e.dma_start(out=dsty[r0:r0 + P, :], in_=xr[i][0:P, :]), reads=[b_xr[i]], dsem=s_xo[i], n=1024)

            for tl in tiles:
                phaseB_tile(*tl)
            kb.emit(block, last=True)
            psB.close()
            stB.close()

        for tl in tiles:
            phaseA_tile(*tl)

        if do_b:
            gate_t = sb(stA, "gate_t", [128, 1])
            b_gate = kb.buf()
            op(POOL, lambda e: e.memset(gate_t[:], 0.0), after=b_wia, writes=[b_gate], n=1)
            ci = 0
            for kc in range(KC):
                k2 = ci % 2
                ci += 1
                gs = g_kv[:, kc:kc + 1]
                st_ = stgE[k2]
                op(SP, lambda e, kc=kc, st_=st_: e.dma_start(out=st_[:, 0:512], in_=w_kv[kc * 128:(kc + 1) * 128, :]),
                   reads=[b_gate], writes=[b_stgE[k2]], dsem=s_stg[k2], n=512)
                rd = [b_stgE[k2], b_g_kv, b_gate]
                op(DVE, lambda e, kc=kc, st_=st_, gs=gs: e.tensor_scalar(out=wk[:, kc, 0:256], in0=st_[:, 0:256], scalar1=gs, scalar2=None, op0=ALU.mult),
                   reads=rd, writes=[b_wk[kc]], n=256)
                op(ACT, lambda e, kc=kc, st_=st_, gs=gs: e.activation(out=wv[:, kc, :], in_=st_[:, 256:512], func=AF.Identity, scale=gs),
                   reads=rd, writes=[b_wk[kc]], n=256)
                kin = st_[:, 0:256].rearrange("p (a b d) -> p a b d", a=2, b=2)
                ksw = wk[:, kc, 256:512].rearrange("p (a b d) -> p a b d", a=2, b=2)
                for b2 in range(2):
                    op(DVE,
                       lambda e, kin=kin, ksw=ksw, b2=b2, gs=gs: e.tensor_scalar(out=ksw[:, :, b2, :], in0=kin[:, :, 1 - b2, :], scalar1=gs, scalar2=None, op0=ALU.mult),
                       reads=rd, writes=[b_wk[kc]], n=128)
            for kc in range(KC):
                k2 = ci % 2
                ci += 1
                gs = g_b[:, kc:kc + 1]
                st_ = stgE[k2]
                op(SP, lambda e, kc=kc, st_=st_: e.dma_start(out=st_[:, 0:1024], in_=w_in_b[kc * 128:(kc + 1) * 128, 0:1024]),
                   reads=[b_gate], writes=[b_stgE[k2]], dsem=s_stg[k2], n=1024)
                op(DVE, lambda e, kc=kc, st_=st_, gs=gs: e.tensor_scalar(out=wq[:, kc, :], in0=st_[:, 0:1024], scalar1=gs, scalar2=None, op0=ALU.mult),
                   reads=[b_stgE[k2], b_g_b, b_gate], writes=[b_wq[kc]], n=1024)

        if not do_b:
            for (kind, r0), bx in x1_bufs.items():
                P = 128 if kind == "p" else 64
                i = cnt["xb"] % NXB
                cnt["xb"] += 1
                srcd = x1p if kind == "p" else x1s
                dstd = yp if kind == "p" else ys
                op(SP, lambda e, i=i, P=P, r0=r0, srcd=srcd: e.dma_start(out=xb[i][0:P, :], in_=srcd[r0:r0 + P, :]),
                   reads=[bx], writes=[b_xb[i]], dsem=s_xb[i])
                op(SP, lambda e, i=i, P=P, r0=r0, dstd=dstd: e.dma_start(out=dstd[r0:r0 + P, :], in_=xb[i][0:P, :]),
                   reads=[b_xb[i]], dsem=s_xo[i])

        kb.emit(block, last=not do_b)
        kb.alias_pre = kb.barrier_tokens()
        psA.close()
        stA.close()
        if do_b:
            phase_b()
    return nc


_NC_CACHE = {}


def _prot_matrix():
    p = np.zeros((128, 128), np.float32)
    for m in range(128):
        if m % 64 < 32:
            p[m + 32, m] = -1.0
        else:
            p[m - 32, m] = 1.0
    return p


def _rope_tables():
    half = 32
    inv = (np.float32(10000.0) ** (-np.arange(half, dtype=np.float32) / np.float32(half))).astype(np.float32)
    pos = np.concatenate([np.arange(SEQ), PAST + np.arange(SL), PAST + np.arange(SL)]).astype(np.float32)
    ang = pos[:, None] * inv[None, :]
    cos = np.cos(ang).astype(np.float32)
    sin = np.sin(ang).astype(np.float32)
    idx = np.arange(128) % 32
    return np.ascontiguousarray(cos[:, idx].T), np.ascontiguousarray(sin[:, idx].T)


def kernel(x_prompt, x_sample, state_conv, state_rnn, cache_k, cache_v,
           norm_pre_a, w_in_a, conv_w_a, conv_b_a, w_gate_r, b_gate_r, w_gate_i, b_gate_i,
           lru_lambda, w_out_a, norm_post_a, norm_kv, w_kv,
           norm_pre_b, w_in_b, attn_sinks, w_out_b, norm_post_b, _do_b=True):
    f = lambda a: np.ascontiguousarray(np.asarray(a, dtype=np.float32))
    key = bool(_do_b)
    if key not in _NC_CACHE:
        _NC_CACHE[key] = build_program(do_b=key)
    nc = _NC_CACHE[key]
    cos_t, sin_t = _rope_tables()
    shared = {
        "norm_pre_a": f(norm_pre_a).reshape(D), "w_in_a": f(w_in_a).reshape(D, 2 * DR),
        "conv_w": f(conv_w_a).reshape(4, DR), "conv_b": f(conv_b_a).reshape(DR),
        "w_gate_r": f(w_gate_r).reshape(16, 88, 88), "b_gate_r": f(b_gate_r).reshape(DR),
        "w_gate_i": f(w_gate_i).reshape(16, 88, 88), "b_gate_i": f(b_gate_i).reshape(DR),
        "lru_lambda": f(lru_lambda).reshape(DR), "w_out_a": f(w_out_a).reshape(DR, D),
        "norm_post_a": f(norm_post_a).reshape(1, D), "norm_kv": f(norm_kv).reshape(D),
        "w_kv": f(w_kv).reshape(D, 512), "norm_pre_b": f(norm_pre_b).reshape(D),
        "w_in_b": f(w_in_b).reshape(D, 2048), "sinks": f(attn_sinks).reshape(16),
        "w_out_b": f(w_out_b).reshape(D, D), "norm_post_b": f(norm_post_b).reshape(1, D),
        "ident": np.eye(128, dtype=np.float32), "prot": _prot_matrix(), "cos_t": cos_t, "sin_t": sin_t,
    }
    xpf, xsf = f(x_prompt), f(x_sample)
    scf, srf = f(state_conv)[0], f(state_rnn)[0]
    ckf, cvf = f(cache_k).reshape(16, 128, 256), f(cache_v).reshape(16, 128, 256)
    in_maps = []
    for c in range(NCORES):
        sl = slice(NSEQ * c, NSEQ * c + NSEQ)
        m = dict(shared)
        m["xp"] = xpf[sl].reshape(NSEQ * SEQ, D)
        m["xs"] = xsf[sl].reshape(NSEQ * SL, D)
        m["st_conv"] = np.ascontiguousarray(scf[sl])
        m["st_rnn"] = np.ascontiguousarray(srf[sl])
        m["cache_k"] = np.ascontiguousarray(ckf[sl])
        m["cache_v"] = np.ascontiguousarray(cvf[sl])
        in_maps.append(m)
    res = run_bass_kernel_spmd(nc, in_maps, core_ids=list(range(NCORES)))
    R = res.results
    cat = lambda k: np.concatenate([np.asarray(r[k], dtype=np.float32) for r in R], axis=0)
    y_prompt = cat("yp").reshape(16, SEQ, D)
    y_sample = cat("ys").reshape(16, SL, D)
    pc = cat("p_conv").reshape(1, 16, 3, DR)
    prn = cat("p_rnn").reshape(1, 16, DR)
    pk = cat("p_k").reshape(16, 128, 4, 64)
    pv = cat("p_v").reshape(16, 128, 4, 64)
    sc = cat("s_conv").reshape(1, 16, 3, DR)
    srn = cat("s_rnn").reshape(1, 16, DR)
    sk = cat("s_k").reshape(16, SL, 4, 64)
    sv = cat("s_v").reshape(16, SL, 4, 64)
    return (y_prompt, y_sample, pc, prn, pk, pv, sc, srn, sk, sv)
```
